# Optimizing a Trainium2 kernel written in Bass

```python
import math
import jax, jax.numpy as jnp
from jax import lax
import numpy as np

D_MODEL = 2048
BATCH = 4
SEQ = 2048
DEPTH = 1
DEC_BATCH = 128
DEC_SEQ = 8
PAST_LEN = 8192
PAGE_SIZE = 128

A_HEADS = 8
A_DK = 128
A_DV = 128
A_WIDTH = A_HEADS * A_DV
A_CHUNK = 64
B_HEADS = 8
Q_RANK = 512
KV_RANK = 512
NOPE_DIM = 128
ROPE_DIM = 64
V_DIM = 128
B_WIDTH = B_HEADS * V_DIM
ROPE_THETA = 10000.0
MLA_SCALE = (NOPE_DIM + ROPE_DIM) ** -0.5
Q_BLOCK = 128
D_FF = 4 * D_MODEL
EPS = 1e-6
IN_SIZES = (A_HEADS * A_DK, A_HEADS * A_DK, A_HEADS * A_DV, A_WIDTH, Q_RANK, KV_RANK, ROPE_DIM, D_MODEL, D_MODEL)
IN_COLS = 3 * A_HEADS * A_DK + A_WIDTH + Q_RANK + KV_RANK + ROPE_DIM + 2 * D_MODEL

kernel_name = "hgrn2_mla_gated_hybrid_step"


def rms_norm(x, g):
    xf = x.astype(jnp.float32)
    r = lax.rsqrt(jnp.mean(xf * xf, axis=-1, keepdims=True) + EPS)
    return (xf * r).astype(x.dtype) * g


def rope(x, pos):
    half = ROPE_DIM // 2
    inv = ROPE_THETA ** (-jnp.arange(half, dtype=jnp.float32) / half)
    ang = pos.astype(jnp.float32)[:, None] * inv[None, :]
    shape = (1, ang.shape[0]) + (1,) * (x.ndim - 3) + (half,)
    cos = jnp.cos(ang).reshape(shape).astype(x.dtype)
    sin = jnp.sin(ang).reshape(shape).astype(x.dtype)
    x1, x2 = x[..., :half], x[..., half:]
    return jnp.concatenate([x1 * cos - x2 * sin, x1 * sin + x2 * cos], axis=-1)


def hgrn2_scan(q, k, v, logf, s0):
    B, T, H, _ = q.shape
    C = math.gcd(T, A_CHUNK)
    n = T // C

    def to_chunks(a):
        return a.reshape(B, n, C, H, a.shape[-1]).transpose(1, 0, 3, 2, 4)

    qc, kc, vc, fc = (to_chunks(a) for a in (q, k, v, logf))
    causal = jnp.tril(jnp.ones((C, C), dtype=bool))

    def step(S, inp):
        qi, ki, vi, fi = inp
        b = jnp.cumsum(fi.astype(jnp.float32), axis=2)
        o_inter = jnp.einsum('bhtk,bhkv->bhtv', qi * jnp.exp(b), S)
        diff = b[:, :, :, None, :] - b[:, :, None, :, :]
        decay = jnp.exp(jnp.where(causal[:, :, None], diff, -jnp.inf))
        scores = jnp.einsum('bhtk,bhsk,bhtsk->bhts', qi, ki, decay)
        o = o_inter + jnp.einsum('bhts,bhsv->bhtv', scores, vi)
        b_last = b[:, :, -1:, :]
        S_new = jnp.exp(b_last[:, :, 0, :])[..., None] * S + jnp.einsum(
            'bhsk,bhsv->bhkv', ki * jnp.exp(b_last - b), vi)
        return S_new.astype(jnp.float32), o.astype(jnp.float32)

    S_fin, o = lax.scan(step, s0, (qc, kc, vc, fc))
    o = o.transpose(1, 0, 3, 2, 4).reshape(B, T, H, -1)
    return o, S_fin


def mla_prompt_attention(q_lat, q_pe, ckv, kpe):
    B, T, H, _ = q_lat.shape
    nb = T // Q_BLOCK

    def blocks(a):
        return a.reshape(B, nb, Q_BLOCK, H, a.shape[-1]).swapaxes(0, 1)

    kpos = jnp.arange(T)

    def one(args):
        ql, qp, i = args
        s = (jnp.einsum('bthc,bsc->bhts', ql, ckv) + jnp.einsum('bthr,bsr->bhts', qp, kpe)).astype(jnp.float32) * MLA_SCALE
        qpos = i * Q_BLOCK + jnp.arange(Q_BLOCK)
        s = jnp.where(kpos[None, :] <= qpos[:, None], s, -jnp.inf)
        p = jax.nn.softmax(s, axis=-1).astype(ckv.dtype)
        return jnp.einsum('bhts,bsc->bthc', p, ckv)

    o = lax.map(one, (blocks(q_lat), blocks(q_pe), jnp.arange(nb)))
    return o.swapaxes(0, 1).reshape(B, T, H, -1)


def mla_sample_attention(q_lat, q_pe, ckv_new, kpe_new, cache_ckv_l, cache_kpe_l, page_table):
    Bd, T, H, _ = q_lat.shape
    ckv_past = cache_ckv_l[page_table].reshape(Bd, -1, KV_RANK)
    kpe_past = cache_kpe_l[page_table].reshape(Bd, -1, ROPE_DIM)
    P = ckv_past.shape[1]
    s_past = jnp.einsum('bthc,bsc->bhts', q_lat, ckv_past) + jnp.einsum('bthr,bsr->bhts', q_pe, kpe_past)
    s_new = jnp.einsum('bthc,bsc->bhts', q_lat, ckv_new) + jnp.einsum('bthr,bsr->bhts', q_pe, kpe_new)
    causal = jnp.tril(jnp.ones((T, T), dtype=bool))
    s_new = jnp.where(causal, s_new.astype(jnp.float32), -jnp.inf)
    s = jnp.concatenate([s_past.astype(jnp.float32), s_new], axis=-1) * MLA_SCALE
    p = jax.nn.softmax(s, axis=-1).astype(ckv_new.dtype)
    return (jnp.einsum('bhts,bsc->bthc', p[..., :P], ckv_past)
            + jnp.einsum('bhts,bsc->bthc', p[..., P:], ckv_new))


def setup_inputs(seed: int = 0) -> dict:
    key = jax.random.key(seed)
    ks = jax.random.split(key, 32)
    n_pages = PAST_LEN // PAGE_SIZE
    n_used = DEC_BATCH * n_pages
    n_pool = (n_used * 5) // 4
    f32 = jnp.float32

    def nrm(k, shape, scale):
        return jax.random.normal(k, shape, dtype=f32) * scale

    def gain(k, shape):
        return 1.0 + nrm(k, shape, 0.02)

    perm = jax.random.permutation(ks[0], n_pool)[:n_used]
    page_table = perm.reshape(DEC_BATCH, n_pages).astype(jnp.int32)
    return {
        "x_prompt": nrm(ks[1], (BATCH, SEQ, D_MODEL), 1.0),
        "x_sample": nrm(ks[2], (DEC_BATCH, DEC_SEQ, D_MODEL), 1.0),
        "c_prompt": nrm(ks[3], (BATCH, D_MODEL), 1.0),
        "c_sample": nrm(ks[4], (DEC_BATCH, D_MODEL), 1.0),
        "cache_ckv": nrm(ks[5], (DEPTH, n_pool, PAGE_SIZE, KV_RANK), 1.0),
        "cache_kpe": nrm(ks[6], (DEPTH, n_pool, PAGE_SIZE, ROPE_DIM), 1.0),
        "state_hgrn": nrm(ks[7], (DEPTH, DEC_BATCH, A_HEADS, A_DK, A_DV), 0.3),
        "page_table": page_table,
        "w_ada": nrm(ks[8], (DEPTH, D_MODEL, 6 * D_MODEL), 0.5 * D_MODEL ** -0.5),
        "b_ada": nrm(ks[9], (DEPTH, 6 * D_MODEL), 0.01),
        "g_pre_mix": gain(ks[10], (DEPTH, D_MODEL)),
        "g_post_mix": gain(ks[11], (DEPTH, D_MODEL)),
        "g_pre_mlp": gain(ks[12], (DEPTH, D_MODEL)),
        "g_post_mlp": gain(ks[13], (DEPTH, D_MODEL)),
        "w_in": nrm(ks[14], (DEPTH, D_MODEL, IN_COLS), D_MODEL ** -0.5),
        "lb_logits": nrm(ks[15], (DEPTH + 1, A_HEADS * A_DK), 0.5),
        "g_hgrn_norm": gain(ks[16], (DEPTH, A_WIDTH)),
        "w_a_out": nrm(ks[17], (DEPTH, A_WIDTH, D_MODEL), A_WIDTH ** -0.5),
        "g_q_norm": gain(ks[18], (DEPTH, Q_RANK)),
        "w_q_up": nrm(ks[19], (DEPTH, Q_RANK, B_HEADS * (NOPE_DIM + ROPE_DIM)), Q_RANK ** -0.5),
        "g_kv_norm": gain(ks[20], (DEPTH, KV_RANK)),
        "w_kv_up": nrm(ks[21], (DEPTH, KV_RANK, B_HEADS * (NOPE_DIM + V_DIM)), KV_RANK ** -0.5),
        "w_b_out": nrm(ks[22], (DEPTH, B_WIDTH, D_MODEL), B_WIDTH ** -0.5),
        "w_o": nrm(ks[23], (DEPTH, D_MODEL, D_MODEL), D_MODEL ** -0.5),
        "w_up": nrm(ks[24], (DEPTH, D_MODEL, D_FF), D_MODEL ** -0.5),
        "w_down": nrm(ks[25], (DEPTH, D_FF, D_MODEL), D_FF ** -0.5),
    }


def reference(x_prompt, x_sample, c_prompt, c_sample, cache_ckv, cache_kpe, state_hgrn, page_table,
              w_ada, b_ada, g_pre_mix, g_post_mix, g_pre_mlp, g_post_mlp, w_in, lb_logits, g_hgrn_norm,
              w_a_out, g_q_norm, w_q_up, g_kv_norm, w_kv_up, w_b_out, w_o, w_up, w_down):
    offsets = np.cumsum(IN_SIZES)[:-1].tolist()
    lb_all = jnp.cumsum(jax.nn.softmax(lb_logits.astype(jnp.float32), axis=0), axis=0)

    def run_group(x, c, pos, s0_layers, attend):
        B, T, _ = x.shape
        ckv_out, kpe_out, s_out = [], [], []
        for l in range(DEPTH):
            mod = jax.nn.silu(c) @ w_ada[l] + b_ada[l]
            sh1, sc1, gt1, sh2, sc2, gt2 = (m[:, None, :] for m in jnp.split(mod, 6, axis=-1))
            h = rms_norm(x, g_pre_mix[l]) * (1.0 + sc1) + sh1
            proj = h @ w_in[l]
            qa, fa, ia, ga, qd, kvd, kpe_raw, gate_a, gate_b = jnp.split(proj, offsets, axis=-1)
            lb = lb_all[l].reshape(A_HEADS, A_DK)
            q_h = jax.nn.silu(qa).reshape(B, T, A_HEADS, A_DK)
            f = lb + (1.0 - lb) * jax.nn.sigmoid(fa.astype(jnp.float32).reshape(B, T, A_HEADS, A_DK))
            logf = jnp.log(f)
            k_h = 1.0 - f
            v_h = ia.reshape(B, T, A_HEADS, A_DV)
            o_a, s_fin = hgrn2_scan(q_h, k_h, v_h, logf, s0_layers(l, B))
            o_a = rms_norm(o_a.astype(h.dtype), g_hgrn_norm[l].reshape(A_HEADS, A_DV))
            o_a = o_a * jax.nn.sigmoid(ga.reshape(B, T, A_HEADS, A_DV))
            y_a = o_a.reshape(B, T, A_WIDTH) @ w_a_out[l]
            q_full = (rms_norm(qd, g_q_norm[l]) @ w_q_up[l]).reshape(B, T, B_HEADS, NOPE_DIM + ROPE_DIM)
            q_nope, q_pe = q_full[..., :NOPE_DIM], rope(q_full[..., NOPE_DIM:], pos)
            ckv = rms_norm(kvd, g_kv_norm[l])
            kpe = rope(kpe_raw, pos)
            w_kv = w_kv_up[l].reshape(KV_RANK, B_HEADS, NOPE_DIM + V_DIM)
            w_uk, w_uv = w_kv[..., :NOPE_DIM], w_kv[..., NOPE_DIM:]
            q_lat = jnp.einsum('bthn,chn->bthc', q_nope, w_uk)
            o_lat = attend(q_lat, q_pe, ckv, kpe, l)
            o_b = jnp.einsum('bthc,chv->bthv', o_lat, w_uv).reshape(B, T, B_WIDTH)
            y_b = o_b @ w_b_out[l]
            merged = jax.nn.sigmoid(gate_a) * y_a + jax.nn.sigmoid(gate_b) * y_b
            x = x + gt1 * rms_norm(merged @ w_o[l], g_post_mix[l])
            h2 = rms_norm(x, g_pre_mlp[l]) * (1.0 + sc2) + sh2
            u = jax.nn.relu(h2 @ w_up[l])
            x = x + gt2 * rms_norm((u * u) @ w_down[l], g_post_mlp[l])
            ckv_out.append(ckv)
            kpe_out.append(kpe)
            s_out.append(s_fin)
        return x, jnp.stack(ckv_out), jnp.stack(kpe_out), jnp.stack(s_out)

    def prompt_attend(q_lat, q_pe, ckv, kpe, l):
        return mla_prompt_attention(q_lat, q_pe, ckv, kpe)

    def sample_attend(q_lat, q_pe, ckv, kpe, l):
        return mla_sample_attention(q_lat, q_pe, ckv, kpe, cache_ckv[l], cache_kpe[l], page_table)

    def prompt_s0(l, B):
        return jnp.zeros((B, A_HEADS, A_DK, A_DV), dtype=jnp.float32)

    def sample_s0(l, B):
        return state_hgrn[l].astype(jnp.float32)

    pos_prompt = jnp.arange(x_prompt.shape[1])
    pos_sample = PAST_LEN + jnp.arange(x_sample.shape[1])
    y_prompt, ckv_p, kpe_p, s_p = run_group(x_prompt, c_prompt, pos_prompt, prompt_s0, prompt_attend)
    y_sample, ckv_s, kpe_s, s_s = run_group(x_sample, c_sample, pos_sample, sample_s0, sample_attend)
    return (y_prompt, y_sample, ckv_p, kpe_p, s_p, ckv_s, kpe_s, s_s)
```

```python
import math
import numpy as np
import ml_dtypes
import concourse.bass as bass
import concourse.mybir as mybir
from concourse.bass_utils import run_bass_kernel_spmd

F32 = mybir.dt.float32
BF16 = mybir.dt.bfloat16
I32 = mybir.dt.int32
AF = mybir.ActivationFunctionType
ALU = mybir.AluOpType
AX = mybir.AxisListType

D = 2048
NH = 8
DK = 128
DV = 128
QR = 512
KVR = 512
NOPE = 128
ROPE = 64
VD = 128
DFF = 8192
EPS = 1e-6
PAST = 8192
PAGE = 128
NPAGES = 64
SCALE = (NOPE + ROPE) ** -0.5
NTP = 1024
NTS = 128
NT = NTP + NTS
NCX = 1024
NB = 16
TS = 8
NEG = -30000.0
NKEY = NCX + NTP + NTS

O_QA, O_FA, O_IA, O_GA, O_QD, O_KVD, O_KPE, O_GTA, O_GTB = 0, 1024, 2048, 3072, 4096, 4608, 5120, 5184, 7232
IN_COLS = 9280
BLKS = [(0, 512), (512, 512), (1024, 128)]
CBLKS = [(0, 512), (512, 512)]


def _flat(deps):
    out = []
    if deps is None:
        return out
    if isinstance(deps, tuple) and len(deps) == 3 and isinstance(deps[0], str):
        return [deps]
    for d in deps:
        out.extend(_flat(d))
    return out


class Eng:
    def __init__(self, nc, name, eng):
        self.nc, self.name, self.eng = nc, name, eng
        self.gen = 0
        self.sem = nc.alloc_semaphore("sem_" + name)
        self.cnt = 0
        self.seen = {}
        self.last = None

    def wait(self, deps):
        best = {}
        for (key, sem, val) in _flat(deps):
            if self.seen.get(key, 0) >= val:
                continue
            if key not in best or best[key][1] < val:
                best[key] = (sem, val)
        for key, (sem, val) in best.items():
            self.eng.wait_ge(sem, val)
            self.seen[key] = val

    def __call__(self, opname, deps=None, sig=True, **kw):
        self.wait(deps)
        inst = getattr(self.eng, opname)(**kw)
        if sig:
            if self.cnt >= 30000:
                self.gen += 1
                self.sem = self.nc.alloc_semaphore(f"sem_{self.name}_{self.gen}")
                self.cnt = 0
            self.cnt += 1
            inst.then_inc(self.sem, 1)
            self.last = (f"{self.name}.{self.gen}", self.sem, self.cnt)
            return self.last
        return None

    def tok(self):
        return self.last


class Ctx:
    def __init__(self, nc):
        self.nc = nc
        self.pe = Eng(nc, "pe", nc.tensor)
        self.act = Eng(nc, "act", nc.scalar)
        self.dve = Eng(nc, "dve", nc.vector)
        self.pool = Eng(nc, "pool", nc.gpsimd)
        self.sp = Eng(nc, "sp", nc.sync)
        self.engs = [self.pe, self.act, self.dve, self.pool, self.sp]
        self.streams = {}
        self.dma_toks = []
        self.ps = [nc.alloc_psum_tensor(f"psb{i}", [128, 512], F32) for i in range(8)]
        self.ps_free = [None] * 8
        self.ps_i = 0
        self._n = 0

    def name(self, s):
        self._n += 1
        return f"{s}_{self._n}"

    def dma(self, q, out, in_, stream, deps=None, indirect=None):
        if stream not in self.streams:
            self.streams[stream] = [self.nc.alloc_semaphore("dsem_" + stream), 0]
        st = self.streams[stream]
        q.wait(deps)
        if indirect is not None:
            inst = q.eng.indirect_dma_start(out=out, out_offset=None, in_=in_,
                                            in_offset=bass.IndirectOffsetOnAxis(ap=indirect, axis=0))
        else:
            inst = q.eng.dma_start(out=out, in_=in_)
        st[1] += 16
        inst.then_inc(st[0], 16)
        tok = ("d_" + stream, st[0], st[1])
        self.dma_toks.append(tok)
        return tok

    def psum(self):
        i = self.ps_i
        self.ps_i = (i + 1) % 8
        return i, self.ps[i], self.ps_free[i]

    def psum_release(self, i, tok):
        self.ps_free[i] = tok

    def barrier(self):
        toks = [e.tok() for e in self.engs if e.tok() is not None]
        latest = {}
        for t in self.dma_toks:
            latest[t[0]] = t
        toks += list(latest.values())
        self.dma_toks = list(latest.values())
        for e in self.engs:
            e.wait(toks)
        self.ps_free = [None] * 8


class Auto:
    def __init__(self, K):
        self.K = K
        self.w = {}
        self.r = {}

    def reset(self):
        self.w = {}
        self.r = {}

    def deps(self, R, W):
        d = []
        for k in R:
            if k in self.w:
                d.append(self.w[k])
        for k in W:
            if k in self.w:
                d.append(self.w[k])
            d.extend(self.r.get(k, {}).values())
        return d

    def note(self, tok, R, W):
        for k in R:
            self.r.setdefault(k, {})[tok[0]] = tok
        for k in W:
            self.w[k] = tok
            self.r[k] = {}

    def op(self, eng, opname, R=(), W=(), **kw):
        tok = eng(opname, deps=self.deps(R, W), **kw)
        self.note(tok, R, W)
        return tok

    def mm(self, W, R, mms):
        pe = self.K.pe
        pe.wait(self.deps(R, W))
        tok = None
        for i, (opname, kw) in enumerate(mms):
            tok = pe(opname, sig=(i == len(mms) - 1), **kw)
        self.note(tok, R, W)
        return tok

    def dma(self, q, out, in_, R, W, stream, indirect=None):
        tok = self.K.dma(q, out, in_, stream, deps=self.deps(R, W), indirect=indirect)
        self.note(tok, R, W)
        return tok

    def dma_group(self, q, items, R, W, stream):
        d = self.deps(R, W)
        tok = None
        for (out, in_) in items:
            tok = self.K.dma(q, out, in_, stream, deps=d)
            d = None
        self.note(tok, R, W)
        return tok


def sb(nc, name, shape, dt):
    return nc.alloc_sbuf_tensor(name, list(shape), dt)


class Bld:
    pass


def MM(out, lhsT, rhs, start, stop):
    return ("matmul", dict(out=out, lhsT=lhsT, rhs=rhs, start=start, stop=stop))


def TR(out, in_, identity):
    return ("transpose", dict(out=out, in_=in_, identity=identity))


ARENA_BYTES = 196608 - 4096
OFF_H = 0
OFF_C = 36864
OFF_LAT = 69632
OFF_OA = 118016
OFF_OB = 136448
OFF_T = 154880


def build_nc(n_pool_rows, stop_after=99, debug=False):
    nc = bass.Bass("TRN2", target_bir_lowering=False)
    K = Ctx(nc)
    A = Auto(K)

    def dbg(name, ap, dt=F32):
        if not debug:
            return
        K.barrier()
        d = nc.dram_tensor("dbg_" + name, list(ap.shape), dt, kind="ExternalOutput").ap()
        K.dma(K.sp, d, ap, "dbg")
        K.barrier()
    pe, act, dve, pool, sp = K.pe, K.act, K.dve, K.pool, K.sp
    B = Bld()
    B.nc, B.K, B.A = nc, K, A
    B.dbg = dbg
    B.debug = debug

    def din(name, shape, dt=F32):
        return nc.dram_tensor(name, list(shape), dt, kind="ExternalInput").ap()

    def dout(name, shape, dt=F32):
        return nc.dram_tensor(name, list(shape), dt, kind="ExternalOutput").ap()

    x_main = din("x_main", [NT, D])
    x_ctx = din("x_ctx", [NCX, D])
    c_all = din("c_all", [17, D])
    cache_all = din("cache_all", [n_pool_rows, KVR + ROPE])
    state_in = din("state_in", [NB, NH, DK, DV])
    ptab = din("ptab", [1, NB * NPAGES], I32)
    w_ada = din("w_ada", [D, 6 * D])
    b_ada = din("b_ada", [1, 6 * D])
    vecs = din("vecs", [96, 128])
    g_q = din("g_q", [1, QR])
    g_kv = din("g_kv", [1, KVR])
    w_in = din("w_in", [D, IN_COLS])
    w_a_out = din("w_a_out", [NH * DV, D])
    w_q_up = din("w_q_up", [QR, NH * (NOPE + ROPE)])
    w_kv_up = din("w_kv_up", [KVR, NH * (NOPE + VD)])
    w_b_out = din("w_b_out", [NH * VD, D])
    w_o = din("w_o", [D, D])
    w_up = din("w_up", [D, DFF])
    w_down = din("w_down", [DFF, D])
    ident_d = din("ident", [128, 128])
    rope_cs = din("rope_cs", [17 * 128, 64])
    rope_sn = din("rope_sn", [17 * 128, 64])
    ropeT_cs = din("ropeT_cs", [64, NT])
    ropeT_sn = din("ropeT_sn", [64, NT])
    ctx_bias = din("ctx_bias", [1, NCX])
    ctx_flag = din("ctx_flag", [128, 1])
    masks_d = din("masks", [128, 1024])
    smask_d = din("smask", [128, NB * 64])
    sel_d = din("sel", [17, 256])

    y_main = dout("y_main", [NT, D])
    ckv_main = dout("ckv_main", [NT, KVR])
    kpe_main = dout("kpe_main", [NT, ROPE])
    s_prompt = dout("s_prompt", [NH, DK, DV])
    s_sample = dout("s_sample", [NB, NH, DK, DV])

    ident = sb(nc, "ident_sb", [128, 128], F32)
    ident_bf = sb(nc, "ident_bf", [128, 128], BF16)
    vecT = sb(nc, "vecT", [128, 96], F32)
    modT = sb(nc, "modT", [128, 96, 17], F32)
    A1 = sb(nc, "A1", [128, 16, 17], F32)
    A2 = sb(nc, "A2", [128, 16, 17], F32)
    G1 = sb(nc, "G1", [128, 16, 17], F32)
    G2 = sb(nc, "G2", [128, 16, 17], F32)
    lbT = sb(nc, "lbT", [128, 8], F32)
    omlT = sb(nc, "omlT", [128, 8], F32)
    nomlT = sb(nc, "nomlT", [128, 8], F32)
    masks = sb(nc, "masks_sb", [128, 1024], F32)
    cflag = sb(nc, "cflag", [128, 1], F32)
    ones_bf = sb(nc, "ones_bf", [128, 128], BF16)
    ones_f = sb(nc, "ones_f", [128, 128], F32)
    eps_t = sb(nc, "eps_t", [128, 1], F32)
    arena = sb(nc, "arena", [128, ARENA_BYTES // 4], F32)
    ar_f = arena[:]
    ar_b = arena[:].bitcast(BF16)
    ar_i = arena[:].bitcast(I32)

    def view(off, shape, dt):
        n = int(np.prod(shape[1:]))
        assert off % 4 == 0
        if dt == BF16:
            assert off + 2 * n <= ARENA_BYTES, (off, shape)
            v = ar_b[:, off // 2: off // 2 + n]
        else:
            assert off + 4 * n <= ARENA_BYTES, (off, shape)
            v = (ar_f if dt == F32 else ar_i)[:, off // 4: off // 4 + n]
        if shape[0] != 128:
            v = v[0:shape[0]]
        if len(shape) == 3:
            v = v.rearrange("p (a b) -> p a b", a=shape[1])
        elif len(shape) == 4:
            v = v.rearrange("p (a b c) -> p a b c", a=shape[1], b=shape[2])
        elif len(shape) == 5:
            v = v.rearrange("p (a b c d) -> p a b c d", a=shape[1], b=shape[2], c=shape[3])
        return v

    def bump(regions):
        state = {"r": 0, "o": regions[0][0]}

        def tmp(name, shape, dt):
            n = int(np.prod(shape[1:])) * (2 if dt == BF16 else 4)
            n = (n + 63) // 64 * 64
            while True:
                s_, e_ = regions[state["r"]]
                if state["o"] + n <= e_:
                    off = state["o"]
                    state["o"] += n
                    return view(off, shape, dt)
                state["r"] += 1
                assert state["r"] < len(regions), ("arena region overflow", name)
                state["o"] = regions[state["r"]][0]
        return tmp

    REG_H, REG_C, REG_LAT = (OFF_H, OFF_C), (OFF_C, OFF_LAT), (OFF_LAT, OFF_OA)
    REG_BIG = (OFF_LAT, OFF_T)
    REG_L = (OFF_T, ARENA_BYTES)
    hT = view(OFF_H, [128, 16, NT], BF16)
    hcT = view(OFF_C, [128, 16, NCX], BF16)
    o = OFF_LAT
    ckvT = view(o, [128, 4, NKEY], BF16); o += 2 * 4 * NKEY
    kpeT = view(o, [128, NKEY], BF16); o += 2 * NKEY
    ckv_tm = view(o, [128, 17, KVR], BF16); o += 2 * 17 * KVR
    qnT = view(o, [128, 4, NT], BF16); o += 2 * 4 * NT
    assert o == OFF_OA, o
    oaT = view(OFF_OA, [128, 8, NT], BF16)
    obT = view(OFF_OB, [128, 8, NT], BF16)
    B.__dict__.update(locals())

    phases = [phase0, phase1, phase2, phase3, phase4, phase5, phase6]
    for i, ph in enumerate(phases):
        if stop_after >= i:
            K.barrier()
            A.reset()
            ph(B)
    K.barrier()
    return nc


def evac_copy(K, which, out, in_, deps):
    if which == 0:
        return K.dve("tensor_copy", deps=deps, out=out, in_=in_)
    return K.act("copy", deps=deps, out=out, in_=in_)


def phase0(B):
    nc, K = B.nc, B.K
    pe, act, dve, pool, sp = K.pe, K.act, K.dve, K.pool, K.sp
    if True:
        tb = B.bump([B.REG_BIG]); tc = B.bump([B.REG_C]); th = B.bump([B.REG_H]); tl = B.bump([B.REG_L])
        modB = tb("modB", [17, 6 * D], F32)
        c_sb = tb("c_sb", [17, D], F32)
        c_si = tb("c_si", [17, D], F32)
        wa = [tc(f"wa{i}", [128, 16, 512], BF16) for i in range(2)]
        bada_bf = th("bada_bf", [1, 6 * D], BF16)
        scT = tl("scT", [128, 16, 17], BF16)
        vec_sb = tl("vec_sb", [96, 128], F32)
        l8 = tl("l8", [128, 8], F32)

        t_c = [K.dma(sp, B.ident[:], B.ident_d, "c0")]
        t_c.append(K.dma(sp, c_sb[:], B.c_all, "c0"))
        t_c.append(K.dma(sp, vec_sb[:], B.vecs, "c0"))
        t_c.append(K.dma(sp, B.masks[:], B.masks_d, "c0"))
        t_c.append(K.dma(sp, B.cflag[:], B.ctx_flag, "c0"))
        t_c = t_c[-1]
        t_b = K.dma(pool, bada_bf[:], B.b_ada, "c1")
        t_o1 = dve("memset", ap=B.ones_bf[:], constant=1.0)
        t_o2 = dve("memset", ap=B.ones_f[:], constant=1.0)
        dve("memset", ap=B.eps_t[:], constant=EPS)

        t_si = act("activation", deps=[t_c], out=c_si[:], in_=c_sb[:], func=AF.Silu)
        i, ps, fr = K.psum()
        for dc in range(16):
            t_tr = pe("transpose", deps=[t_si, t_c, fr], sig=(dc == 15), out=ps[:, dc * 17:(dc + 1) * 17],
                      in_=c_si[0:17, dc * 128:(dc + 1) * 128], identity=B.ident[0:17, 0:17])
        t_scT = dve("tensor_copy", deps=[t_tr], out=scT[:], in_=ps[:, 0:272].rearrange("p (a b) -> p a b", b=17))
        K.psum_release(i, t_scT)

        i, ps, fr = K.psum()
        t_tr = pe("transpose", deps=[t_c, fr], out=ps[:, 0:96], in_=vec_sb[0:96, :], identity=B.ident[0:96, 0:96])
        t_vec = dve("tensor_copy", deps=[t_tr], out=B.vecT[:], in_=ps[:, 0:96])
        K.psum_release(i, t_vec)

        wv = B.w_ada.rearrange("(c p) n -> p c n", p=128)
        mm_tok = [None, None]
        ev_toks = []
        for cb in range(24):
            buf = wa[cb % 2]
            t_w = K.dma(pool, buf[:], wv[:, :, cb * 512:(cb + 1) * 512], f"wa{cb % 2}", deps=[mm_tok[cb % 2]])
            i, ps, fr = K.psum()
            for dc in range(16):
                pe("matmul", deps=[t_w, fr, t_scT], sig=False, out=ps[0:17, :], lhsT=scT[:, dc, :], rhs=buf[:, dc, :],
                   start=(dc == 0), stop=False)
            t_mm = pe("matmul", deps=[t_b, t_o1], out=ps[0:17, :], lhsT=B.ones_bf[0:1, 0:17],
                      rhs=bada_bf[0:1, cb * 512:(cb + 1) * 512], start=False, stop=True)
            mm_tok[cb % 2] = t_mm
            t_ev = evac_copy(K, cb % 2, modB[:, cb * 512:(cb + 1) * 512], ps[0:17, :], [t_mm])
            K.psum_release(i, t_ev)
            ev_toks.append(t_ev)
        t_mt = []
        for g in range(4):
            i, ps, fr = K.psum()
            for k in range(24):
                j = g * 24 + k
                t_tr = pe("transpose", deps=[ev_toks, fr], sig=(k == 23), out=ps[:, k * 17:(k + 1) * 17],
                          in_=modB[0:17, j * 128:(j + 1) * 128], identity=B.ident[0:17, 0:17])
            t_e = dve("tensor_copy", deps=[t_tr], out=B.modT[:, g * 24:(g + 1) * 24, :],
                      in_=ps[:, 0:408].rearrange("p (a b) -> p a b", b=17))
            K.psum_release(i, t_e)
            t_mt.append(t_e)

        def bc(lo):
            return B.vecT[:, lo:lo + 16].unsqueeze(2).broadcast_to([128, 16, 17])
        dve("scalar_tensor_tensor", deps=[t_mt, t_vec], out=B.A1[:], in0=B.modT[:, 16:32, :], scalar=1.0, in1=bc(0),
            op0=ALU.add, op1=ALU.mult)
        dve("scalar_tensor_tensor", out=B.A2[:], in0=B.modT[:, 64:80, :], scalar=1.0, in1=bc(32),
            op0=ALU.add, op1=ALU.mult)
        dve("tensor_tensor", out=B.G1[:], in0=B.modT[:, 32:48, :], in1=bc(16), op=ALU.mult)
        dve("tensor_tensor", out=B.G2[:], in0=B.modT[:, 80:96, :], in1=bc(48), op=ALU.mult)
        t_l = dve("tensor_tensor", out=l8[:], in0=B.vecT[:, 72:80], in1=B.vecT[:, 80:88], op=ALU.subtract)
        t_lb = act("activation", deps=[t_l], out=B.lbT[:], in_=l8[:], func=AF.Sigmoid)
        dve("tensor_scalar", deps=[t_lb], out=B.omlT[:], in0=B.lbT[:], scalar1=-1.0, scalar2=1.0, op0=ALU.mult, op1=ALU.add)
        dve("tensor_scalar", deps=[t_lb], out=B.nomlT[:], in0=B.lbT[:], scalar1=1.0, scalar2=-1.0, op0=ALU.mult, op1=ALU.add)
        dve("tensor_copy", deps=[t_c], out=B.ident_bf[:], in_=B.ident[:])
        K.barrier()
        B.dbg("modT", B.modT[:])
        B.dbg("A1", B.A1[:])
        B.dbg("vecT", B.vecT[:])
        B.dbg("lbT", B.lbT[:])


def expand_mod(B, dst, src, deps=None):
    return B.K.dve("tensor_copy", deps=deps, out=dst[:].rearrange("p a (b t) -> p a b t", t=TS),
                   in_=src[:, :, 1:17].unsqueeze(3).broadcast_to([128, 16, NB, TS]))


def phase1(B):
    nc, K = B.nc, B.K
    pe, act, dve, pool, sp = K.pe, K.act, K.dve, K.pool, K.sp
    if True:
        tmp = B.bump([B.REG_BIG, B.REG_L])
        xt = [tmp(f"xt{i}", [128, D], F32) for i in range(2)]
        xs = [tmp(f"xs{i}", [128, D], F32) for i in range(2)]
        junk = tmp("junk", [128, D], BF16)
        ss = tmp("ss", [128, 17], F32)
        rt = tmp("rt", [128, 17], F32)
        rstd = tmp("rstd", [128, 17], F32)
        A1s = tmp("A1s", [128, 16, 128], F32)
        B1s = tmp("B1s", [128, 16, 128], F32)
        tmpS = [tmp(f"tmpS{i}", [128, 4, 128], F32) for i in range(2)]
        t_A1s = expand_mod(B, A1s, B.A1)
        t_B1s = expand_mod(B, B1s, B.modT[:, 0:16, :])
        xt_free = [None, None]
        xs_free = [None, None]
        tiles = [("m", t) for t in range(9)] + [("c", u) for u in range(8)]
        for k, (kind, t) in enumerate(tiles):
            p = k % 2
            src = B.x_main[t * 128:(t + 1) * 128, :] if kind == "m" else B.x_ctx[t * 128:(t + 1) * 128, :]
            t_x = K.dma(sp, xt[p][:], src, f"x{p}", deps=[xt_free[p]])
            CUT = 9
            if CUT == 0:
                xt_free[p] = t_x
                continue
            t_sq = act("activation", deps=[t_x], out=junk[:], in_=xt[p][:], func=AF.Square, accum_out=ss[:, k:k + 1])
            t_sr = act("activation", deps=[t_sq], out=rt[:, k:k + 1], in_=ss[:, k:k + 1], func=AF.Sqrt, scale=1.0 / D, bias=B.eps_t[:, 0:1])
            t_rs = dve("reciprocal", deps=[t_sr], out=rstd[:, k:k + 1], in_=rt[:, k:k + 1])
            t_xs = dve("tensor_scalar", deps=[t_rs, t_x, xs_free[p]], out=xs[p][:], in0=xt[p][:], scalar1=rstd[:, k:k + 1],
                       scalar2=None, op0=ALU.mult)
            xt_free[p] = [t_xs, t_sq]
            if CUT == 1:
                xs_free[p] = t_xs
                continue
            last_tr = None
            for g in range(4):
                i, ps, fr = K.psum()
                for j in range(4):
                    dc = g * 4 + j
                    t_tr = pe("transpose", deps=[t_xs, fr], sig=(j == 3), out=ps[:, j * 128:(j + 1) * 128],
                              in_=xs[p][:, dc * 128:(dc + 1) * 128], identity=B.ident[:])
                last_tr = t_tr
                evs = []
                if CUT == 2:
                    K.psum_release(i, t_tr)
                    continue
                if kind == "m" and t == 8:
                    ts_ = tmpS[g % 2]
                    t_1 = dve("tensor_tensor", deps=[t_tr, t_A1s], out=ts_[:], in0=ps[:].rearrange("p (a b) -> p a b", b=128),
                              in1=A1s[:, g * 4:(g + 1) * 4, :], op=ALU.mult)
                    t_2 = dve("tensor_tensor", deps=[t_1, t_B1s], out=B.hT[:, g * 4:(g + 1) * 4, t * 128:(t + 1) * 128], in0=ts_[:],
                              in1=B1s[:, g * 4:(g + 1) * 4, :], op=ALU.add)
                    evs = [t_1, t_2]
                else:
                    for j in range(4):
                        dc = g * 4 + j
                        dst = (B.hT if kind == "m" else B.hcT)[:, dc, t * 128:(t + 1) * 128]
                        if True:
                            evs.append(dve("tensor_scalar", deps=[t_tr], out=dst, in0=ps[:, j * 128:(j + 1) * 128],
                                           scalar1=B.A1[:, dc, 0:1], scalar2=B.modT[:, dc, 0:1], op0=ALU.mult, op1=ALU.add))
                        else:
                            evs.append(act("activation", deps=[t_tr], out=dst, in_=ps[:, j * 128:(j + 1) * 128], func=AF.Identity,
                                           scale=B.A1[:, dc, 0:1], bias=B.modT[:, dc, 0:1]))
                K.psum_release(i, evs)
            xs_free[p] = last_tr
        K.barrier()
        B.dbg("hT", B.hT[:], BF16)
        B.dbg("hcT", B.hcT[:], BF16)


def rms_rstd_a(B, ps_in, pskey, junk, st, c, n):
    A, K = B.A, B.K
    A.op(K.act, "activation", R=[pskey], W=["junk2", ("st", c)], out=junk, in_=ps_in, func=AF.Square,
         accum_out=st[:, c:c + 1])
    A.op(K.act, "activation", R=[("st", c)], W=[("st", c + 1)], out=st[:, c + 1:c + 2], in_=st[:, c:c + 1],
         func=AF.Sqrt, scale=1.0 / n, bias=B.eps_t[:, 0:1])
    A.op(K.dve, "reciprocal", R=[("st", c + 1)], W=[("st", c + 2)], out=st[:, c + 2:c + 3], in_=st[:, c + 1:c + 2])
    return st[:, c + 2:c + 3], ("st", c + 2)


def phase2(B):
    nc, K, A = B.nc, B.K, B.A
    pe, act, dve, pool, sp = K.pe, K.act, K.dve, K.pool, K.sp
    ps = K.ps
    tmp = B.bump([(OFF_OA, ARENA_BYTES)])
    wm = tmp("wm", [128, 16, 1088], BF16)
    gq_b = tmp("gq_b", [128, QR], F32)
    gkv_b = tmp("gkv_b", [128, KVR], F32)
    cs_t = tmp("cs_t", [128, 17, 64], F32)
    sn_t = tmp("sn_t", [128, 17, 64], F32)
    junk = tmp("junk2", [128, 512], BF16)
    st = tmp("st2", [128, 6 * 17], F32)
    ckv32 = [tmp(f"ckv32_{i}", [128, KVR], F32) for i in range(2)]
    qn32 = [tmp(f"qn32_{i}", [128, QR], F32) for i in range(2)]
    r1 = tmp("r1", [128, 64], F32)
    r2 = tmp("r2", [128, 64], F32)
    kpe32 = [tmp(f"kpe32_{i}", [128, 64], F32) for i in range(2)]
    wv = B.w_in.rearrange("(c p) n -> p c n", p=128)
    A.dma_group(sp, [(gq_b[:], B.g_q.partition_broadcast(128)), (gkv_b[:], B.g_kv.partition_broadcast(128)),
                     (cs_t[:], B.rope_cs.rearrange("(t p) r -> p t r", p=128)),
                     (sn_t[:], B.rope_sn.rearrange("(t p) r -> p t r", p=128))],
                R=[], W=["gq_b", "gkv_b", "cs_t", "sn_t"], stream="c0")
    for j, (c0, n) in enumerate([(0, 512), (512, 512), (1024, 64)]):
        for hh in range(2):
            A.dma(pool, wm[:, hh * 8:(hh + 1) * 8, c0:c0 + n], wv[:, hh * 8:(hh + 1) * 8, O_QD + c0:O_QD + c0 + n],
                  R=[], W=[("wm", j, hh)], stream=f"wm{j}{hh}")
    WM = [("wm", j, hh) for j in range(3) for hh in range(2)]
    A.dma(pool, B.kpeT[64:65, 0:NCX], B.ctx_bias, R=[], W=["kpeT_b"], stream="wmb")
    A.op(dve, "memset", W=["kpeT_b2"], ap=B.kpeT[64:65, NCX:], constant=0.0)
    A.op(dve, "memset", W=[("st", i) for i in range(6 * 17)], ap=st[:], constant=0.0)
    tiles = [("m", t) for t in range(9)] + [("c", u) for u in range(8)]
    for k, (kind, t) in enumerate(tiles):
        p = k % 2
        hsrc = B.hT if kind == "m" else B.hcT
        kcol = (NCX + t * 128) if kind == "m" else t * 128
        kidx = kcol // 128
        ti = t if kind == "m" else 9 + t
        cols = slice(t * 128, (t + 1) * 128)
        A.mm([f"ps{p}"], WM, [MM(ps[p][:, 0:512], hsrc[:, dc, cols], wm[:, dc, 512:1024], dc == 0, dc == 15) for dc in range(16)])
        A.mm(["ps4"], WM, [MM(ps[4][:, 0:64], hsrc[:, dc, cols], wm[:, dc, 1024:1088], dc == 0, dc == 15) for dc in range(16)])
        if kind == "m":
            A.mm([f"ps{2 + p}"], WM, [MM(ps[2 + p][:, 0:512], hsrc[:, dc, cols], wm[:, dc, 0:512], dc == 0, dc == 15) for dc in range(16)])
        rs, rk = rms_rstd_a(B, ps[p][:, 0:512], f"ps{p}", junk[:], st, 3 * k, KVR)
        A.op(dve, "scalar_tensor_tensor", R=[f"ps{p}", rk, "gkv_b"], W=[f"ckv32_{p}"], out=ckv32[p][:], in0=ps[p][:, 0:512],
             scalar=rs, in1=gkv_b[:], op0=ALU.mult, op1=ALU.mult)
        if kind == "m":
            A.dma(sp, B.ckv_main[cols, :], ckv32[p][:], R=[f"ckv32_{p}"], W=[], stream=f"ock{p}")
        A.op(pool, "tensor_copy", R=[f"ckv32_{p}"], W=[("ckv_tm", kidx)], out=B.ckv_tm[:, kidx, :], in_=ckv32[p][:])
        A.mm(["ps5"], [f"ckv32_{p}"], [TR(ps[5][:, cc * 128:(cc + 1) * 128], ckv32[p][:, cc * 128:(cc + 1) * 128], B.ident[:])
                                      for cc in range(4)])
        A.op(act, "copy", R=["ps5"], W=[("ckvT", kidx)], out=B.ckvT[:, :, kcol:kcol + 128],
             in_=ps[5][:].rearrange("p (a b) -> p a b", b=128))
        A.op(dve, "tensor_tensor", R=["ps4", "cs_t"], W=["r1"], out=r1[:], in0=ps[4][:, 0:64], in1=cs_t[:, ti, :], op=ALU.mult)
        A.op(dve, "tensor_tensor", R=["ps4", "sn_t"], W=["r2a"], out=r2[:, 0:32], in0=ps[4][:, 32:64], in1=sn_t[:, ti, 0:32], op=ALU.mult)
        A.op(dve, "tensor_tensor", R=["ps4", "sn_t"], W=["r2b"], out=r2[:, 32:64], in0=ps[4][:, 0:32], in1=sn_t[:, ti, 32:64], op=ALU.mult)
        A.op(dve, "tensor_tensor", R=["r1", "r2a", "r2b"], W=[f"kpe32_{p}"], out=kpe32[p][:], in0=r1[:], in1=r2[:], op=ALU.add)
        if kind == "m":
            A.dma(sp, B.kpe_main[cols, :], kpe32[p][:], R=[f"kpe32_{p}"], W=[], stream=f"okp{p}")
        A.mm(["ps7"], [f"kpe32_{p}"], [TR(ps[7][0:64, 0:128], kpe32[p][:, 0:64], B.ident[:])])
        A.op(act, "copy", R=["ps7"], W=[("kpeT", kidx)], out=B.kpeT[0:64, kcol:kcol + 128], in_=ps[7][0:64, 0:128])
        if kind == "m":
            rs, rk = rms_rstd_a(B, ps[2 + p][:, 0:512], f"ps{2 + p}", junk[:], st, 51 + 3 * k, QR)
            A.op(dve, "scalar_tensor_tensor", R=[f"ps{2 + p}", rk, "gq_b"], W=[f"qn32_{p}"], out=qn32[p][:], in0=ps[2 + p][:, 0:512],
                 scalar=rs, in1=gq_b[:], op0=ALU.mult, op1=ALU.mult)
            A.mm(["ps6"], [f"qn32_{p}"], [TR(ps[6][:, cc * 128:(cc + 1) * 128], qn32[p][:, cc * 128:(cc + 1) * 128], B.ident[:])
                                         for cc in range(4)])
            A.op(act, "copy", R=["ps6"], W=[("qnT", t)], out=B.qnT[:, :, cols], in_=ps[6][:].rearrange("p (a b) -> p a b", b=128))
    if B.debug and B.stop_after == 2:
        B.dbg("ckvT", B.ckvT[:], BF16)
        B.dbg("kpeT", B.kpeT[:], BF16)
        B.dbg("qnT", B.qnT[:], BF16)


def phase3(B):
    nc, K, A = B.nc, B.K, B.A
    pe, act, dve, pool, sp = K.pe, K.act, K.dve, K.pool, K.sp
    ps = K.ps
    tmp = B.bump([(OFF_OB, ARENA_BYTES), (OFF_C, OFF_LAT)])
    Sctx = tmp("Sctx", [128, 8, 128], F32)
    wf = tmp("wf", [128, 16, 128], BF16)
    wi = tmp("wi", [128, 16, 128], BF16)
    sig = tmp("sig", [128, 512], F32)
    logf = tmp("logf", [128, 512], F32)
    bT = tmp("bT", [128, 512], F32)
    kT = tmp("kT", [128, 512], F32)
    e1 = tmp("e1", [128, 512], F32)
    ke32 = tmp("ke32", [128, 512], F32)
    kdT = tmp("kdT", [128, 512], F32)
    kd_tm = tmp("kd_tm", [128, 4, 128], BF16)
    v_tm = tmp("v_tm", [128, 4, 128], BF16)
    ebl = tmp("ebl", [128, 16], F32)
    S32 = tmp("S32", [128, 128], F32)
    S_bf = tmp("S_bf", [128, 128], BF16)
    cm64 = tmp("cm64", [128, 512], F32)
    cm8 = tmp("cm8", [128, 128], F32)
    wq = tmp("wq", [128, 16, 128], BF16)
    wg = tmp("wg", [128, 16, 128], BF16)
    qT = tmp("qT", [128, 512], F32)
    qeT = tmp("qeT", [128, 512], BF16)
    keT = tmp("keT", [128, 512], BF16)
    AT = [tmp(f"AT{i}", [128, 128], BF16) for i in range(2)]
    oT = tmp("oT", [128, 512], F32)
    sq = tmp("sq", [128, 512], BF16)
    rstd = tmp("rstd", [128, 512], F32)
    sg = tmp("sg", [128, 512], F32)
    t1 = tmp("t1", [128, 512], F32)
    Sin32 = tmp("Sin32", [128, 16, 128], F32)
    Sin_bf = tmp("Sin_bf", [128, 16, 128], BF16)
    Sout32 = tmp("Sout32", [128, 16, 128], F32)
    kdm = [tmp(f"kdm{i}", [128, 128], BF16) for i in range(2)]
    wv = B.w_in.rearrange("(c p) n -> p c n", p=128)
    M64 = B.masks[:, 0:128]
    M8 = B.masks[:, 128:256]
    IND = B.masks[:, 760:776]
    ghT = B.vecT[:, 88:96]

    A.op(dve, "memset", W=["cm64"], ap=cm64[:], constant=1.0)
    A.op(dve, "memset", W=["cm64"], ap=cm64[:].rearrange("p (c t) -> p c t", t=64)[:, :, 0:1], constant=0.0)
    A.op(dve, "memset", W=["cm8"], ap=cm8[:], constant=1.0)
    A.op(dve, "memset", W=["cm8"], ap=cm8[:].rearrange("p (c t) -> p c t", t=8)[:, :, 0:1], constant=0.0)

    def ldw(dst, key, col0):
        for hh in range(2):
            A.dma(pool, dst[:, hh * 8:(hh + 1) * 8, :], wv[:, hh * 8:(hh + 1) * 8, col0:col0 + 128], R=[], W=[(key, hh)],
                  stream=f"{key}{hh}")
        return [(key, 0), (key, 1)]

    def proj_T(bank, w, wk, hsrc, c0, n):
        A.mm([f"ps{bank}"], wk, [MM(ps[bank][:, 0:n], w[:, dc, :], hsrc[:, dc, c0:c0 + n], dc == 0, dc == 15) for dc in range(16)])

    for is_ctx in (True, False):
        hsrc = B.hcT if is_ctx else B.hT
        blks = CBLKS if is_ctx else BLKS
        for h in range(NH):
            wfk = ldw(wf, "wf", O_FA + h * 128)
            wik = ldw(wi, "wi", O_IA + h * 128)
            if not is_ctx:
                wqk = ldw(wq, "wq", O_QA + h * 128)
                wgk = ldw(wg, "wg", O_GA + h * 128)
            if is_ctx:
                A.op(dve, "memset", W=["S32"], ap=S32[:], constant=0.0)
            else:
                A.op(dve, "tensor_copy", R=[("Sctx", h)], W=["S32"], out=S32[:], in_=Sctx[:, h, :])
                A.op(act, "copy", R=["S32"], W=["S_bf"], out=S_bf[:], in_=S32[:])
            for (c0, n) in blks:
                sample = (c0 == NTP) and not is_ctx
                tiles = n // 128
                csz = 8 if sample else 64
                nch = n // csz
                cm = cm8 if sample else cm64
                proj_T(0, wf, wfk, hsrc, c0, n)
                A.op(act, "activation", R=["ps0"], W=["sig"], out=sig[:, 0:n], in_=ps[0][:, 0:n], func=AF.Sigmoid)
                A.op(dve, "tensor_scalar", R=["sig"], W=["kT"], out=kT[:, 0:n], in0=sig[:, 0:n], scalar1=B.omlT[:, h:h + 1],
                     scalar2=B.lbT[:, h:h + 1], op0=ALU.mult, op1=ALU.add)
                A.op(act, "activation", R=["kT"], W=["logf"], out=logf[:, 0:n], in_=kT[:, 0:n], func=AF.Ln)
                A.op(dve, "tensor_scalar", R=["sig", "logf"], W=["kT"], out=kT[:, 0:n], in0=sig[:, 0:n], scalar1=B.nomlT[:, h:h + 1],
                     scalar2=B.omlT[:, h:h + 1], op0=ALU.mult, op1=ALU.add)
                A.op(dve, "tensor_tensor_scan", R=["logf", "cm64", "cm8"], W=["bT"], out=bT[:, 0:n], data0=cm[:, 0:n],
                     data1=logf[:, 0:n], initial=0.0, op0=ALU.mult, op1=ALU.add)
                A.op(act, "activation", R=["bT"], W=["ebl"], out=ebl[:, 0:nch],
                     in_=bT[:, 0:n].rearrange("p (c t) -> p c t", t=csz)[:, :, csz - 1], func=AF.Exp)
                A.op(act, "activation", R=["bT"], W=["e1"], out=e1[:, 0:n], in_=bT[:, 0:n], func=AF.Exp, scale=-1.0)
                A.op(dve, "tensor_tensor", R=["kT", "e1"], W=["ke32"], out=ke32[:, 0:n], in0=kT[:, 0:n], in1=e1[:, 0:n], op=ALU.mult)
                A.op(dve, "tensor_tensor", R=["ke32", "ebl"], W=["kdT"], out=kdT[:, 0:n].rearrange("p (c t) -> p c t", t=csz),
                     in0=ke32[:, 0:n].rearrange("p (c t) -> p c t", t=csz),
                     in1=ebl[:, 0:nch].unsqueeze(2).broadcast_to([128, nch, csz]), op=ALU.mult)
                if not is_ctx:
                    A.op(pool, "tensor_copy", R=["ke32"], W=["keT"], out=keT[:, 0:n], in_=ke32[:, 0:n])
                    proj_T(1, wq, wqk, hsrc, c0, n)
                    A.op(act, "activation", R=["ps1"], W=["qT"], out=qT[:, 0:n], in_=ps[1][:, 0:n], func=AF.Silu)
                    A.op(act, "activation", R=["bT"], W=["e1"], out=e1[:, 0:n], in_=bT[:, 0:n], func=AF.Exp)
                    A.op(dve, "tensor_tensor", R=["qT", "e1"], W=["qeT"], out=qeT[:, 0:n], in0=qT[:, 0:n], in1=e1[:, 0:n], op=ALU.mult)
                    proj_T(2, wg, wgk, hsrc, c0, n)
                    A.op(act, "activation", R=["ps2"], W=["sg"], out=sg[:, 0:n], in_=ps[2][:, 0:n], func=AF.Sigmoid)
                for t in range(tiles):
                    A.mm(["ps3"], wik, [MM(ps[3][:, t * 128:(t + 1) * 128], hsrc[:, dc, c0 + t * 128:c0 + (t + 1) * 128], wi[:, dc, :],
                                           dc == 0, dc == 15) for dc in range(16)])
                A.op(act, "copy", R=["ps3"], W=["v_tm"], out=v_tm[:, 0:tiles, :], in_=ps[3][:, 0:n].rearrange("p (a b) -> p a b", b=128))
                A.mm(["ps4"], ["kdT"], [TR(ps[4][:, t * 128:(t + 1) * 128], kdT[:, t * 128:(t + 1) * 128], B.ident[:]) for t in range(tiles)])
                A.op(dve, "tensor_copy", R=["ps4"], W=["kd_tm"], out=kd_tm[:, 0:tiles, :], in_=ps[4][:, 0:n].rearrange("p (a b) -> p a b", b=128))
                if not sample:
                    for t in range(tiles):
                        tc_ = slice(t * 128, (t + 1) * 128)
                        if not is_ctx:
                            A.mm(["ps5"], ["keT", "qeT"], [MM(ps[5][:, 0:128], keT[:, tc_], qeT[:, tc_], True, True)])
                            A.op(dve, "tensor_tensor", R=["ps5"], W=[f"AT{t % 2}"], out=AT[t % 2][:], in0=ps[5][:, 0:128], in1=M64, op=ALU.mult)
                        for j in range(2):
                            ch = t * 2 + j
                            cc_ = slice(t * 128 + j * 64, t * 128 + (j + 1) * 64)
                            if not is_ctx:
                                A.mm(["ps6"], ["S_bf", "qeT"], [MM(ps[6][:, cc_], S_bf[:], qeT[:, cc_], j == 0, False)])
                            A.mm(["ps7"], ["kd_tm", "v_tm"], [MM(ps[7][:, 0:128], kd_tm[j * 64:(j + 1) * 64, t, :], v_tm[j * 64:(j + 1) * 64, t, :],
                                                                True, True)])
                            A.op(dve, "scalar_tensor_tensor", R=["ps7", "ebl", "S32"], W=["S32"], out=S32[:], in0=S32[:], scalar=ebl[:, ch:ch + 1],
                                 in1=ps[7][:, 0:128], op0=ALU.mult, op1=ALU.add)
                            if not is_ctx:
                                A.op(act, "copy", R=["S32"], W=["S_bf"], out=S_bf[:], in_=S32[:])
                        if not is_ctx:
                            A.mm(["ps6"], ["v_tm", f"AT{t % 2}"], [MM(ps[6][:, tc_], v_tm[:, t, :], AT[t % 2][:], False, True)])
                    if not is_ctx and c0 + n == NTP:
                        A.dma(sp, B.s_prompt[h], S32[:], R=["S32"], W=[], stream="osp")
                else:
                    A.dma(sp, Sin32[:], B.state_in[:, h].rearrange("b k v -> k b v"), R=[], W=["Sin32"], stream="sin")
                    A.op(pool, "tensor_copy", R=["Sin32"], W=["Sin_bf"], out=Sin_bf[:], in_=Sin32[:])
                    A.mm(["ps5"], ["keT", "qeT"], [MM(ps[5][:, 0:128], keT[:, 0:128], qeT[:, 0:128], True, True)])
                    A.op(dve, "tensor_tensor", R=["ps5"], W=["AT0"], out=AT[0][:], in0=ps[5][:, 0:128], in1=M8, op=ALU.mult)
                    A.mm(["ps6"], ["Sin_bf", "qeT"], [MM(ps[6][:, b * 8:(b + 1) * 8], Sin_bf[:, b, :], qeT[:, b * 8:(b + 1) * 8], b == 0, False)
                                                      for b in range(NB)])
                    A.mm(["ps6"], ["v_tm", "AT0"], [MM(ps[6][:, 0:128], v_tm[:, 0, :], AT[0][:], False, True)])
                    for b in range(NB):
                        pb = 7 if b % 2 == 0 else 4
                        A.op(dve, "tensor_scalar", R=["kd_tm"], W=[f"kdm{b % 2}"], out=kdm[b % 2][:], in0=kd_tm[:, 0, :], scalar1=IND[:, b:b + 1],
                             scalar2=None, op0=ALU.mult)
                        A.mm([f"ps{pb}"], [f"kdm{b % 2}", "v_tm"], [MM(ps[pb][:, 0:128], kdm[b % 2][:], v_tm[:, 0, :], True, True)])
                        A.op(dve, "scalar_tensor_tensor", R=[f"ps{pb}", "ebl", "Sin32"], W=["Sout32"], out=Sout32[:, b, :], in0=Sin32[:, b, :],
                             scalar=ebl[:, b:b + 1], in1=ps[pb][:, 0:128], op0=ALU.mult, op1=ALU.add)
                    A.dma(sp, B.s_sample[:, h].rearrange("b k v -> k b v"), Sout32[:], R=["Sout32"], W=[], stream="oss")
                if not is_ctx:
                    A.op(act, "copy", R=["ps6"], W=["oT"], out=oT[:, 0:n], in_=ps[6][:, 0:n])
                    A.op(pool, "tensor_tensor", R=["oT"], W=["sq"], out=sq[:, 0:n], in0=oT[:, 0:n], in1=oT[:, 0:n], op=ALU.mult)
                    A.mm(["ps0"], ["sq"], [MM(ps[0][:, 0:n], B.ones_bf[:], sq[:, 0:n], True, True)])
                    A.op(act, "activation", R=["ps0"], W=["rstd"], out=rstd[:, 0:n], in_=ps[0][:, 0:n], func=AF.Sqrt, scale=1.0 / DV,
                         bias=B.eps_t[:, 0:1])
                    A.op(dve, "reciprocal", R=["rstd"], W=["rstd"], out=rstd[:, 0:n], in_=rstd[:, 0:n])
                    A.op(dve, "scalar_tensor_tensor", R=["oT", "rstd"], W=["t1"], out=t1[:, 0:n], in0=oT[:, 0:n], scalar=ghT[:, h:h + 1],
                         in1=rstd[:, 0:n], op0=ALU.mult, op1=ALU.mult)
                    A.op(dve, "tensor_tensor", R=["t1", "sg"], W=[("oaT", h, c0)], out=B.oaT[:, h, c0:c0 + n], in0=t1[:, 0:n], in1=sg[:, 0:n],
                         op=ALU.mult)
            if is_ctx:
                A.op(dve, "tensor_scalar", R=["S32"], W=[("Sctx", h)], out=Sctx[:, h, :], in0=S32[:], scalar1=B.cflag[:, 0:1], scalar2=None,
                     op0=ALU.mult)
    if B.debug and B.stop_after == 3:
        B.dbg("oaT", B.oaT[:], BF16)


def phase4(B):
    nc, K, A = B.nc, B.K, B.A
    pe, act, dve, pool, sp = K.pe, K.act, K.dve, K.pool, K.sp
    ps = K.ps
    psb = [p[:].bitcast(BF16) for p in ps]
    regions = [(OFF_C, OFF_LAT), (OFF_T, ARENA_BYTES)]
    tmp = B.bump(regions)
    wkv = tmp("wkv", [128, 4, 2048], BF16)
    qsT = tmp("qsT", [128, 5, NB, 64], BF16)
    OFF_4B = OFF_C + 2 * 4 * 2048 + 2 * 5 * NB * 64
    r1 = tmp("r1q", [64, 512], F32)
    r2 = tmp("r2q", [64, 512], F32)
    wqh = tmp("wqh", [128, 4, 192], BF16)
    tmp = B.bump([(OFF_T, ARENA_BYTES)])
    w_ukT = tmp("w_ukT", [128, 8, 512], BF16)
    q_nopeT = tmp("q_nopeT", [128, NT], BF16)
    q_peT = tmp("q_peT", [128, NT], BF16)
    csT = tmp("csT", [64, NT], F32)
    snT = tmp("snT", [64, NT], F32)
    q_latT = tmp("q_latT", [128, 4, 512], BF16)
    pT = [tmp(f"pT{i}", [128, 512], BF16) for i in range(2)]
    olat_sb = tmp("olat_sb", [128, 4, 512], BF16)
    rden = tmp("rden", [128, 512], F32)
    MC = B.masks[:, 256:384]
    LAT = ["lat"]

    wkv_v = B.w_kv_up.rearrange("(c p) n -> p c n", p=128)
    for cc in range(4):
        for q4 in range(4):
            A.dma(pool, wkv[:, cc, q4 * 512:(q4 + 1) * 512], wkv_v[:, cc, q4 * 512:(q4 + 1) * 512], R=[], W=[("wkv", cc, q4)], stream=f"wkv{q4}")
    WKV = [("wkv", cc, q4) for cc in range(4) for q4 in range(4)]
    A.dma_group(sp, [(csT[:], B.ropeT_cs), (snT[:], B.ropeT_sn)], R=[], W=["csT", "snT"], stream="c0")
    A.op(dve, "memset", W=["q_peT1"], ap=q_peT[64:65, :], constant=1.0)
    for h in range(NH):
        A.mm(["ps7"], WKV, [TR(psb[7][:, cc * 128:(cc + 1) * 128], wkv[:, cc, h * 256:h * 256 + 128], B.ident_bf[:]) for cc in range(4)])
        A.op(act if h % 2 else dve, "copy" if h % 2 else "tensor_copy", R=["ps7"], W=[("w_ukT", h)], out=w_ukT[:, h, :], in_=psb[7][:, 0:512])
    wq_v = B.w_q_up.rearrange("(c p) n -> p c n", p=128)
    for h in range(NH):
        A.dma(pool, wqh[:], wq_v[:, :, h * 192:(h + 1) * 192], R=[], W=["wqh"], stream="wqh")
        for bi, (c0, n) in enumerate(BLKS):
            cols = slice(c0, c0 + n)
            A.mm(["ps7"], ["wqh"], [MM(ps[7][:, 0:n], wqh[:, cc, 0:128], B.qnT[:, cc, cols], cc == 0, cc == 3) for cc in range(4)])
            A.op(act, "copy", R=["ps7"], W=[("q_nopeT", bi)], out=q_nopeT[:, cols], in_=ps[7][:, 0:n])
            A.mm(["ps5"], ["wqh"], [MM(ps[5][0:64, 0:n], wqh[:, cc, 128:192], B.qnT[:, cc, cols], cc == 0, cc == 3) for cc in range(4)])
            A.mm(["ps6"], ["wqh"], [MM(ps[6][0:32, 0:n], wqh[:, cc, 160:192], B.qnT[:, cc, cols], cc == 0, cc == 3) for cc in range(4)]
                 + [MM(ps[6][32:64, 0:n], wqh[:, cc, 128:160], B.qnT[:, cc, cols], cc == 0, cc == 3) for cc in range(4)])
            A.op(dve, "tensor_tensor", R=["ps5", "csT"], W=["r1q"], out=r1[:, 0:n], in0=ps[5][0:64, 0:n], in1=csT[:, cols], op=ALU.mult)
            A.op(dve, "tensor_tensor", R=["ps6", "snT"], W=["r2q"], out=r2[:, 0:n], in0=ps[6][0:64, 0:n], in1=snT[:, cols], op=ALU.mult)
            A.op(dve, "tensor_tensor", R=["r1q", "r2q"], W=[("q_peT", bi)], out=q_peT[0:64, cols], in0=r1[:, 0:n], in1=r2[:, 0:n], op=ALU.add)
        for cc in range(4):
            A.mm(["ps7"], [("w_ukT", h), ("q_nopeT", 2)], [MM(ps[7][:, 0:128], w_ukT[:, h, cc * 128:(cc + 1) * 128], q_nopeT[:, NTP:NT], True, True)])
            A.op(act if cc % 2 else dve, "copy" if cc % 2 else "tensor_copy", R=["ps7"], W=[("qsT", h)], out=qsT[:, cc, :, h * 8:(h + 1) * 8],
                 in_=ps[7][:, 0:128].rearrange("p (b t) -> p b t", t=TS))
        A.op(dve, "tensor_copy", R=[("q_peT", 2)], W=[("qsT", h)], out=qsT[0:64, 4, :, h * 8:(h + 1) * 8],
             in_=q_peT[0:64, NTP:NT].rearrange("p (b t) -> p b t", t=TS))
        for qb in range(2):
            c0 = qb * 512
            for cc in range(4):
                A.mm(["ps7"], [("w_ukT", h), ("q_nopeT", qb)], [MM(ps[7][:, 0:512], w_ukT[:, h, cc * 128:(cc + 1) * 128], q_nopeT[:, c0:c0 + 512], True, True)])
                A.op(act if cc % 2 else dve, "copy" if cc % 2 else "tensor_copy", R=["ps7"], W=["q_latT"], out=q_latT[:, cc, :], in_=ps[7][:, 0:512])
            keys = [(u, 0, False) for u in range(8)]
            for j in range(4 * qb + 4):
                keys.append((8 + j, 128 * max(0, j - 4 * qb), j >= 4 * qb))
            for i, (kidx, cs, diag) in enumerate(keys):
                sbk = 5 + i % 2
                kc = slice(kidx * 128, (kidx + 1) * 128)
                last = i == len(keys) - 1
                A.mm([f"ps{sbk}"], LAT + ["q_latT", ("q_peT", qb), "q_peT1"],
                     [MM(ps[sbk][:, cs:512], B.ckvT[:, cc, kc], q_latT[:, cc, cs:512], cc == 0, False) for cc in range(4)]
                     + [MM(ps[sbk][:, cs:512], B.kpeT[0:65, kc], q_peT[0:65, c0 + cs:c0 + 512], False, True)])
                A.op(act, "activation", R=[f"ps{sbk}"], W=[f"pT{i % 2}"], out=pT[i % 2][:, cs:512], in_=ps[sbk][:, cs:512], func=AF.Exp, scale=SCALE)
                if diag:
                    A.op(pool, "tensor_tensor", R=[f"pT{i % 2}"], W=[f"pT{i % 2}"], out=pT[i % 2][:, cs:cs + 128], in0=pT[i % 2][:, cs:cs + 128],
                         in1=MC, op=ALU.mult)
                for cc in range(4):
                    A.mm([f"ps{cc}"], LAT + [f"pT{i % 2}"], [MM(ps[cc][:, cs:512], B.ckv_tm[:, kidx, cc * 128:(cc + 1) * 128], pT[i % 2][:, cs:512], i == 0, last)])
                A.mm(["ps4"], [f"pT{i % 2}"], [MM(ps[4][:, cs:512], B.ones_bf[:], pT[i % 2][:, cs:512], i == 0, last)])
            for cc in range(4):
                A.op(act if cc % 2 else dve, "copy" if cc % 2 else "tensor_copy", R=[f"ps{cc}"], W=["olat_sb"], out=olat_sb[:, cc, :], in_=ps[cc][:, 0:512])
            A.op(dve, "reciprocal", R=["ps4"], W=["rden"], out=rden[:], in_=ps[4][:, 0:512])
            A.mm(["ps7"], ["olat_sb"] + WKV, [MM(ps[7][:, 0:512], wkv[:, cc, h * 256 + 128:h * 256 + 256], olat_sb[:, cc, :], cc == 0, cc == 3) for cc in range(4)])
            A.op(dve, "tensor_tensor", R=["ps7", "rden"], W=[("obT", h, qb)], out=B.obT[:, h, c0:c0 + 512], in0=ps[7][:, 0:512], in1=rden[:], op=ALU.mult)

    K.barrier()
    A.reset()
    tmp = B.bump([(OFF_4B, OFF_LAT), (OFF_T, ARENA_BYTES)])
    idx_i = tmp("idx_i", [128, NB * NPAGES], I32)
    ptab_i = tmp("ptab_i", [128, NB * NPAGES], I32)
    ptab_f = tmp("ptab_f", [128, NB * NPAGES], F32)
    smask = tmp("smask", [128, NB * 64], F32)
    NSLOT = 4
    pg32 = [tmp(f"pg32_{i}", [128, 576], F32) for i in range(NSLOT)]
    pg_bf = [tmp(f"pg_bf{i}", [128, 576], BF16) for i in range(NSLOT)]
    KT_pg = [tmp(f"KT_pg{i}", [128, 640], BF16) for i in range(2)]
    pTs = [tmp(f"pTs{i}", [128, 64], BF16) for i in range(2)]
    olat_s = tmp("olat_s", [64, 512], BF16)
    olatT_s = tmp("olatT_s", [128, 256], BF16)
    rden_s = tmp("rden_s", [64, 1], F32)
    PIOTA = B.masks[:, 800:801]
    A.dma_group(sp, [(ptab_i[:], B.ptab.partition_broadcast(128)), (smask[:], B.smask_d)], R=[], W=["ptab_i", "smask"], stream="c0")
    A.op(dve, "tensor_copy", R=["ptab_i"], W=["ptab_f"], out=ptab_f[:], in_=ptab_i[:])
    A.op(dve, "tensor_scalar", R=["ptab_f"], W=["idx_i"], out=idx_i[:], in0=ptab_f[:], scalar1=128.0, scalar2=PIOTA, op0=ALU.mult, op1=ALU.add)
    pages = [(b, j) for b in range(NB) for j in range(NPAGES + 1)]
    NPG = len(pages)

    def stA(n):
        b, j = pages[n]
        if j == NPAGES:
            return
        sl = n % NSLOT
        col = b * NPAGES + j
        A.dma(pool, pg32[sl][:], B.cache_all, R=["idx_i"], W=[("pg32", sl)], stream=f"pgc{sl}", indirect=idx_i[:, col:col + 1])

    def stB(n):
        b, j = pages[n]
        if j == NPAGES:
            return
        sl = n % NSLOT
        A.op(dve, "tensor_copy", R=[("pg32", sl)], W=[("pg_bf", sl)], out=pg_bf[sl][:], in_=pg32[sl][:])

    def stC(n):
        b, j = pages[n]
        if j == NPAGES:
            return
        sl, s2 = n % NSLOT, n % 2
        tb = 2 + s2
        A.mm([f"ps{tb}"], [("pg_bf", sl)], [TR(psb[tb][:, i * 128:(i + 1) * 128], pg_bf[sl][:, i * 128:(i + 1) * 128], B.ident_bf[:]) for i in range(4)]
             + [TR(psb[tb][0:64, 512:640], pg_bf[sl][:, 512:576], B.ident_bf[:])])
        A.op(act, "copy", R=[f"ps{tb}"], W=[("KT", s2, 0)], out=KT_pg[s2][:, 0:512], in_=psb[tb][:, 0:512])
        A.op(act, "copy", R=[f"ps{tb}"], W=[("KT", s2, 1)], out=KT_pg[s2][0:64, 512:640], in_=psb[tb][0:64, 512:640])

    def stD(n):
        b, j = pages[n]
        s2 = n % 2
        qv = [qsT[:, i, b, :] for i in range(4)]
        qp = qsT[0:64, 4, b, :]
        if j < NPAGES:
            kts = [KT_pg[s2][:, i * 128:(i + 1) * 128] for i in range(4)]
            ktp = KT_pg[s2][0:64, 512:640]
            Rk = [("KT", s2, 0), ("KT", s2, 1)]
        else:
            kts = [B.ckvT[:, i, NCX + NTP:NKEY] for i in range(4)]
            ktp = B.kpeT[0:64, NCX + NTP:NKEY]
            Rk = []
        sbk = 4 + s2
        A.mm([f"ps{sbk}"], Rk, [MM(ps[sbk][:, 0:64], kts[i], qv[i], i == 0, False) for i in range(4)] + [MM(ps[sbk][:, 0:64], ktp, qp, False, True)])
        A.op(act, "activation", R=[f"ps{sbk}"], W=[f"pTs{s2}"], out=pTs[s2][:], in_=ps[sbk][:, 0:64], func=AF.Exp, scale=SCALE)
        if j == NPAGES:
            A.op(dve, "tensor_tensor", R=[f"pTs{s2}", "smask"], W=[f"pTs{s2}"], out=pTs[s2][:], in0=pTs[s2][:], in1=smask[:, b * 64:(b + 1) * 64], op=ALU.mult)

    def stE(n):
        b, j = pages[n]
        sl, s2 = n % NSLOT, n % 2
        if j < NPAGES:
            V = pg_bf[sl][:, 0:512]
            Rv = [("pg_bf", sl)]
        else:
            V = B.ckv_tm[:, 16, :]
            Rv = []
        A.mm(["ps0"], Rv + [f"pTs{s2}"], [MM(ps[0][0:64, 0:512], pTs[s2][:], V, j == 0, j == NPAGES)])
        A.mm(["ps1"], [f"pTs{s2}"], [MM(ps[1][0:64, 0:1], pTs[s2][:], B.ones_bf[:, 0:1], j == 0, j == NPAGES)])
        if j < NPAGES:
            return
        A.op(dve, "reciprocal", R=["ps1"], W=["rden_s"], out=rden_s[:], in_=ps[1][0:64, 0:1])
        A.op(dve, "tensor_scalar", R=["ps0", "rden_s"], W=["olat_s"], out=olat_s[:], in0=ps[0][0:64, 0:512], scalar1=rden_s[:, 0:1], scalar2=None, op0=ALU.mult)
        A.mm(["ps6"], ["olat_s"], [TR(psb[6][:, cc * 64:(cc + 1) * 64], olat_s[0:64, cc * 128:(cc + 1) * 128], B.ident_bf[0:64, 0:64]) for cc in range(4)])
        A.op(act, "copy", R=["ps6"], W=["olatT_s"], out=olatT_s[:], in_=psb[6][:, 0:256])
        A.mm(["ps7"], ["olatT_s"], [MM(ps[7][:, h * 8:(h + 1) * 8], wkv[:, cc, h * 256 + 128:h * 256 + 256], olatT_s[:, cc * 64 + h * 8:cc * 64 + (h + 1) * 8],
                                       (h == 0 and cc == 0), (h == NH - 1 and cc == 3)) for h in range(NH) for cc in range(4)])
        A.op(dve, "tensor_copy", R=["ps7"], W=[("obT_s", b)], out=B.obT[:, :, NTP + b * 8:NTP + (b + 1) * 8],
             in_=ps[7][:, 0:64].rearrange("p (h t) -> p h t", t=TS))

    for i in range(-6, NPG):
        for (st_, off) in ((stA, 6), (stB, 3), (stC, 2), (stD, 1), (stE, 0)):
            if 0 <= i + off < NPG:
                st_(i + off)
    if B.debug and B.stop_after == 4:
        B.dbg("obT", B.obT[:], BF16)


OFF_M = OFF_LAT
OFF_H2 = OFF_LAT + 36864


def build_G_tm(B, G_fm, GB, G_tm, sel, sel_lo, tag):
    K, A = B.K, B.A
    ps = K.ps
    for cb in range(4):
        A.mm([f"ps{cb}"], [tag + "GB", "sel"], [MM(ps[cb][:, 0:512], sel[0:17, sel_lo:sel_lo + 128], GB[0:17, cb * 512:(cb + 1) * 512], True, True)])
        A.op(K.act if cb % 2 else K.dve, "copy" if cb % 2 else "tensor_copy", R=[f"ps{cb}"], W=["G_tm"], out=G_tm[:, cb * 512:(cb + 1) * 512], in_=ps[cb][:, 0:512])


def build_GB(B, G_fm, GB, tag):
    K, A = B.K, B.A
    ps = K.ps
    for cb in range(4):
        A.mm([f"ps{cb}"], [], [TR(ps[cb][0:17, j * 128:(j + 1) * 128], G_fm[:, cb * 4 + j, :], B.ident[:]) for j in range(4)])
        A.op(K.act if cb % 2 else K.dve, "copy" if cb % 2 else "tensor_copy", R=[f"ps{cb}"], W=[tag + "GB"], out=GB[0:17, cb * 512:(cb + 1) * 512], in_=ps[cb][0:17, 0:512])


def phase5(B):
    nc, K, A = B.nc, B.K, B.A
    pe, act, dve, pool, sp = K.pe, K.act, K.dve, K.pool, K.sp
    ps = K.ps
    mT = B.view(OFF_M, [128, 16, NT], BF16)
    h2T = B.view(OFF_H2, [128, 16, NT], BF16)
    tw = B.bump([(OFF_C, OFF_LAT)])
    wa = [tw(f"wa{i}", [128, 8, 128], BF16) for i in range(2)]
    wb = [tw(f"wb{i}", [128, 8, 128], BF16) for i in range(2)]
    wga = [tw(f"wga{i}", [128, 16, 128], BF16) for i in range(2)]
    wgb = [tw(f"wgb{i}", [128, 16, 128], BF16) for i in range(2)]
    tt = B.bump([(OFF_T, ARENA_BYTES)])
    sga = [tt(f"sga{i}", [128, 512], F32) for i in range(2)]
    sgb = [tt(f"sgb{i}", [128, 512], F32) for i in range(2)]
    t1 = [tt(f"t1_{i}", [128, 512], F32) for i in range(2)]
    t2 = [tt(f"t2_{i}", [128, 512], F32) for i in range(2)]
    wv = B.w_in.rearrange("(c p) n -> p c n", p=128)
    wav = B.w_a_out.rearrange("(c p) n -> p c n", p=128)
    wbv = B.w_b_out.rearrange("(c p) n -> p c n", p=128)
    it = 0
    for fo in range(16):
        s = fo % 2
        fc = slice(fo * 128, (fo + 1) * 128)
        A.dma(pool, wa[s][:], wav[:, :, fc], R=[], W=[f"wa{s}"], stream=f"wa{s}")
        A.dma(pool, wb[s][:], wbv[:, :, fc], R=[], W=[f"wb{s}"], stream=f"wb{s}")
        for hh in range(2):
            A.dma(pool, wga[s][:, hh * 8:(hh + 1) * 8, :], wv[:, hh * 8:(hh + 1) * 8, O_GTA + fo * 128:O_GTA + (fo + 1) * 128], R=[], W=[(f"wga{s}", hh)],
                  stream=f"wga{s}{hh}")
            A.dma(pool, wgb[s][:, hh * 8:(hh + 1) * 8, :], wv[:, hh * 8:(hh + 1) * 8, O_GTB + fo * 128:O_GTB + (fo + 1) * 128], R=[], W=[(f"wgb{s}", hh)],
                  stream=f"wgb{s}{hh}")
        for (c0, n) in BLKS:
            p = it % 2
            it += 1
            b0 = 4 * p
            cols = slice(c0, c0 + n)
            A.mm([f"ps{b0}"], [f"wa{s}"], [MM(ps[b0][:, 0:n], wa[s][:, hc, :], B.oaT[:, hc, cols], hc == 0, hc == 7) for hc in range(8)])
            A.mm([f"ps{b0 + 1}"], [f"wb{s}"], [MM(ps[b0 + 1][:, 0:n], wb[s][:, hc, :], B.obT[:, hc, cols], hc == 0, hc == 7) for hc in range(8)])
            A.mm([f"ps{b0 + 2}"], [(f"wga{s}", 0), (f"wga{s}", 1)], [MM(ps[b0 + 2][:, 0:n], wga[s][:, dc, :], B.hT[:, dc, cols], dc == 0, dc == 15) for dc in range(16)])
            A.mm([f"ps{b0 + 3}"], [(f"wgb{s}", 0), (f"wgb{s}", 1)], [MM(ps[b0 + 3][:, 0:n], wgb[s][:, dc, :], B.hT[:, dc, cols], dc == 0, dc == 15) for dc in range(16)])
            A.op(act, "activation", R=[f"ps{b0 + 2}"], W=[f"sga{p}"], out=sga[p][:, 0:n], in_=ps[b0 + 2][:, 0:n], func=AF.Sigmoid)
            A.op(act, "activation", R=[f"ps{b0 + 3}"], W=[f"sgb{p}"], out=sgb[p][:, 0:n], in_=ps[b0 + 3][:, 0:n], func=AF.Sigmoid)
            A.op(dve, "tensor_tensor", R=[f"ps{b0}", f"sga{p}"], W=[f"t1_{p}"], out=t1[p][:, 0:n], in0=ps[b0][:, 0:n], in1=sga[p][:, 0:n], op=ALU.mult)
            A.op(dve, "tensor_tensor", R=[f"ps{b0 + 1}", f"sgb{p}"], W=[f"t2_{p}"], out=t2[p][:, 0:n], in0=ps[b0 + 1][:, 0:n], in1=sgb[p][:, 0:n], op=ALU.mult)
            A.op(pool, "tensor_tensor", R=[f"t1_{p}", f"t2_{p}"], W=[("mT", fo, c0)], out=mT[:, fo, cols], in0=t1[p][:, 0:n], in1=t2[p][:, 0:n], op=ALU.add)
    if B.debug and B.stop_after == 5:
        B.dbg("mT", mT[:], BF16)
    K.barrier()
    A.reset()
    wo = B.view(0, [128, 16, D], BF16)
    tmp = B.bump([(OFF_H2 + 36864, ARENA_BYTES)])
    G_tm = tmp("G_tm", [128, D], F32)
    x_t = tmp("x_t", [128, D], F32)
    x1_t = tmp("x1_t", [128, D], F32)
    xs_t = tmp("xs_t", [128, D], F32)
    GB = tmp("GB", [17, D], F32)
    sel = tmp("sel", [17, 256], F32)
    st = tmp("st5", [128, 16], F32)
    tS = tmp("tS5", [128, 4, 128], F32)
    junk = xs_t[:].bitcast(BF16)[:, 0:D]
    wov = B.w_o.rearrange("(c p) n -> p c n", p=128)
    for dc in range(16):
        for q4 in range(4):
            A.dma(pool, wo[:, dc, q4 * 512:(q4 + 1) * 512], wov[:, dc, q4 * 512:(q4 + 1) * 512], R=[], W=[("wo", dc, q4)], stream=f"wo{q4}")
    WO = [("wo", dc, q4) for dc in range(16) for q4 in range(4)]
    A.dma(sp, sel[:], B.sel_d, R=[], W=["sel"], stream="c0")
    build_GB(B, B.G1, GB, "g1")
    for t in range(9):
        cols = slice(t * 128, (t + 1) * 128)
        if t == 0 or t == 8:
            build_G_tm(B, B.G1, GB, G_tm, sel, 0 if t == 0 else 128, "g1")
        A.dma(sp, x_t[:], B.x_main[cols, :], R=[], W=["x_t"], stream="x_t")
        for cb in range(4):
            A.mm([f"ps{cb}"], WO, [MM(ps[cb][:, 0:512], mT[:, dc, cols], wo[:, dc, cb * 512:(cb + 1) * 512], dc == 0, dc == 15) for dc in range(16)])
        A.op(dve, "memset", W=["st5"], ap=st[:], constant=0.0)
        for cb in range(4):
            A.op(act, "activation", R=[f"ps{cb}"], W=["xs_t", "st5"], out=junk[:, 0:512], in_=ps[cb][:, 0:512], func=AF.Square, accum_out=st[:, cb:cb + 1])
        A.op(dve, "tensor_tensor", R=["st5"], W=["st5"], out=st[:, 4:5], in0=st[:, 0:1], in1=st[:, 1:2], op=ALU.add)
        A.op(dve, "tensor_tensor", R=["st5"], W=["st5"], out=st[:, 5:6], in0=st[:, 2:3], in1=st[:, 3:4], op=ALU.add)
        A.op(dve, "tensor_tensor", R=["st5"], W=["st5"], out=st[:, 6:7], in0=st[:, 4:5], in1=st[:, 5:6], op=ALU.add)
        A.op(act, "activation", R=["st5"], W=["st5"], out=st[:, 7:8], in_=st[:, 6:7], func=AF.Sqrt, scale=1.0 / D, bias=B.eps_t[:, 0:1])
        A.op(dve, "reciprocal", R=["st5"], W=["st5"], out=st[:, 8:9], in_=st[:, 7:8])
        for cb in range(4):
            cs_ = slice(cb * 512, (cb + 1) * 512)
            A.op(dve, "scalar_tensor_tensor", R=[f"ps{cb}", "st5", "G_tm"], W=[("x1_t", cb)], out=x1_t[:, cs_], in0=ps[cb][:, 0:512], scalar=st[:, 8:9],
                 in1=G_tm[:, cs_], op0=ALU.mult, op1=ALU.mult)
            A.op(pool, "tensor_tensor", R=[("x1_t", cb), "x_t"], W=[("x1_t", cb)], out=x1_t[:, cs_], in0=x1_t[:, cs_], in1=x_t[:, cs_], op=ALU.add)
        X1 = [("x1_t", cb) for cb in range(4)]
        A.dma(sp, B.y_main[cols, :], x1_t[:], R=X1, W=[], stream="x1o")
        A.op(act, "activation", R=X1, W=["xs_t", "st5"], out=junk[:], in_=x1_t[:], func=AF.Square, accum_out=st[:, 9:10])
        A.op(act, "activation", R=["st5"], W=["st5"], out=st[:, 10:11], in_=st[:, 9:10], func=AF.Sqrt, scale=1.0 / D, bias=B.eps_t[:, 0:1])
        A.op(dve, "reciprocal", R=["st5"], W=["st5"], out=st[:, 11:12], in_=st[:, 10:11])
        A.op(dve, "tensor_scalar", R=X1 + ["st5"], W=["xs_t"], out=xs_t[:], in0=x1_t[:], scalar1=st[:, 11:12], scalar2=None, op0=ALU.mult)
        for g in range(4):
            pb = 4 + g
            A.mm([f"ps{pb}"], ["xs_t"], [TR(ps[pb][:, j * 128:(j + 1) * 128], xs_t[:, (g * 4 + j) * 128:(g * 4 + j + 1) * 128], B.ident[:]) for j in range(4)])
            if t == 8:
                A.op(dve, "tensor_tensor", R=[f"ps{pb}"], W=["tS5"], out=tS[:].rearrange("p a (b t) -> p a b t", t=TS),
                     in0=ps[pb][:].rearrange("p (a b t) -> p a b t", a=4, t=TS),
                     in1=B.A2[:, g * 4:(g + 1) * 4, 1:17].unsqueeze(3).broadcast_to([128, 4, NB, TS]), op=ALU.mult)
                A.op(dve, "tensor_tensor", R=["tS5"], W=[("h2T", t, g)], out=h2T[:, g * 4:(g + 1) * 4, cols].rearrange("p a (b t) -> p a b t", t=TS),
                     in0=tS[:].rearrange("p a (b t) -> p a b t", t=TS),
                     in1=B.modT[:, 48 + g * 4:48 + (g + 1) * 4, 1:17].unsqueeze(3).broadcast_to([128, 4, NB, TS]), op=ALU.add)
            else:
                for j in range(4):
                    dc = g * 4 + j
                    if True:
                        A.op(dve, "tensor_scalar", R=[f"ps{pb}"], W=[("h2T", t, g, j)], out=h2T[:, dc, cols], in0=ps[pb][:, j * 128:(j + 1) * 128],
                             scalar1=B.A2[:, dc, 0:1], scalar2=B.modT[:, 48 + dc, 0:1], op0=ALU.mult, op1=ALU.add)
                    else:
                        A.op(act, "activation", R=[f"ps{pb}"], W=[("h2T", t, g, j)], out=h2T[:, dc, cols], in_=ps[pb][:, j * 128:(j + 1) * 128],
                             func=AF.Identity, scale=B.A2[:, dc, 0:1], bias=B.modT[:, 48 + dc, 0:1])
    if B.debug and B.stop_after == 5:
        B.dbg("h2T", h2T[:], BF16)


def phase6(B):
    nc, K, A = B.nc, B.K, B.A
    pe, act, dve, pool, sp = K.pe, K.act, K.dve, K.pool, K.sp
    ps = K.ps
    h2T = B.view(OFF_H2, [128, 16, NT], BF16)
    uT = B.view(0, [128, 64, 512], BF16)
    tmp = B.bump([(65536, OFF_H2), (OFF_H2 + 36864, ARENA_BYTES)])
    NWU, NWD = 3, 4
    wup = [tmp(f"wup{i}", [128, 16, 128], BF16) for i in range(NWU)]
    wd = [tmp(f"wd{i}", [128, 1024], BF16) for i in range(NWD)]
    GB = tmp("GB6", [17, D], F32)
    sel = tmp("sel6", [17, 256], F32)
    st = tmp("st6", [128, 16], F32)
    rl = [tmp(f"rl{i}", [128, 512], F32) for i in range(2)]
    junk = tmp("junk6", [128, D], BF16)
    m_tm = tmp("m_tm", [128, 4, D], F32)
    G_tm = tmp("G_tm6", [128, D], F32)
    x1_t = tmp("x1_t6", [128, D], F32)
    wuv = B.w_up.rearrange("(c p) n -> p c n", p=128)
    A.dma(sp, sel[:], B.sel_d, R=[], W=["sel"], stream="c0")
    build_GB(B, B.G2, GB, "g2")
    iu = 0
    idn = 0
    for bi, (c0, n) in enumerate(BLKS):
        tiles = n // 128
        cols = slice(c0, c0 + n)
        if bi == 0 or bi == 2:
            build_G_tm(B, B.G2, GB, G_tm, sel, 0 if bi == 0 else 128, "g2")
        for ffc in range(64):
            s = iu % NWU
            pb = iu % 4
            iu += 1
            for hh in range(2):
                A.dma(pool, wup[s][:, hh * 8:(hh + 1) * 8, :], wuv[:, hh * 8:(hh + 1) * 8, ffc * 128:(ffc + 1) * 128], R=[], W=[(f"wup{s}", hh)],
                      stream=f"wup{s}{hh}")
            A.mm([f"ps{pb}"], [(f"wup{s}", 0), (f"wup{s}", 1)], [MM(ps[pb][:, 0:n], wup[s][:, dc, :], h2T[:, dc, cols], dc == 0, dc == 15) for dc in range(16)])
            A.op(act, "activation", R=[f"ps{pb}"], W=[f"rl{pb % 2}"], out=rl[pb % 2][:, 0:n], in_=ps[pb][:, 0:n], func=AF.Relu)
            A.op(dve if ffc % 2 else pool, "tensor_tensor", R=[f"rl{pb % 2}"], W=[("uT", ffc)], out=uT[:, ffc, 0:n], in0=rl[pb % 2][:, 0:n],
                 in1=rl[pb % 2][:, 0:n], op=ALU.mult)
        for half in range(2):
            banks = [f"ps{ti * 2 + cbk}" for ti in range(tiles) for cbk in range(2)]
            for ffc in range(64):
                s = idn % NWD
                idn += 1
                A.dma(pool, wd[s][:], B.w_down[ffc * 128:(ffc + 1) * 128, half * 1024:(half + 1) * 1024], R=[], W=[f"wd{s}"], stream=f"wd{s}")
                A.mm(banks, [f"wd{s}", ("uT", ffc)], [MM(ps[ti * 2 + cbk][:, 0:512], uT[:, ffc, ti * 128:(ti + 1) * 128], wd[s][:, cbk * 512:(cbk + 1) * 512],
                                                        ffc == 0, ffc == 63) for ti in range(tiles) for cbk in range(2)])
            for ti in range(tiles):
                for cbk in range(2):
                    pb = ti * 2 + cbk
                    c_ = half * 1024 + cbk * 512
                    A.op(act if cbk else dve, "copy" if cbk else "tensor_copy", R=[f"ps{pb}"], W=[("m_tm", ti)], out=m_tm[:, ti, c_:c_ + 512], in_=ps[pb][:, 0:512])
        for ti in range(tiles):
            t = c0 // 128 + ti
            rows = slice(t * 128, (t + 1) * 128)
            A.dma(sp, x1_t[:], B.y_main[rows, :], R=[], W=["x1_t"], stream="x1i")
            A.op(dve, "memset", W=["st6"], ap=st[:], constant=0.0)
            A.op(act, "activation", R=[("m_tm", ti)], W=["junk6", "st6"], out=junk[:], in_=m_tm[:, ti, :], func=AF.Square, accum_out=st[:, 0:1])
            A.op(act, "activation", R=["st6"], W=["st6"], out=st[:, 1:2], in_=st[:, 0:1], func=AF.Sqrt, scale=1.0 / D, bias=B.eps_t[:, 0:1])
            A.op(dve, "reciprocal", R=["st6"], W=["st6"], out=st[:, 2:3], in_=st[:, 1:2])
            A.op(dve, "scalar_tensor_tensor", R=[("m_tm", ti), "st6", "G_tm"], W=[("m_tm", ti)], out=m_tm[:, ti, :], in0=m_tm[:, ti, :], scalar=st[:, 2:3],
                 in1=G_tm[:], op0=ALU.mult, op1=ALU.mult)
            A.op(pool, "tensor_tensor", R=[("m_tm", ti), "x1_t"], W=["x1_t"], out=x1_t[:], in0=m_tm[:, ti, :], in1=x1_t[:], op=ALU.add)
            A.dma(sp, B.y_main[rows, :], x1_t[:], R=["x1_t"], W=[], stream="yo")


def _host_consts(core):
    half = core % 2
    inv = (10000.0 ** (-np.arange(32, dtype=np.float32) / 32.0)).astype(np.float32)
    pos_main = np.concatenate([half * NTP + np.arange(NTP), PAST + (np.arange(NTS) % TS)]).astype(np.float32)
    pos_ctx = np.arange(NCX).astype(np.float32)
    pos = np.concatenate([pos_main, pos_ctx])
    ang = pos[:, None] * inv[None, :]
    cos = np.cos(ang).astype(np.float32)
    sin = np.sin(ang).astype(np.float32)
    cs = np.concatenate([cos, cos], axis=1)
    sn = np.concatenate([-sin, sin], axis=1)
    masks = np.zeros((128, 1024), np.float32)
    r = np.arange(128)
    masks[:, 0:128] = ((r[:, None] // 64 == r[None, :] // 64) & (r[:, None] <= r[None, :])).astype(np.float32)
    masks[:, 128:256] = ((r[:, None] // 8 == r[None, :] // 8) & (r[:, None] <= r[None, :])).astype(np.float32)
    masks[:, 256:384] = (r[:, None] <= r[None, :]).astype(np.float32)
    masks[:, 760:776] = (r[:, None] // 8 == np.arange(16)[None, :]).astype(np.float32)
    masks[:, 800] = r.astype(np.float32)
    sm = np.zeros((128, NB, NH, TS), np.float32)
    for s in range(128):
        sm[s, s // 8, :, (s % 8):] = 1.0
    sel = np.zeros((17, 256), np.float32)
    sel[0, 0:128] = 1.0
    for tok in range(128):
        sel[1 + tok // 8, 128 + tok] = 1.0
    flag = np.full((128, 1), float(half), np.float32)
    cbias = np.full((1, NCX), 0.0 if half == 1 else NEG, np.float32)
    return dict(rope_cs=cs, rope_sn=sn, ropeT_cs=np.ascontiguousarray(cs[:NT].T), ropeT_sn=np.ascontiguousarray(sn[:NT].T),
                masks=masks, smask=sm.reshape(128, NB * 64), sel=sel, ctx_flag=flag, ctx_bias=cbias,
                ident=np.eye(128, dtype=np.float32))


def make_in_maps(inputs, cores, cache_all, ptab_all):
    f = lambda a: np.ascontiguousarray(np.asarray(a, dtype=np.float32))
    x_prompt = f(inputs["x_prompt"]); x_sample = f(inputs["x_sample"])
    c_prompt = f(inputs["c_prompt"]); c_sample = f(inputs["c_sample"])
    vecs = np.concatenate([
        f(inputs["g_pre_mix"]).reshape(16, 128), f(inputs["g_post_mix"]).reshape(16, 128),
        f(inputs["g_pre_mlp"]).reshape(16, 128), f(inputs["g_post_mlp"]).reshape(16, 128),
        f(inputs["g_q_norm"]).reshape(4, 128), f(inputs["g_kv_norm"]).reshape(4, 128),
        f(inputs["lb_logits"]).reshape(16, 128), f(inputs["g_hgrn_norm"]).reshape(8, 128)], axis=0)
    shared = dict(
        w_ada=f(inputs["w_ada"])[0], b_ada=f(inputs["b_ada"]).reshape(1, 6 * D), vecs=vecs,
        g_q=f(inputs["g_q_norm"]).reshape(1, QR),
        g_kv=f(inputs["g_kv_norm"]).reshape(1, KVR), w_in=f(inputs["w_in"])[0], w_a_out=f(inputs["w_a_out"])[0],
        w_q_up=f(inputs["w_q_up"])[0], w_kv_up=f(inputs["w_kv_up"])[0], w_b_out=f(inputs["w_b_out"])[0],
        w_o=f(inputs["w_o"])[0], w_up=f(inputs["w_up"])[0], w_down=f(inputs["w_down"])[0],
        cache_all=cache_all)
    state = f(inputs["state_hgrn"])[0]
    maps = []
    for c in cores:
        b, half = c // 2, c % 2
        m = dict(shared)
        m["x_main"] = np.concatenate([x_prompt[b, half * NTP:(half + 1) * NTP], x_sample[c * NB:(c + 1) * NB].reshape(NTS, D)], axis=0)
        m["x_ctx"] = x_prompt[b, 0:NCX]
        m["c_all"] = np.concatenate([c_prompt[b:b + 1], c_sample[c * NB:(c + 1) * NB]], axis=0)
        m["state_in"] = state[c * NB:(c + 1) * NB]
        m["ptab"] = np.ascontiguousarray(ptab_all[c * NB:(c + 1) * NB]).astype(np.int32).reshape(1, NB * NPAGES)
        m.update(_host_consts(c))
        maps.append(m)
    return maps


def assemble(results, cores):
    y_prompt = np.zeros((4, 2048, D), np.float32); y_sample = np.zeros((128, TS, D), np.float32)
    ckv_p = np.zeros((1, 4, 2048, KVR), np.float32); kpe_p = np.zeros((1, 4, 2048, ROPE), np.float32)
    s_p = np.zeros((1, 4, NH, DK, DV), np.float32)
    ckv_s = np.zeros((1, 128, TS, KVR), np.float32); kpe_s = np.zeros((1, 128, TS, ROPE), np.float32)
    s_s = np.zeros((1, 128, NH, DK, DV), np.float32)
    for r, c in zip(results, cores):
        b, half = c // 2, c % 2
        sl = slice(half * NTP, (half + 1) * NTP)
        y_prompt[b, sl] = r["y_main"][:NTP]; y_sample[c * NB:(c + 1) * NB] = r["y_main"][NTP:].reshape(NB, TS, D)
        ckv_p[0, b, sl] = r["ckv_main"][:NTP]; ckv_s[0, c * NB:(c + 1) * NB] = r["ckv_main"][NTP:].reshape(NB, TS, KVR)
        kpe_p[0, b, sl] = r["kpe_main"][:NTP]; kpe_s[0, c * NB:(c + 1) * NB] = r["kpe_main"][NTP:].reshape(NB, TS, ROPE)
        if half == 1:
            s_p[0, b] = r["s_prompt"]
        s_s[0, c * NB:(c + 1) * NB] = r["s_sample"]
    return (y_prompt, y_sample, ckv_p, kpe_p, s_p, ckv_s, kpe_s, s_s)


def kernel(**inputs):
    cores = list(range(8))
    cache_ckv = np.ascontiguousarray(np.asarray(inputs["cache_ckv"], dtype=np.float32)).reshape(-1, KVR)
    cache_kpe = np.ascontiguousarray(np.asarray(inputs["cache_kpe"], dtype=np.float32)).reshape(-1, ROPE)
    ptab = np.asarray(inputs["page_table"]).astype(np.int32)
    cache_all = np.concatenate([cache_ckv, cache_kpe], axis=1)
    del cache_ckv, cache_kpe
    nc = build_nc(cache_all.shape[0])
    maps = make_in_maps(inputs, cores, cache_all, ptab)
    res = run_bass_kernel_spmd(nc, maps, core_ids=cores)
    return assemble(res.results, cores)
```

```python
import math
import numpy as np
import ml_dtypes
import concourse.bass as bass
import concourse.mybir as mybir
from concourse.bass_utils import run_bass_kernel_spmd

F32 = mybir.dt.float32
BF16 = mybir.dt.bfloat16
I32 = mybir.dt.int32
AF = mybir.ActivationFunctionType
ALU = mybir.AluOpType
AX = mybir.AxisListType

D = 2048
NH = 8
DK = 128
DV = 128
QR = 512
KVR = 512
NOPE = 128
ROPE = 64
VD = 128
DFF = 8192
EPS = 1e-6
PAST = 8192
PAGE = 128
NPAGES = 64
SCALE = (NOPE + ROPE) ** -0.5
NTP = 1024
NTS = 128
NT = NTP + NTS
NCX = 1024
NB = 16
TS = 8
NEG = -30000.0
NKEY = NCX + NTP + NTS

O_QA, O_FA, O_IA, O_GA, O_QD, O_KVD, O_KPE, O_GTA, O_GTB = 0, 1024, 2048, 3072, 4096, 4608, 5120, 5184, 7232
IN_COLS = 9280
BLKS = [(0, 512), (512, 512), (1024, 128)]
CBLKS = [(0, 512), (512, 512)]


def _flat(deps):
    out = []
    if deps is None:
        return out
    if isinstance(deps, tuple) and len(deps) == 3 and isinstance(deps[0], str):
        return [deps]
    for d in deps:
        out.extend(_flat(d))
    return out


class Eng:
    def __init__(self, nc, name, eng):
        self.nc, self.name, self.eng = nc, name, eng
        self.gen = 0
        self.sem = nc.alloc_semaphore("sem_" + name)
        self.cnt = 0
        self.seen = {}
        self.last = None

    def wait(self, deps):
        best = {}
        for (key, sem, val) in _flat(deps):
            if self.seen.get(key, 0) >= val:
                continue
            if key not in best or best[key][1] < val:
                best[key] = (sem, val)
        for key, (sem, val) in best.items():
            self.eng.wait_ge(sem, val)
            self.seen[key] = val

    def __call__(self, opname, deps=None, sig=True, **kw):
        self.wait(deps)
        inst = getattr(self.eng, opname)(**kw)
        if sig:
            if self.cnt >= 30000:
                self.gen += 1
                self.sem = self.nc.alloc_semaphore(f"sem_{self.name}_{self.gen}")
                self.cnt = 0
            self.cnt += 1
            inst.then_inc(self.sem, 1)
            self.last = (f"{self.name}.{self.gen}", self.sem, self.cnt)
            return self.last
        return None

    def tok(self):
        return self.last


class Ctx:
    def __init__(self, nc):
        self.nc = nc
        self.pe = Eng(nc, "pe", nc.tensor)
        self.act = Eng(nc, "act", nc.scalar)
        self.dve = Eng(nc, "dve", nc.vector)
        self.pool = Eng(nc, "pool", nc.gpsimd)
        self.sp = Eng(nc, "sp", nc.sync)
        self.engs = [self.pe, self.act, self.dve, self.pool, self.sp]
        self.streams = {}
        self.dma_toks = []
        self.ps = [nc.alloc_psum_tensor(f"psb{i}", [128, 512], F32) for i in range(8)]
        self.ps_free = [None] * 8
        self.ps_i = 0
        self._n = 0

    def name(self, s):
        self._n += 1
        return f"{s}_{self._n}"

    def dma(self, q, out, in_, stream, deps=None, indirect=None):
        if stream not in self.streams:
            self.streams[stream] = [self.nc.alloc_semaphore("dsem_" + stream), 0]
        st = self.streams[stream]
        q.wait(deps)
        if indirect is not None:
            inst = q.eng.indirect_dma_start(out=out, out_offset=None, in_=in_,
                                            in_offset=bass.IndirectOffsetOnAxis(ap=indirect, axis=0))
        else:
            inst = q.eng.dma_start(out=out, in_=in_)
        st[1] += 16
        inst.then_inc(st[0], 16)
        tok = ("d_" + stream, st[0], st[1])
        self.dma_toks.append(tok)
        return tok

    def psum(self):
        i = self.ps_i
        self.ps_i = (i + 1) % 8
        return i, self.ps[i], self.ps_free[i]

    def psum_release(self, i, tok):
        self.ps_free[i] = tok

    def barrier(self):
        toks = [e.tok() for e in self.engs if e.tok() is not None]
        latest = {}
        for t in self.dma_toks:
            latest[t[0]] = t
        toks += list(latest.values())
        self.dma_toks = list(latest.values())
        for e in self.engs:
            e.wait(toks)
        self.ps_free = [None] * 8


class Auto:
    def __init__(self, K):
        self.K = K
        self.w = {}
        self.r = {}

    def reset(self):
        self.w = {}
        self.r = {}

    def deps(self, R, W):
        d = []
        for k in R:
            if k in self.w:
                d.append(self.w[k])
        for k in W:
            if k in self.w:
                d.append(self.w[k])
            d.extend(self.r.get(k, {}).values())
        return d

    def note(self, tok, R, W):
        for k in R:
            self.r.setdefault(k, {})[tok[0]] = tok
        for k in W:
            self.w[k] = tok
            self.r[k] = {}

    def op(self, eng, opname, R=(), W=(), **kw):
        tok = eng(opname, deps=self.deps(R, W), **kw)
        self.note(tok, R, W)
        return tok

    def mm(self, W, R, mms):
        pe = self.K.pe
        pe.wait(self.deps(R, W))
        tok = None
        for i, (opname, kw) in enumerate(mms):
            tok = pe(opname, sig=(i == len(mms) - 1), **kw)
        self.note(tok, R, W)
        return tok

    def dma(self, q, out, in_, R, W, stream, indirect=None):
        tok = self.K.dma(q, out, in_, stream, deps=self.deps(R, W), indirect=indirect)
        self.note(tok, R, W)
        return tok

    def dma_group(self, q, items, R, W, stream):
        d = self.deps(R, W)
        tok = None
        for (out, in_) in items:
            tok = self.K.dma(q, out, in_, stream, deps=d)
            d = None
        self.note(tok, R, W)
        return tok


def sb(nc, name, shape, dt):
    return nc.alloc_sbuf_tensor(name, list(shape), dt)


class Bld:
    pass


def MM(out, lhsT, rhs, start, stop):
    return ("matmul", dict(out=out, lhsT=lhsT, rhs=rhs, start=start, stop=stop))


def TR(out, in_, identity):
    return ("transpose", dict(out=out, in_=in_, identity=identity))


ARENA_BYTES = 196608 - 4096
OFF_H = 0
OFF_C = 36864
OFF_LAT = 69632
OFF_OA = 118016
OFF_OB = 136448
OFF_T = 154880


def build_nc(n_pool_rows, stop_after=99, debug=False):
    nc = bass.Bass("TRN2", target_bir_lowering=False)
    K = Ctx(nc)
    A = Auto(K)

    def dbg(name, ap, dt=F32):
        if not debug:
            return
        K.barrier()
        d = nc.dram_tensor("dbg_" + name, list(ap.shape), dt, kind="ExternalOutput").ap()
        K.dma(K.sp, d, ap, "dbg")
        K.barrier()
    pe, act, dve, pool, sp = K.pe, K.act, K.dve, K.pool, K.sp
    B = Bld()
    B.nc, B.K, B.A = nc, K, A
    B.dbg = dbg
    B.debug = debug

    def din(name, shape, dt=F32):
        return nc.dram_tensor(name, list(shape), dt, kind="ExternalInput").ap()

    def dout(name, shape, dt=F32):
        return nc.dram_tensor(name, list(shape), dt, kind="ExternalOutput").ap()

    x_main = din("x_main", [NT, D])
    x_ctx = din("x_ctx", [NCX, D])
    c_all = din("c_all", [17, D])
    cache_all = din("cache_all", [n_pool_rows, KVR + ROPE])
    state_in = din("state_in", [NB, NH, DK, DV])
    ptab = din("ptab", [1, NB * NPAGES], I32)
    w_ada = din("w_ada", [D, 6 * D])
    b_ada = din("b_ada", [1, 6 * D])
    vecs = din("vecs", [96, 128])
    g_q = din("g_q", [1, QR])
    g_kv = din("g_kv", [1, KVR])
    w_in = din("w_in", [D, IN_COLS])
    w_a_out = din("w_a_out", [NH * DV, D])
    w_q_up = din("w_q_up", [QR, NH * (NOPE + ROPE)])
    w_kv_up = din("w_kv_up", [KVR, NH * (NOPE + VD)])
    w_b_out = din("w_b_out", [NH * VD, D])
    w_o = din("w_o", [D, D])
    w_up = din("w_up", [D, DFF])
    w_down = din("w_down", [DFF, D])
    ident_d = din("ident", [128, 128])
    rope_cs = din("rope_cs", [17 * 128, 64])
    rope_sn = din("rope_sn", [17 * 128, 64])
    ropeT_cs = din("ropeT_cs", [64, NT])
    ropeT_sn = din("ropeT_sn", [64, NT])
    ctx_bias = din("ctx_bias", [1, NCX])
    ctx_flag = din("ctx_flag", [128, 1])
    masks_d = din("masks", [128, 1024])
    smask_d = din("smask", [128, NB * 64])
    sel_d = din("sel", [17, 256])

    y_main = dout("y_main", [NT, D])
    ckv_main = dout("ckv_main", [NT, KVR])
    kpe_main = dout("kpe_main", [NT, ROPE])
    s_prompt = dout("s_prompt", [NH, DK, DV])
    s_sample = dout("s_sample", [NB, NH, DK, DV])

    ident = sb(nc, "ident_sb", [128, 128], F32)
    ident_bf = sb(nc, "ident_bf", [128, 128], BF16)
    vecT = sb(nc, "vecT", [128, 96], F32)
    modT = sb(nc, "modT", [128, 96, 17], F32)
    A1 = sb(nc, "A1", [128, 16, 17], F32)
    A2 = sb(nc, "A2", [128, 16, 17], F32)
    G1 = sb(nc, "G1", [128, 16, 17], F32)
    G2 = sb(nc, "G2", [128, 16, 17], F32)
    lbT = sb(nc, "lbT", [128, 8], F32)
    omlT = sb(nc, "omlT", [128, 8], F32)
    nomlT = sb(nc, "nomlT", [128, 8], F32)
    masks = sb(nc, "masks_sb", [128, 1024], F32)
    cflag = sb(nc, "cflag", [128, 1], F32)
    ones_bf = sb(nc, "ones_bf", [128, 128], BF16)
    ones_f = sb(nc, "ones_f", [128, 128], F32)
    eps_t = sb(nc, "eps_t", [128, 1], F32)
    arena = sb(nc, "arena", [128, ARENA_BYTES // 4], F32)
    ar_f = arena[:]
    ar_b = arena[:].bitcast(BF16)
    ar_i = arena[:].bitcast(I32)

    def view(off, shape, dt):
        n = int(np.prod(shape[1:]))
        assert off % 4 == 0
        if dt == BF16:
            assert off + 2 * n <= ARENA_BYTES, (off, shape)
            v = ar_b[:, off // 2: off // 2 + n]
        else:
            assert off + 4 * n <= ARENA_BYTES, (off, shape)
            v = (ar_f if dt == F32 else ar_i)[:, off // 4: off // 4 + n]
        if shape[0] != 128:
            v = v[0:shape[0]]
        if len(shape) == 3:
            v = v.rearrange("p (a b) -> p a b", a=shape[1])
        elif len(shape) == 4:
            v = v.rearrange("p (a b c) -> p a b c", a=shape[1], b=shape[2])
        elif len(shape) == 5:
            v = v.rearrange("p (a b c d) -> p a b c d", a=shape[1], b=shape[2], c=shape[3])
        return v

    def bump(regions):
        state = {"r": 0, "o": regions[0][0]}

        def tmp(name, shape, dt):
            n = int(np.prod(shape[1:])) * (2 if dt == BF16 else 4)
            n = (n + 63) // 64 * 64
            while True:
                s_, e_ = regions[state["r"]]
                if state["o"] + n <= e_:
                    off = state["o"]
                    state["o"] += n
                    return view(off, shape, dt)
                state["r"] += 1
                assert state["r"] < len(regions), ("arena region overflow", name)
                state["o"] = regions[state["r"]][0]
        return tmp

    REG_H, REG_C, REG_LAT = (OFF_H, OFF_C), (OFF_C, OFF_LAT), (OFF_LAT, OFF_OA)
    REG_BIG = (OFF_LAT, OFF_T)
    REG_L = (OFF_T, ARENA_BYTES)
    hT = view(OFF_H, [128, 16, NT], BF16)
    hcT = view(OFF_C, [128, 16, NCX], BF16)
    o = OFF_LAT
    ckvT = view(o, [128, 4, NKEY], BF16); o += 2 * 4 * NKEY
    kpeT = view(o, [128, NKEY], BF16); o += 2 * NKEY
    ckv_tm = view(o, [128, 17, KVR], BF16); o += 2 * 17 * KVR
    qnT = view(o, [128, 4, NT], BF16); o += 2 * 4 * NT
    assert o == OFF_OA, o
    oaT = view(OFF_OA, [128, 8, NT], BF16)
    obT = view(OFF_OB, [128, 8, NT], BF16)
    B.__dict__.update(locals())

    phases = [phase0, phase1, phase2, phase3, phase4, phase5, phase6]
    for i, ph in enumerate(phases):
        if stop_after >= i:
            K.barrier()
            A.reset()
            ph(B)
    K.barrier()
    return nc


def evac_copy(K, which, out, in_, deps):
    if which == 0:
        return K.dve("tensor_copy", deps=deps, out=out, in_=in_)
    return K.act("copy", deps=deps, out=out, in_=in_)


def phase0(B):
    nc, K = B.nc, B.K
    pe, act, dve, pool, sp = K.pe, K.act, K.dve, K.pool, K.sp
    if True:
        tb = B.bump([B.REG_BIG]); tc = B.bump([B.REG_C]); th = B.bump([B.REG_H]); tl = B.bump([B.REG_L])
        modB = tb("modB", [17, 6 * D], F32)
        c_sb = tb("c_sb", [17, D], F32)
        c_si = tb("c_si", [17, D], F32)
        wa = [tc(f"wa{i}", [128, 16, 512], BF16) for i in range(2)]
        bada_bf = th("bada_bf", [1, 6 * D], BF16)
        scT = tl("scT", [128, 16, 17], BF16)
        vec_sb = tl("vec_sb", [96, 128], F32)
        l8 = tl("l8", [128, 8], F32)

        t_c = [K.dma(sp, B.ident[:], B.ident_d, "c0")]
        t_c.append(K.dma(sp, c_sb[:], B.c_all, "c0"))
        t_c.append(K.dma(sp, vec_sb[:], B.vecs, "c0"))
        t_c.append(K.dma(sp, B.masks[:], B.masks_d, "c0"))
        t_c.append(K.dma(sp, B.cflag[:], B.ctx_flag, "c0"))
        t_c = t_c[-1]
        t_b = K.dma(pool, bada_bf[:], B.b_ada, "c1")
        t_o1 = dve("memset", ap=B.ones_bf[:], constant=1.0)
        t_o2 = dve("memset", ap=B.ones_f[:], constant=1.0)
        dve("memset", ap=B.eps_t[:], constant=EPS)

        t_si = act("activation", deps=[t_c], out=c_si[:], in_=c_sb[:], func=AF.Silu)
        i, ps, fr = K.psum()
        for dc in range(16):
            t_tr = pe("transpose", deps=[t_si, t_c, fr], sig=(dc == 15), out=ps[:, dc * 17:(dc + 1) * 17],
                      in_=c_si[0:17, dc * 128:(dc + 1) * 128], identity=B.ident[0:17, 0:17])
        t_scT = dve("tensor_copy", deps=[t_tr], out=scT[:], in_=ps[:, 0:272].rearrange("p (a b) -> p a b", b=17))
        K.psum_release(i, t_scT)

        i, ps, fr = K.psum()
        t_tr = pe("transpose", deps=[t_c, fr], out=ps[:, 0:96], in_=vec_sb[0:96, :], identity=B.ident[0:96, 0:96])
        t_vec = dve("tensor_copy", deps=[t_tr], out=B.vecT[:], in_=ps[:, 0:96])
        K.psum_release(i, t_vec)

        wv = B.w_ada.rearrange("(c p) n -> p c n", p=128)
        mm_tok = [None, None]
        ev_toks = []
        for cb in range(24):
            buf = wa[cb % 2]
            t_w = K.dma(pool, buf[:], wv[:, :, cb * 512:(cb + 1) * 512], f"wa{cb % 2}", deps=[mm_tok[cb % 2]])
            i, ps, fr = K.psum()
            for dc in range(16):
                pe("matmul", deps=[t_w, fr, t_scT], sig=False, out=ps[0:17, :], lhsT=scT[:, dc, :], rhs=buf[:, dc, :],
                   start=(dc == 0), stop=False)
            t_mm = pe("matmul", deps=[t_b, t_o1], out=ps[0:17, :], lhsT=B.ones_bf[0:1, 0:17],
                      rhs=bada_bf[0:1, cb * 512:(cb + 1) * 512], start=False, stop=True)
            mm_tok[cb % 2] = t_mm
            t_ev = evac_copy(K, cb % 2, modB[:, cb * 512:(cb + 1) * 512], ps[0:17, :], [t_mm])
            K.psum_release(i, t_ev)
            ev_toks.append(t_ev)
        t_mt = []
        for g in range(4):
            i, ps, fr = K.psum()
            for k in range(24):
                j = g * 24 + k
                t_tr = pe("transpose", deps=[ev_toks, fr], sig=(k == 23), out=ps[:, k * 17:(k + 1) * 17],
                          in_=modB[0:17, j * 128:(j + 1) * 128], identity=B.ident[0:17, 0:17])
            t_e = dve("tensor_copy", deps=[t_tr], out=B.modT[:, g * 24:(g + 1) * 24, :],
                      in_=ps[:, 0:408].rearrange("p (a b) -> p a b", b=17))
            K.psum_release(i, t_e)
            t_mt.append(t_e)

        def bc(lo):
            return B.vecT[:, lo:lo + 16].unsqueeze(2).broadcast_to([128, 16, 17])
        dve("scalar_tensor_tensor", deps=[t_mt, t_vec], out=B.A1[:], in0=B.modT[:, 16:32, :], scalar=1.0, in1=bc(0),
            op0=ALU.add, op1=ALU.mult)
        dve("scalar_tensor_tensor", out=B.A2[:], in0=B.modT[:, 64:80, :], scalar=1.0, in1=bc(32),
            op0=ALU.add, op1=ALU.mult)
        dve("tensor_tensor", out=B.G1[:], in0=B.modT[:, 32:48, :], in1=bc(16), op=ALU.mult)
        dve("tensor_tensor", out=B.G2[:], in0=B.modT[:, 80:96, :], in1=bc(48), op=ALU.mult)
        t_l = dve("tensor_tensor", out=l8[:], in0=B.vecT[:, 72:80], in1=B.vecT[:, 80:88], op=ALU.subtract)
        t_lb = act("activation", deps=[t_l], out=B.lbT[:], in_=l8[:], func=AF.Sigmoid)
        dve("tensor_scalar", deps=[t_lb], out=B.omlT[:], in0=B.lbT[:], scalar1=-1.0, scalar2=1.0, op0=ALU.mult, op1=ALU.add)
        dve("tensor_scalar", deps=[t_lb], out=B.nomlT[:], in0=B.lbT[:], scalar1=1.0, scalar2=-1.0, op0=ALU.mult, op1=ALU.add)
        dve("tensor_copy", deps=[t_c], out=B.ident_bf[:], in_=B.ident[:])
        K.barrier()
        B.dbg("modT", B.modT[:])
        B.dbg("A1", B.A1[:])
        B.dbg("vecT", B.vecT[:])
        B.dbg("lbT", B.lbT[:])


def expand_mod(B, dst, src, deps=None):
    return B.K.dve("tensor_copy", deps=deps, out=dst[:].rearrange("p a (b t) -> p a b t", t=TS),
                   in_=src[:, :, 1:17].unsqueeze(3).broadcast_to([128, 16, NB, TS]))


def phase1(B):
    nc, K = B.nc, B.K
    pe, act, dve, pool, sp = K.pe, K.act, K.dve, K.pool, K.sp
    if True:
        tmp = B.bump([B.REG_BIG, B.REG_L])
        xt = [tmp(f"xt{i}", [128, D], F32) for i in range(2)]
        xs = [tmp(f"xs{i}", [128, D], F32) for i in range(2)]
        junk = tmp("junk", [128, D], BF16)
        ss = tmp("ss", [128, 17], F32)
        rt = tmp("rt", [128, 17], F32)
        rstd = tmp("rstd", [128, 17], F32)
        A1s = tmp("A1s", [128, 16, 128], F32)
        B1s = tmp("B1s", [128, 16, 128], F32)
        tmpS = [tmp(f"tmpS{i}", [128, 4, 128], F32) for i in range(2)]
        t_A1s = expand_mod(B, A1s, B.A1)
        t_B1s = expand_mod(B, B1s, B.modT[:, 0:16, :])
        xt_free = [None, None]
        xs_free = [None, None]
        tiles = [("m", t) for t in range(9)] + [("c", u) for u in range(8)]
        for k, (kind, t) in enumerate(tiles):
            p = k % 2
            src = B.x_main[t * 128:(t + 1) * 128, :] if kind == "m" else B.x_ctx[t * 128:(t + 1) * 128, :]
            t_x = K.dma(sp, xt[p][:], src, f"x{p}", deps=[xt_free[p]])
            CUT = 9
            if CUT == 0:
                xt_free[p] = t_x
                continue
            t_sq = act("activation", deps=[t_x], out=junk[:], in_=xt[p][:], func=AF.Square, accum_out=ss[:, k:k + 1])
            t_sr = act("activation", deps=[t_sq], out=rt[:, k:k + 1], in_=ss[:, k:k + 1], func=AF.Sqrt, scale=1.0 / D, bias=B.eps_t[:, 0:1])
            t_rs = dve("reciprocal", deps=[t_sr], out=rstd[:, k:k + 1], in_=rt[:, k:k + 1])
            t_xs = dve("tensor_scalar", deps=[t_rs, t_x, xs_free[p]], out=xs[p][:], in0=xt[p][:], scalar1=rstd[:, k:k + 1],
                       scalar2=None, op0=ALU.mult)
            xt_free[p] = [t_xs, t_sq]
            if CUT == 1:
                xs_free[p] = t_xs
                continue
            last_tr = None
            for g in range(4):
                i, ps, fr = K.psum()
                for j in range(4):
                    dc = g * 4 + j
                    t_tr = pe("transpose", deps=[t_xs, fr], sig=(j == 3), out=ps[:, j * 128:(j + 1) * 128],
                              in_=xs[p][:, dc * 128:(dc + 1) * 128], identity=B.ident[:])
                last_tr = t_tr
                evs = []
                if CUT == 2:
                    K.psum_release(i, t_tr)
                    continue
                if kind == "m" and t == 8:
                    ts_ = tmpS[g % 2]
                    t_1 = dve("tensor_tensor", deps=[t_tr, t_A1s], out=ts_[:], in0=ps[:].rearrange("p (a b) -> p a b", b=128),
                              in1=A1s[:, g * 4:(g + 1) * 4, :], op=ALU.mult)
                    t_2 = dve("tensor_tensor", deps=[t_1, t_B1s], out=B.hT[:, g * 4:(g + 1) * 4, t * 128:(t + 1) * 128], in0=ts_[:],
                              in1=B1s[:, g * 4:(g + 1) * 4, :], op=ALU.add)
                    evs = [t_1, t_2]
                else:
                    for j in range(4):
                        dc = g * 4 + j
                        dst = (B.hT if kind == "m" else B.hcT)[:, dc, t * 128:(t + 1) * 128]
                        if True:
                            evs.append(dve("tensor_scalar", deps=[t_tr], out=dst, in0=ps[:, j * 128:(j + 1) * 128],
                                           scalar1=B.A1[:, dc, 0:1], scalar2=B.modT[:, dc, 0:1], op0=ALU.mult, op1=ALU.add))
                        else:
                            evs.append(act("activation", deps=[t_tr], out=dst, in_=ps[:, j * 128:(j + 1) * 128], func=AF.Identity,
                                           scale=B.A1[:, dc, 0:1], bias=B.modT[:, dc, 0:1]))
                K.psum_release(i, evs)
            xs_free[p] = last_tr
        K.barrier()
        B.dbg("hT", B.hT[:], BF16)
        B.dbg("hcT", B.hcT[:], BF16)


def rms_rstd_a(B, ps_in, pskey, junk, st, c, n):
    A, K = B.A, B.K
    A.op(K.act, "activation", R=[pskey], W=["junk2", ("st", c)], out=junk, in_=ps_in, func=AF.Square,
         accum_out=st[:, c:c + 1])
    A.op(K.act, "activation", R=[("st", c)], W=[("st", c + 1)], out=st[:, c + 1:c + 2], in_=st[:, c:c + 1],
         func=AF.Sqrt, scale=1.0 / n, bias=B.eps_t[:, 0:1])
    A.op(K.dve, "reciprocal", R=[("st", c + 1)], W=[("st", c + 2)], out=st[:, c + 2:c + 3], in_=st[:, c + 1:c + 2])
    return st[:, c + 2:c + 3], ("st", c + 2)


def phase2(B):
    nc, K, A = B.nc, B.K, B.A
    pe, act, dve, pool, sp = K.pe, K.act, K.dve, K.pool, K.sp
    ps = K.ps
    tmp = B.bump([(OFF_OA, ARENA_BYTES)])
    wm = tmp("wm", [128, 16, 1088], BF16)
    gq_b = tmp("gq_b", [128, QR], F32)
    gkv_b = tmp("gkv_b", [128, KVR], F32)
    cs_t = tmp("cs_t", [128, 17, 64], F32)
    sn_t = tmp("sn_t", [128, 17, 64], F32)
    junk = tmp("junk2", [128, 512], BF16)
    st = tmp("st2", [128, 6 * 17], F32)
    ckv32 = [tmp(f"ckv32_{i}", [128, KVR], F32) for i in range(2)]
    qn32 = [tmp(f"qn32_{i}", [128, QR], F32) for i in range(2)]
    r1 = tmp("r1", [128, 64], F32)
    r2 = tmp("r2", [128, 64], F32)
    kpe32 = [tmp(f"kpe32_{i}", [128, 64], F32) for i in range(2)]
    wv = B.w_in.rearrange("(c p) n -> p c n", p=128)
    A.dma_group(sp, [(gq_b[:], B.g_q.partition_broadcast(128)), (gkv_b[:], B.g_kv.partition_broadcast(128)),
                     (cs_t[:], B.rope_cs.rearrange("(t p) r -> p t r", p=128)),
                     (sn_t[:], B.rope_sn.rearrange("(t p) r -> p t r", p=128))],
                R=[], W=["gq_b", "gkv_b", "cs_t", "sn_t"], stream="c0")
    for j, (c0, n) in enumerate([(0, 512), (512, 512), (1024, 64)]):
        for hh in range(2):
            A.dma(pool, wm[:, hh * 8:(hh + 1) * 8, c0:c0 + n], wv[:, hh * 8:(hh + 1) * 8, O_QD + c0:O_QD + c0 + n],
                  R=[], W=[("wm", j, hh)], stream=f"wm{j}{hh}")
    WM = [("wm", j, hh) for j in range(3) for hh in range(2)]
    A.dma(pool, B.kpeT[64:65, 0:NCX], B.ctx_bias, R=[], W=["kpeT_b"], stream="wmb")
    A.op(dve, "memset", W=["kpeT_b2"], ap=B.kpeT[64:65, NCX:], constant=0.0)
    A.op(dve, "memset", W=[("st", i) for i in range(6 * 17)], ap=st[:], constant=0.0)
    tiles = [("m", t) for t in range(9)] + [("c", u) for u in range(8)]
    for k, (kind, t) in enumerate(tiles):
        p = k % 2
        hsrc = B.hT if kind == "m" else B.hcT
        kcol = (NCX + t * 128) if kind == "m" else t * 128
        kidx = kcol // 128
        ti = t if kind == "m" else 9 + t
        cols = slice(t * 128, (t + 1) * 128)
        A.mm([f"ps{p}"], WM, [MM(ps[p][:, 0:512], hsrc[:, dc, cols], wm[:, dc, 512:1024], dc == 0, dc == 15) for dc in range(16)])
        A.mm(["ps4"], WM, [MM(ps[4][:, 0:64], hsrc[:, dc, cols], wm[:, dc, 1024:1088], dc == 0, dc == 15) for dc in range(16)])
        if kind == "m":
            A.mm([f"ps{2 + p}"], WM, [MM(ps[2 + p][:, 0:512], hsrc[:, dc, cols], wm[:, dc, 0:512], dc == 0, dc == 15) for dc in range(16)])
        rs, rk = rms_rstd_a(B, ps[p][:, 0:512], f"ps{p}", junk[:], st, 3 * k, KVR)
        A.op(dve, "scalar_tensor_tensor", R=[f"ps{p}", rk, "gkv_b"], W=[f"ckv32_{p}"], out=ckv32[p][:], in0=ps[p][:, 0:512],
             scalar=rs, in1=gkv_b[:], op0=ALU.mult, op1=ALU.mult)
        if kind == "m":
            A.dma(sp, B.ckv_main[cols, :], ckv32[p][:], R=[f"ckv32_{p}"], W=[], stream=f"ock{p}")
        A.op(pool, "tensor_copy", R=[f"ckv32_{p}"], W=[("ckv_tm", kidx)], out=B.ckv_tm[:, kidx, :], in_=ckv32[p][:])
        A.mm(["ps5"], [f"ckv32_{p}"], [TR(ps[5][:, cc * 128:(cc + 1) * 128], ckv32[p][:, cc * 128:(cc + 1) * 128], B.ident[:])
                                      for cc in range(4)])
        A.op(act, "copy", R=["ps5"], W=[("ckvT", kidx)], out=B.ckvT[:, :, kcol:kcol + 128],
             in_=ps[5][:].rearrange("p (a b) -> p a b", b=128))
        A.op(dve, "tensor_tensor", R=["ps4", "cs_t"], W=["r1"], out=r1[:], in0=ps[4][:, 0:64], in1=cs_t[:, ti, :], op=ALU.mult)
        A.op(dve, "tensor_tensor", R=["ps4", "sn_t"], W=["r2a"], out=r2[:, 0:32], in0=ps[4][:, 32:64], in1=sn_t[:, ti, 0:32], op=ALU.mult)
        A.op(dve, "tensor_tensor", R=["ps4", "sn_t"], W=["r2b"], out=r2[:, 32:64], in0=ps[4][:, 0:32], in1=sn_t[:, ti, 32:64], op=ALU.mult)
        A.op(dve, "tensor_tensor", R=["r1", "r2a", "r2b"], W=[f"kpe32_{p}"], out=kpe32[p][:], in0=r1[:], in1=r2[:], op=ALU.add)
        if kind == "m":
            A.dma(sp, B.kpe_main[cols, :], kpe32[p][:], R=[f"kpe32_{p}"], W=[], stream=f"okp{p}")
        A.mm(["ps7"], [f"kpe32_{p}"], [TR(ps[7][0:64, 0:128], kpe32[p][:, 0:64], B.ident[:])])
        A.op(act, "copy", R=["ps7"], W=[("kpeT", kidx)], out=B.kpeT[0:64, kcol:kcol + 128], in_=ps[7][0:64, 0:128])
        if kind == "m":
            rs, rk = rms_rstd_a(B, ps[2 + p][:, 0:512], f"ps{2 + p}", junk[:], st, 51 + 3 * k, QR)
            A.op(dve, "scalar_tensor_tensor", R=[f"ps{2 + p}", rk, "gq_b"], W=[f"qn32_{p}"], out=qn32[p][:], in0=ps[2 + p][:, 0:512],
                 scalar=rs, in1=gq_b[:], op0=ALU.mult, op1=ALU.mult)
            A.mm(["ps6"], [f"qn32_{p}"], [TR(ps[6][:, cc * 128:(cc + 1) * 128], qn32[p][:, cc * 128:(cc + 1) * 128], B.ident[:])
                                         for cc in range(4)])
            A.op(act, "copy", R=["ps6"], W=[("qnT", t)], out=B.qnT[:, :, cols], in_=ps[6][:].rearrange("p (a b) -> p a b", b=128))
    if B.debug and B.stop_after == 2:
        B.dbg("ckvT", B.ckvT[:], BF16)
        B.dbg("kpeT", B.kpeT[:], BF16)
        B.dbg("qnT", B.qnT[:], BF16)


def phase3(B):
    nc, K, A = B.nc, B.K, B.A
    pe, act, dve, pool, sp = K.pe, K.act, K.dve, K.pool, K.sp
    ps = K.ps
    tmp = B.bump([(OFF_OB, ARENA_BYTES), (OFF_C, OFF_LAT)])
    Sctx = tmp("Sctx", [128, 8, 128], F32)
    wf2 = [tmp(f"wf{i}", [128, 16, 128], BF16) for i in range(2)]
    wi2 = [tmp(f"wi{i}", [128, 16, 128], BF16) for i in range(2)]
    sig = tmp("sig", [128, 512], F32)
    logf = tmp("logf", [128, 512], F32)
    bT = tmp("bT", [128, 512], F32)
    kT = tmp("kT", [128, 512], F32)
    e1 = tmp("e1", [128, 512], F32)
    ke32 = tmp("ke32", [128, 512], F32)
    kdT = tmp("kdT", [128, 512], F32)
    kd_tm = tmp("kd_tm", [128, 4, 128], BF16)
    v_tm = tmp("v_tm", [128, 4, 128], BF16)
    ebl = tmp("ebl", [128, 16], F32)
    S32 = tmp("S32", [128, 128], F32)
    S_bf = tmp("S_bf", [128, 128], BF16)
    cm64 = tmp("cm64", [128, 512], F32)
    cm8 = tmp("cm8", [128, 128], F32)
    wq2 = [tmp(f"wq{i}", [128, 16, 128], BF16) for i in range(2)]
    wg2 = [tmp(f"wg{i}", [128, 16, 128], BF16) for i in range(2)]
    qT = tmp("qT", [128, 512], F32)
    qeT = tmp("qeT", [128, 512], BF16)
    keT = tmp("keT", [128, 512], BF16)
    AT = [tmp(f"AT{i}", [128, 128], BF16) for i in range(2)]
    oT = tmp("oT", [128, 512], F32)
    sq = tmp("sq", [128, 512], BF16)
    rstd = tmp("rstd", [128, 512], F32)
    sg = tmp("sg", [128, 512], F32)
    t1 = tmp("t1", [128, 512], F32)
    Sin32 = tmp("Sin32", [128, 16, 128], F32)
    Sin_bf = tmp("Sin_bf", [128, 16, 128], BF16)
    Sout32 = [tmp(f"Sout32_{i}", [128, 128], F32) for i in range(2)]
    kdm = [tmp(f"kdm{i}", [128, 128], BF16) for i in range(2)]
    wv = B.w_in.rearrange("(c p) n -> p c n", p=128)
    M64 = B.masks[:, 0:128]
    M8 = B.masks[:, 128:256]
    IND = B.masks[:, 760:776]
    ghT = B.vecT[:, 88:96]

    A.op(dve, "memset", W=["cm64"], ap=cm64[:], constant=1.0)
    A.op(dve, "memset", W=["cm64"], ap=cm64[:].rearrange("p (c t) -> p c t", t=64)[:, :, 0:1], constant=0.0)
    A.op(dve, "memset", W=["cm8"], ap=cm8[:], constant=1.0)
    A.op(dve, "memset", W=["cm8"], ap=cm8[:].rearrange("p (c t) -> p c t", t=8)[:, :, 0:1], constant=0.0)

    def ldw(dst, key, col0):
        for hh in range(2):
            A.dma(pool, dst[:, hh * 8:(hh + 1) * 8, :], wv[:, hh * 8:(hh + 1) * 8, col0:col0 + 128], R=[], W=[(key, hh)],
                  stream=f"{key}{hh}")
        return [(key, 0), (key, 1)]

    def proj_T(bank, w, wk, hsrc, c0, n):
        A.mm([f"ps{bank}"], wk, [MM(ps[bank][:, 0:n], w[:, dc, :], hsrc[:, dc, c0:c0 + n], dc == 0, dc == 15) for dc in range(16)])

    for is_ctx in (True, False):
        hsrc = B.hcT if is_ctx else B.hT
        blks = CBLKS if is_ctx else BLKS
        def ldhead(hh_):
            s_ = hh_ % 2
            ks = [ldw(wf2[s_], f"wf{s_}", O_FA + hh_ * 128), ldw(wi2[s_], f"wi{s_}", O_IA + hh_ * 128)]
            if not is_ctx:
                ks += [ldw(wq2[s_], f"wq{s_}", O_QA + hh_ * 128), ldw(wg2[s_], f"wg{s_}", O_GA + hh_ * 128)]
            return ks
        nxt = ldhead(0)
        for h in range(NH):
            cur = nxt
            if h + 1 < NH:
                nxt = ldhead(h + 1)
            wf, wi = wf2[h % 2], wi2[h % 2]
            wfk, wik = cur[0], cur[1]
            if not is_ctx:
                wq, wg = wq2[h % 2], wg2[h % 2]
                wqk, wgk = cur[2], cur[3]
            if is_ctx:
                A.op(dve, "memset", W=["S32"], ap=S32[:], constant=0.0)
            else:
                A.op(dve, "tensor_copy", R=[("Sctx", h)], W=["S32"], out=S32[:], in_=Sctx[:, h, :])
                A.op(act, "copy", R=["S32"], W=["S_bf"], out=S_bf[:], in_=S32[:])
            for (c0, n) in blks:
                sample = (c0 == NTP) and not is_ctx
                tiles = n // 128
                csz = 8 if sample else 64
                nch = n // csz
                cm = cm8 if sample else cm64
                proj_T(0, wf, wfk, hsrc, c0, n)
                A.op(act, "activation", R=["ps0"], W=["sig"], out=sig[:, 0:n], in_=ps[0][:, 0:n], func=AF.Sigmoid)
                A.op(dve, "tensor_scalar", R=["sig"], W=["kT"], out=kT[:, 0:n], in0=sig[:, 0:n], scalar1=B.omlT[:, h:h + 1],
                     scalar2=B.lbT[:, h:h + 1], op0=ALU.mult, op1=ALU.add)
                A.op(act, "activation", R=["kT"], W=["logf"], out=logf[:, 0:n], in_=kT[:, 0:n], func=AF.Ln)
                A.op(dve, "tensor_scalar", R=["sig", "logf"], W=["kT"], out=kT[:, 0:n], in0=sig[:, 0:n], scalar1=B.nomlT[:, h:h + 1],
                     scalar2=B.omlT[:, h:h + 1], op0=ALU.mult, op1=ALU.add)
                A.op(dve, "tensor_tensor_scan", R=["logf", "cm64", "cm8"], W=["bT"], out=bT[:, 0:n], data0=cm[:, 0:n],
                     data1=logf[:, 0:n], initial=0.0, op0=ALU.mult, op1=ALU.add)
                A.op(act, "activation", R=["bT"], W=["ebl"], out=ebl[:, 0:nch],
                     in_=bT[:, 0:n].rearrange("p (c t) -> p c t", t=csz)[:, :, csz - 1], func=AF.Exp)
                A.op(act, "activation", R=["bT"], W=["e1"], out=e1[:, 0:n], in_=bT[:, 0:n], func=AF.Exp, scale=-1.0)
                A.op(dve, "tensor_tensor", R=["kT", "e1"], W=["ke32"], out=ke32[:, 0:n], in0=kT[:, 0:n], in1=e1[:, 0:n], op=ALU.mult)
                A.op(dve, "tensor_tensor", R=["ke32", "ebl"], W=["kdT"], out=kdT[:, 0:n].rearrange("p (c t) -> p c t", t=csz),
                     in0=ke32[:, 0:n].rearrange("p (c t) -> p c t", t=csz),
                     in1=ebl[:, 0:nch].unsqueeze(2).broadcast_to([128, nch, csz]), op=ALU.mult)
                if not is_ctx:
                    A.op(pool, "tensor_copy", R=["ke32"], W=["keT"], out=keT[:, 0:n], in_=ke32[:, 0:n])
                    proj_T(1, wq, wqk, hsrc, c0, n)
                    A.op(act, "activation", R=["ps1"], W=["qT"], out=qT[:, 0:n], in_=ps[1][:, 0:n], func=AF.Silu)
                    A.op(act, "activation", R=["bT"], W=["e1"], out=e1[:, 0:n], in_=bT[:, 0:n], func=AF.Exp)
                    A.op(dve, "tensor_tensor", R=["qT", "e1"], W=["qeT"], out=qeT[:, 0:n], in0=qT[:, 0:n], in1=e1[:, 0:n], op=ALU.mult)
                    proj_T(2, wg, wgk, hsrc, c0, n)
                    A.op(act, "activation", R=["ps2"], W=["sg"], out=sg[:, 0:n], in_=ps[2][:, 0:n], func=AF.Sigmoid)
                for t in range(tiles):
                    A.mm(["ps3"], wik, [MM(ps[3][:, t * 128:(t + 1) * 128], hsrc[:, dc, c0 + t * 128:c0 + (t + 1) * 128], wi[:, dc, :],
                                           dc == 0, dc == 15) for dc in range(16)])
                A.op(act, "copy", R=["ps3"], W=["v_tm"], out=v_tm[:, 0:tiles, :], in_=ps[3][:, 0:n].rearrange("p (a b) -> p a b", b=128))
                A.mm(["ps4"], ["kdT"], [TR(ps[4][:, t * 128:(t + 1) * 128], kdT[:, t * 128:(t + 1) * 128], B.ident[:]) for t in range(tiles)])
                A.op(dve, "tensor_copy", R=["ps4"], W=["kd_tm"], out=kd_tm[:, 0:tiles, :], in_=ps[4][:, 0:n].rearrange("p (a b) -> p a b", b=128))
                if not sample:
                    for t in range(tiles):
                        tc_ = slice(t * 128, (t + 1) * 128)
                        if not is_ctx:
                            A.mm(["ps5"], ["keT", "qeT"], [MM(ps[5][:, 0:128], keT[:, tc_], qeT[:, tc_], True, True)])
                            A.op(dve, "tensor_tensor", R=["ps5"], W=[f"AT{t % 2}"], out=AT[t % 2][:], in0=ps[5][:, 0:128], in1=M64, op=ALU.mult)
                        for j in range(2):
                            ch = t * 2 + j
                            cc_ = slice(t * 128 + j * 64, t * 128 + (j + 1) * 64)
                            if not is_ctx:
                                A.mm(["ps6"], ["S_bf", "qeT"], [MM(ps[6][:, cc_], S_bf[:], qeT[:, cc_], j == 0, False)])
                            A.mm(["ps7"], ["kd_tm", "v_tm"], [MM(ps[7][:, 0:128], kd_tm[j * 64:(j + 1) * 64, t, :], v_tm[j * 64:(j + 1) * 64, t, :],
                                                                True, True)])
                            A.op(dve, "scalar_tensor_tensor", R=["ps7", "ebl", "S32"], W=["S32"], out=S32[:], in0=S32[:], scalar=ebl[:, ch:ch + 1],
                                 in1=ps[7][:, 0:128], op0=ALU.mult, op1=ALU.add)
                            if not is_ctx:
                                A.op(act, "copy", R=["S32"], W=["S_bf"], out=S_bf[:], in_=S32[:])
                        if not is_ctx:
                            A.mm(["ps6"], ["v_tm", f"AT{t % 2}"], [MM(ps[6][:, tc_], v_tm[:, t, :], AT[t % 2][:], False, True)])
                    if not is_ctx and c0 + n == NTP:
                        A.dma(sp, B.s_prompt[h], S32[:], R=["S32"], W=[], stream="osp")
                else:
                    A.dma(sp, Sin32[:], B.state_in[:, h].rearrange("b k v -> k b v"), R=[], W=["Sin32"], stream="sin")
                    A.op(pool, "tensor_copy", R=["Sin32"], W=["Sin_bf"], out=Sin_bf[:], in_=Sin32[:])
                    A.mm(["ps5"], ["keT", "qeT"], [MM(ps[5][:, 0:128], keT[:, 0:128], qeT[:, 0:128], True, True)])
                    A.op(dve, "tensor_tensor", R=["ps5"], W=["AT0"], out=AT[0][:], in0=ps[5][:, 0:128], in1=M8, op=ALU.mult)
                    A.mm(["ps6"], ["Sin_bf", "qeT"], [MM(ps[6][:, b * 8:(b + 1) * 8], Sin_bf[:, b, :], qeT[:, b * 8:(b + 1) * 8], b == 0, False)
                                                      for b in range(NB)])
                    A.mm(["ps6"], ["v_tm", "AT0"], [MM(ps[6][:, 0:128], v_tm[:, 0, :], AT[0][:], False, True)])
                    for b in range(NB):
                        pb = 7 if b % 2 == 0 else 4
                        A.op(dve, "tensor_scalar", R=["kd_tm"], W=[f"kdm{b % 2}"], out=kdm[b % 2][:], in0=kd_tm[:, 0, :], scalar1=IND[:, b:b + 1],
                             scalar2=None, op0=ALU.mult)
                        A.mm([f"ps{pb}"], [f"kdm{b % 2}", "v_tm"], [MM(ps[pb][:, 0:128], kdm[b % 2][:], v_tm[:, 0, :], True, True)])
                        A.op(dve, "scalar_tensor_tensor", R=[f"ps{pb}", "ebl", "Sin32"], W=[f"Sout32_{b % 2}"], out=Sout32[b % 2][:], in0=Sin32[:, b, :],
                             scalar=ebl[:, b:b + 1], in1=ps[pb][:, 0:128], op0=ALU.mult, op1=ALU.add)
                        A.dma(sp, B.s_sample[b, h], Sout32[b % 2][:], R=[f"Sout32_{b % 2}"], W=[], stream=f"oss{b % 2}")
                if not is_ctx:
                    A.op(act, "copy", R=["ps6"], W=["oT"], out=oT[:, 0:n], in_=ps[6][:, 0:n])
                    A.op(pool, "tensor_tensor", R=["oT"], W=["sq"], out=sq[:, 0:n], in0=oT[:, 0:n], in1=oT[:, 0:n], op=ALU.mult)
                    A.mm(["ps0"], ["sq"], [MM(ps[0][:, 0:n], B.ones_bf[:], sq[:, 0:n], True, True)])
                    A.op(act, "activation", R=["ps0"], W=["rstd"], out=rstd[:, 0:n], in_=ps[0][:, 0:n], func=AF.Sqrt, scale=1.0 / DV,
                         bias=B.eps_t[:, 0:1])
                    A.op(dve, "reciprocal", R=["rstd"], W=["rstd"], out=rstd[:, 0:n], in_=rstd[:, 0:n])
                    A.op(dve, "scalar_tensor_tensor", R=["oT", "rstd"], W=["t1"], out=t1[:, 0:n], in0=oT[:, 0:n], scalar=ghT[:, h:h + 1],
                         in1=rstd[:, 0:n], op0=ALU.mult, op1=ALU.mult)
                    A.op(dve, "tensor_tensor", R=["t1", "sg"], W=[("oaT", h, c0)], out=B.oaT[:, h, c0:c0 + n], in0=t1[:, 0:n], in1=sg[:, 0:n],
                         op=ALU.mult)
            if is_ctx:
                A.op(dve, "tensor_scalar", R=["S32"], W=[("Sctx", h)], out=Sctx[:, h, :], in0=S32[:], scalar1=B.cflag[:, 0:1], scalar2=None,
                     op0=ALU.mult)
    if B.debug and B.stop_after == 3:
        B.dbg("oaT", B.oaT[:], BF16)


def phase4(B):
    nc, K, A = B.nc, B.K, B.A
    pe, act, dve, pool, sp = K.pe, K.act, K.dve, K.pool, K.sp
    ps = K.ps
    psb = [p[:].bitcast(BF16) for p in ps]
    regions = [(OFF_C, OFF_LAT), (OFF_T, ARENA_BYTES)]
    tmp = B.bump(regions)
    wkv = tmp("wkv", [128, 4, 2048], BF16)
    qsT = tmp("qsT", [128, 5, NB, 64], BF16)
    OFF_4B = OFF_C + 2 * 4 * 2048 + 2 * 5 * NB * 64
    r1 = tmp("r1q", [64, 512], F32)
    r2 = tmp("r2q", [64, 512], F32)
    wqh = tmp("wqh", [128, 4, 192], BF16)
    tmp = B.bump([(OFF_T, ARENA_BYTES)])
    w_ukT = tmp("w_ukT", [128, 8, 512], BF16)
    q_nopeT = tmp("q_nopeT", [128, NT], BF16)
    q_peT = tmp("q_peT", [128, NT], BF16)
    csT = tmp("csT", [64, NT], F32)
    snT = tmp("snT", [64, NT], F32)
    q_latT = tmp("q_latT", [128, 4, 512], BF16)
    pT = [tmp(f"pT{i}", [128, 512], BF16) for i in range(2)]
    olat_sb = tmp("olat_sb", [128, 4, 512], BF16)
    rden = tmp("rden", [128, 512], F32)
    MC = B.masks[:, 256:384]
    LAT = ["lat"]

    wkv_v = B.w_kv_up.rearrange("(c p) n -> p c n", p=128)
    for cc in range(4):
        for q4 in range(4):
            A.dma(pool, wkv[:, cc, q4 * 512:(q4 + 1) * 512], wkv_v[:, cc, q4 * 512:(q4 + 1) * 512], R=[], W=[("wkv", cc, q4)], stream=f"wkv{q4}")
    WKV = [("wkv", cc, q4) for cc in range(4) for q4 in range(4)]
    A.dma_group(sp, [(csT[:], B.ropeT_cs), (snT[:], B.ropeT_sn)], R=[], W=["csT", "snT"], stream="c0")
    A.op(dve, "memset", W=["q_peT1"], ap=q_peT[64:65, :], constant=1.0)
    for h in range(NH):
        A.mm(["ps7"], WKV, [TR(psb[7][:, cc * 128:(cc + 1) * 128], wkv[:, cc, h * 256:h * 256 + 128], B.ident_bf[:]) for cc in range(4)])
        A.op(act if h % 2 else dve, "copy" if h % 2 else "tensor_copy", R=["ps7"], W=[("w_ukT", h)], out=w_ukT[:, h, :], in_=psb[7][:, 0:512])
    wq_v = B.w_q_up.rearrange("(c p) n -> p c n", p=128)
    for h in range(NH):
        A.dma(pool, wqh[:], wq_v[:, :, h * 192:(h + 1) * 192], R=[], W=["wqh"], stream="wqh")
        for bi, (c0, n) in enumerate(BLKS):
            cols = slice(c0, c0 + n)
            A.mm(["ps7"], ["wqh"], [MM(ps[7][:, 0:n], wqh[:, cc, 0:128], B.qnT[:, cc, cols], cc == 0, cc == 3) for cc in range(4)])
            A.op(act, "copy", R=["ps7"], W=[("q_nopeT", bi)], out=q_nopeT[:, cols], in_=ps[7][:, 0:n])
            A.mm(["ps5"], ["wqh"], [MM(ps[5][0:64, 0:n], wqh[:, cc, 128:192], B.qnT[:, cc, cols], cc == 0, cc == 3) for cc in range(4)])
            A.mm(["ps6"], ["wqh"], [MM(ps[6][0:32, 0:n], wqh[:, cc, 160:192], B.qnT[:, cc, cols], cc == 0, cc == 3) for cc in range(4)]
                 + [MM(ps[6][32:64, 0:n], wqh[:, cc, 128:160], B.qnT[:, cc, cols], cc == 0, cc == 3) for cc in range(4)])
            A.op(dve, "tensor_tensor", R=["ps5", "csT"], W=["r1q"], out=r1[:, 0:n], in0=ps[5][0:64, 0:n], in1=csT[:, cols], op=ALU.mult)
            A.op(dve, "tensor_tensor", R=["ps6", "snT"], W=["r2q"], out=r2[:, 0:n], in0=ps[6][0:64, 0:n], in1=snT[:, cols], op=ALU.mult)
            A.op(dve, "tensor_tensor", R=["r1q", "r2q"], W=[("q_peT", bi)], out=q_peT[0:64, cols], in0=r1[:, 0:n], in1=r2[:, 0:n], op=ALU.add)
        for cc in range(4):
            A.mm(["ps7"], [("w_ukT", h), ("q_nopeT", 2)], [MM(ps[7][:, 0:128], w_ukT[:, h, cc * 128:(cc + 1) * 128], q_nopeT[:, NTP:NT], True, True)])
            A.op(act if cc % 2 else dve, "copy" if cc % 2 else "tensor_copy", R=["ps7"], W=[("qsT", h)], out=qsT[:, cc, :, h * 8:(h + 1) * 8],
                 in_=ps[7][:, 0:128].rearrange("p (b t) -> p b t", t=TS))
        A.op(dve, "tensor_copy", R=[("q_peT", 2)], W=[("qsT", h)], out=qsT[0:64, 4, :, h * 8:(h + 1) * 8],
             in_=q_peT[0:64, NTP:NT].rearrange("p (b t) -> p b t", t=TS))
        for qb in range(2):
            c0 = qb * 512
            for cc in range(4):
                A.mm(["ps7"], [("w_ukT", h), ("q_nopeT", qb)], [MM(ps[7][:, 0:512], w_ukT[:, h, cc * 128:(cc + 1) * 128], q_nopeT[:, c0:c0 + 512], True, True)])
                A.op(act if cc % 2 else dve, "copy" if cc % 2 else "tensor_copy", R=["ps7"], W=["q_latT"], out=q_latT[:, cc, :], in_=ps[7][:, 0:512])
            keys = [(u, 0, False) for u in range(8)]
            for j in range(4 * qb + 4):
                keys.append((8 + j, 128 * max(0, j - 4 * qb), j >= 4 * qb))
            def st_score(i):
                kidx, cs, diag = keys[i]
                sbk = 5 + i % 2
                kc = slice(kidx * 128, (kidx + 1) * 128)
                A.mm([f"ps{sbk}"], LAT + ["q_latT", ("q_peT", qb), "q_peT1"],
                     [MM(ps[sbk][:, cs:512], B.ckvT[:, cc, kc], q_latT[:, cc, cs:512], cc == 0, False) for cc in range(4)]
                     + [MM(ps[sbk][:, cs:512], B.kpeT[0:65, kc], q_peT[0:65, c0 + cs:c0 + 512], False, True)])
                A.op(act, "activation", R=[f"ps{sbk}"], W=[f"pT{i % 2}"], out=pT[i % 2][:, cs:512], in_=ps[sbk][:, cs:512], func=AF.Exp, scale=SCALE)
                if diag:
                    A.op(pool, "tensor_tensor", R=[f"pT{i % 2}"], W=[f"pT{i % 2}"], out=pT[i % 2][:, cs:cs + 128], in0=pT[i % 2][:, cs:cs + 128],
                         in1=MC, op=ALU.mult)

            def st_pv(i):
                kidx, cs, diag = keys[i]
                last = i == len(keys) - 1
                for cc in range(4):
                    A.mm([f"ps{cc}"], LAT + [f"pT{i % 2}"], [MM(ps[cc][:, cs:512], B.ckv_tm[:, kidx, cc * 128:(cc + 1) * 128], pT[i % 2][:, cs:512], i == 0, last)])
                A.mm(["ps4"], [f"pT{i % 2}"], [MM(ps[4][:, cs:512], B.ones_bf[:], pT[i % 2][:, cs:512], i == 0, last)])

            for i in range(-1, len(keys)):
                if i + 1 < len(keys):
                    st_score(i + 1)
                if i >= 0:
                    st_pv(i)
            for cc in range(4):
                A.op(act if cc % 2 else dve, "copy" if cc % 2 else "tensor_copy", R=[f"ps{cc}"], W=["olat_sb"], out=olat_sb[:, cc, :], in_=ps[cc][:, 0:512])
            A.op(dve, "reciprocal", R=["ps4"], W=["rden"], out=rden[:], in_=ps[4][:, 0:512])
            A.mm(["ps7"], ["olat_sb"] + WKV, [MM(ps[7][:, 0:512], wkv[:, cc, h * 256 + 128:h * 256 + 256], olat_sb[:, cc, :], cc == 0, cc == 3) for cc in range(4)])
            A.op(dve, "tensor_tensor", R=["ps7", "rden"], W=[("obT", h, qb)], out=B.obT[:, h, c0:c0 + 512], in0=ps[7][:, 0:512], in1=rden[:], op=ALU.mult)

    K.barrier()
    A.reset()
    tmp = B.bump([(OFF_4B, OFF_LAT), (OFF_T, ARENA_BYTES)])
    idx_i = tmp("idx_i", [128, NB * NPAGES], I32)
    ptab_i = tmp("ptab_i", [128, NB * NPAGES], I32)
    ptab_f = tmp("ptab_f", [128, NB * NPAGES], F32)
    smask = tmp("smask", [128, NB * 64], F32)
    NSLOT = 4
    pg32 = [tmp(f"pg32_{i}", [128, 576], F32) for i in range(NSLOT)]
    pg_bf = [tmp(f"pg_bf{i}", [128, 576], BF16) for i in range(NSLOT)]
    KT_pg = [tmp(f"KT_pg{i}", [128, 640], BF16) for i in range(2)]
    pTs = [tmp(f"pTs{i}", [128, 64], BF16) for i in range(2)]
    olat_s = tmp("olat_s", [64, 512], BF16)
    olatT_s = tmp("olatT_s", [128, 256], BF16)
    rden_s = tmp("rden_s", [64, 1], F32)
    PIOTA = B.masks[:, 800:801]
    A.dma_group(sp, [(ptab_i[:], B.ptab.partition_broadcast(128)), (smask[:], B.smask_d)], R=[], W=["ptab_i", "smask"], stream="c0")
    A.op(dve, "tensor_copy", R=["ptab_i"], W=["ptab_f"], out=ptab_f[:], in_=ptab_i[:])
    A.op(dve, "tensor_scalar", R=["ptab_f"], W=["idx_i"], out=idx_i[:], in0=ptab_f[:], scalar1=128.0, scalar2=PIOTA, op0=ALU.mult, op1=ALU.add)
    pages = [(b, j) for b in range(NB) for j in range(NPAGES + 1)]
    NPG = len(pages)

    def stA(n):
        b, j = pages[n]
        if j == NPAGES:
            return
        sl = n % NSLOT
        col = b * NPAGES + j
        A.dma(pool, pg32[sl][:], B.cache_all, R=["idx_i"], W=[("pg32", sl)], stream=f"pgc{sl}", indirect=idx_i[:, col:col + 1])

    def stB(n):
        b, j = pages[n]
        if j == NPAGES:
            return
        sl = n % NSLOT
        A.op(dve, "tensor_copy", R=[("pg32", sl)], W=[("pg_bf", sl)], out=pg_bf[sl][:], in_=pg32[sl][:])

    def stC(n):
        b, j = pages[n]
        if j == NPAGES:
            return
        sl, s2 = n % NSLOT, n % 2
        tb = 2 + s2
        A.mm([f"ps{tb}"], [("pg_bf", sl)], [TR(psb[tb][:, i * 128:(i + 1) * 128], pg_bf[sl][:, i * 128:(i + 1) * 128], B.ident_bf[:]) for i in range(4)]
             + [TR(psb[tb][0:64, 512:640], pg_bf[sl][:, 512:576], B.ident_bf[:])])
        A.op(act, "copy", R=[f"ps{tb}"], W=[("KT", s2, 0)], out=KT_pg[s2][:, 0:512], in_=psb[tb][:, 0:512])
        A.op(act, "copy", R=[f"ps{tb}"], W=[("KT", s2, 1)], out=KT_pg[s2][0:64, 512:640], in_=psb[tb][0:64, 512:640])

    def stD(n):
        b, j = pages[n]
        s2 = n % 2
        qv = [qsT[:, i, b, :] for i in range(4)]
        qp = qsT[0:64, 4, b, :]
        if j < NPAGES:
            kts = [KT_pg[s2][:, i * 128:(i + 1) * 128] for i in range(4)]
            ktp = KT_pg[s2][0:64, 512:640]
            Rk = [("KT", s2, 0), ("KT", s2, 1)]
        else:
            kts = [B.ckvT[:, i, NCX + NTP:NKEY] for i in range(4)]
            ktp = B.kpeT[0:64, NCX + NTP:NKEY]
            Rk = []
        sbk = 4 + s2
        A.mm([f"ps{sbk}"], Rk, [MM(ps[sbk][:, 0:64], kts[i], qv[i], i == 0, False) for i in range(4)] + [MM(ps[sbk][:, 0:64], ktp, qp, False, True)])
        A.op(act, "activation", R=[f"ps{sbk}"], W=[f"pTs{s2}"], out=pTs[s2][:], in_=ps[sbk][:, 0:64], func=AF.Exp, scale=SCALE)
        if j == NPAGES:
            A.op(dve, "tensor_tensor", R=[f"pTs{s2}", "smask"], W=[f"pTs{s2}"], out=pTs[s2][:], in0=pTs[s2][:], in1=smask[:, b * 64:(b + 1) * 64], op=ALU.mult)

    def stE(n):
        b, j = pages[n]
        sl, s2 = n % NSLOT, n % 2
        if j < NPAGES:
            V = pg_bf[sl][:, 0:512]
            Rv = [("pg_bf", sl)]
        else:
            V = B.ckv_tm[:, 16, :]
            Rv = []
        A.mm(["ps0"], Rv + [f"pTs{s2}"], [MM(ps[0][0:64, 0:512], pTs[s2][:], V, j == 0, j == NPAGES)])
        A.mm(["ps1"], [f"pTs{s2}"], [MM(ps[1][0:64, 0:1], pTs[s2][:], B.ones_bf[:, 0:1], j == 0, j == NPAGES)])
        if j < NPAGES:
            return
        A.op(dve, "reciprocal", R=["ps1"], W=["rden_s"], out=rden_s[:], in_=ps[1][0:64, 0:1])
        A.op(dve, "tensor_scalar", R=["ps0", "rden_s"], W=["olat_s"], out=olat_s[:], in0=ps[0][0:64, 0:512], scalar1=rden_s[:, 0:1], scalar2=None, op0=ALU.mult)
        A.mm(["ps6"], ["olat_s"], [TR(psb[6][:, cc * 64:(cc + 1) * 64], olat_s[0:64, cc * 128:(cc + 1) * 128], B.ident_bf[0:64, 0:64]) for cc in range(4)])
        A.op(act, "copy", R=["ps6"], W=["olatT_s"], out=olatT_s[:], in_=psb[6][:, 0:256])
        A.mm(["ps7"], ["olatT_s"], [MM(ps[7][:, h * 8:(h + 1) * 8], wkv[:, cc, h * 256 + 128:h * 256 + 256], olatT_s[:, cc * 64 + h * 8:cc * 64 + (h + 1) * 8],
                                       (h == 0 and cc == 0), (h == NH - 1 and cc == 3)) for h in range(NH) for cc in range(4)])
        A.op(dve, "tensor_copy", R=["ps7"], W=[("obT_s", b)], out=B.obT[:, :, NTP + b * 8:NTP + (b + 1) * 8],
             in_=ps[7][:, 0:64].rearrange("p (h t) -> p h t", t=TS))

    for i in range(-6, NPG):
        for (st_, off) in ((stA, 6), (stB, 3), (stC, 2), (stD, 1), (stE, 0)):
            if 0 <= i + off < NPG:
                st_(i + off)
    if B.debug and B.stop_after == 4:
        B.dbg("obT", B.obT[:], BF16)


OFF_M = OFF_LAT
OFF_H2 = OFF_LAT + 36864


def build_G_tm(B, G_fm, GB, G_tm, sel, sel_lo, tag):
    K, A = B.K, B.A
    ps = K.ps
    for cb in range(4):
        A.mm([f"ps{cb}"], [tag + "GB", "sel"], [MM(ps[cb][:, 0:512], sel[0:17, sel_lo:sel_lo + 128], GB[0:17, cb * 512:(cb + 1) * 512], True, True)])
        A.op(K.act if cb % 2 else K.dve, "copy" if cb % 2 else "tensor_copy", R=[f"ps{cb}"], W=["G_tm"], out=G_tm[:, cb * 512:(cb + 1) * 512], in_=ps[cb][:, 0:512])


def build_GB(B, G_fm, GB, tag):
    K, A = B.K, B.A
    ps = K.ps
    for cb in range(4):
        A.mm([f"ps{cb}"], [], [TR(ps[cb][0:17, j * 128:(j + 1) * 128], G_fm[:, cb * 4 + j, :], B.ident[:]) for j in range(4)])
        A.op(K.act if cb % 2 else K.dve, "copy" if cb % 2 else "tensor_copy", R=[f"ps{cb}"], W=[tag + "GB"], out=GB[0:17, cb * 512:(cb + 1) * 512], in_=ps[cb][0:17, 0:512])


def phase5(B):
    nc, K, A = B.nc, B.K, B.A
    pe, act, dve, pool, sp = K.pe, K.act, K.dve, K.pool, K.sp
    ps = K.ps
    mT = B.view(OFF_M, [128, 16, NT], BF16)
    h2T = B.view(OFF_H2, [128, 16, NT], BF16)
    tw = B.bump([(OFF_C, OFF_LAT)])
    wa = [tw(f"wa{i}", [128, 8, 128], BF16) for i in range(2)]
    wb = [tw(f"wb{i}", [128, 8, 128], BF16) for i in range(2)]
    wga = [tw(f"wga{i}", [128, 16, 128], BF16) for i in range(2)]
    wgb = [tw(f"wgb{i}", [128, 16, 128], BF16) for i in range(2)]
    tt = B.bump([(OFF_T, ARENA_BYTES)])
    sga = [tt(f"sga{i}", [128, 512], F32) for i in range(2)]
    sgb = [tt(f"sgb{i}", [128, 512], F32) for i in range(2)]
    t1 = [tt(f"t1_{i}", [128, 512], F32) for i in range(2)]
    t2 = [tt(f"t2_{i}", [128, 512], F32) for i in range(2)]
    wv = B.w_in.rearrange("(c p) n -> p c n", p=128)
    wav = B.w_a_out.rearrange("(c p) n -> p c n", p=128)
    wbv = B.w_b_out.rearrange("(c p) n -> p c n", p=128)
    it = 0
    for fo in range(16):
        s = fo % 2
        fc = slice(fo * 128, (fo + 1) * 128)
        A.dma(pool, wa[s][:], wav[:, :, fc], R=[], W=[f"wa{s}"], stream=f"wa{s}")
        A.dma(pool, wb[s][:], wbv[:, :, fc], R=[], W=[f"wb{s}"], stream=f"wb{s}")
        for hh in range(2):
            A.dma(pool, wga[s][:, hh * 8:(hh + 1) * 8, :], wv[:, hh * 8:(hh + 1) * 8, O_GTA + fo * 128:O_GTA + (fo + 1) * 128], R=[], W=[(f"wga{s}", hh)],
                  stream=f"wga{s}{hh}")
            A.dma(pool, wgb[s][:, hh * 8:(hh + 1) * 8, :], wv[:, hh * 8:(hh + 1) * 8, O_GTB + fo * 128:O_GTB + (fo + 1) * 128], R=[], W=[(f"wgb{s}", hh)],
                  stream=f"wgb{s}{hh}")
        for (c0, n) in BLKS:
            p = it % 2
            it += 1
            b0 = 4 * p
            cols = slice(c0, c0 + n)
            A.mm([f"ps{b0}"], [f"wa{s}"], [MM(ps[b0][:, 0:n], wa[s][:, hc, :], B.oaT[:, hc, cols], hc == 0, hc == 7) for hc in range(8)])
            A.mm([f"ps{b0 + 1}"], [f"wb{s}"], [MM(ps[b0 + 1][:, 0:n], wb[s][:, hc, :], B.obT[:, hc, cols], hc == 0, hc == 7) for hc in range(8)])
            A.mm([f"ps{b0 + 2}"], [(f"wga{s}", 0), (f"wga{s}", 1)], [MM(ps[b0 + 2][:, 0:n], wga[s][:, dc, :], B.hT[:, dc, cols], dc == 0, dc == 15) for dc in range(16)])
            A.mm([f"ps{b0 + 3}"], [(f"wgb{s}", 0), (f"wgb{s}", 1)], [MM(ps[b0 + 3][:, 0:n], wgb[s][:, dc, :], B.hT[:, dc, cols], dc == 0, dc == 15) for dc in range(16)])
            A.op(act, "activation", R=[f"ps{b0 + 2}"], W=[f"sga{p}"], out=sga[p][:, 0:n], in_=ps[b0 + 2][:, 0:n], func=AF.Sigmoid)
            A.op(act, "activation", R=[f"ps{b0 + 3}"], W=[f"sgb{p}"], out=sgb[p][:, 0:n], in_=ps[b0 + 3][:, 0:n], func=AF.Sigmoid)
            A.op(dve, "tensor_tensor", R=[f"ps{b0}", f"sga{p}"], W=[f"t1_{p}"], out=t1[p][:, 0:n], in0=ps[b0][:, 0:n], in1=sga[p][:, 0:n], op=ALU.mult)
            A.op(dve, "tensor_tensor", R=[f"ps{b0 + 1}", f"sgb{p}"], W=[f"t2_{p}"], out=t2[p][:, 0:n], in0=ps[b0 + 1][:, 0:n], in1=sgb[p][:, 0:n], op=ALU.mult)
            A.op(pool, "tensor_tensor", R=[f"t1_{p}", f"t2_{p}"], W=[("mT", fo, c0)], out=mT[:, fo, cols], in0=t1[p][:, 0:n], in1=t2[p][:, 0:n], op=ALU.add)
    if B.debug and B.stop_after == 5:
        B.dbg("mT", mT[:], BF16)
    K.barrier()
    A.reset()
    wo = B.view(0, [128, 16, D], BF16)
    tmp = B.bump([(OFF_H2 + 36864, ARENA_BYTES)])
    G_tm = tmp("G_tm", [128, D], F32)
    x_t = tmp("x_t", [128, D], F32)
    x1_t = tmp("x1_t", [128, D], F32)
    xs_t = tmp("xs_t", [128, D], F32)
    GB = tmp("GB", [17, D], F32)
    sel = tmp("sel", [17, 256], F32)
    st = tmp("st5", [128, 16], F32)
    tS = tmp("tS5", [128, 4, 128], F32)
    junk = xs_t[:].bitcast(BF16)[:, 0:D]
    wov = B.w_o.rearrange("(c p) n -> p c n", p=128)
    for dc in range(16):
        for q4 in range(4):
            A.dma(pool, wo[:, dc, q4 * 512:(q4 + 1) * 512], wov[:, dc, q4 * 512:(q4 + 1) * 512], R=[], W=[("wo", dc, q4)], stream=f"wo{q4}")
    WO = [("wo", dc, q4) for dc in range(16) for q4 in range(4)]
    A.dma(sp, sel[:], B.sel_d, R=[], W=["sel"], stream="c0")
    build_GB(B, B.G1, GB, "g1")
    for t in range(9):
        cols = slice(t * 128, (t + 1) * 128)
        if t == 0 or t == 8:
            build_G_tm(B, B.G1, GB, G_tm, sel, 0 if t == 0 else 128, "g1")
        A.dma(sp, x_t[:], B.x_main[cols, :], R=[], W=["x_t"], stream="x_t")
        for cb in range(4):
            A.mm([f"ps{cb}"], WO, [MM(ps[cb][:, 0:512], mT[:, dc, cols], wo[:, dc, cb * 512:(cb + 1) * 512], dc == 0, dc == 15) for dc in range(16)])
        A.op(dve, "memset", W=["st5"], ap=st[:], constant=0.0)
        for cb in range(4):
            A.op(act, "activation", R=[f"ps{cb}"], W=["xs_t", "st5"], out=junk[:, 0:512], in_=ps[cb][:, 0:512], func=AF.Square, accum_out=st[:, cb:cb + 1])
        A.op(dve, "tensor_tensor", R=["st5"], W=["st5"], out=st[:, 4:5], in0=st[:, 0:1], in1=st[:, 1:2], op=ALU.add)
        A.op(dve, "tensor_tensor", R=["st5"], W=["st5"], out=st[:, 5:6], in0=st[:, 2:3], in1=st[:, 3:4], op=ALU.add)
        A.op(dve, "tensor_tensor", R=["st5"], W=["st5"], out=st[:, 6:7], in0=st[:, 4:5], in1=st[:, 5:6], op=ALU.add)
        A.op(act, "activation", R=["st5"], W=["st5"], out=st[:, 7:8], in_=st[:, 6:7], func=AF.Sqrt, scale=1.0 / D, bias=B.eps_t[:, 0:1])
        A.op(dve, "reciprocal", R=["st5"], W=["st5"], out=st[:, 8:9], in_=st[:, 7:8])
        for cb in range(4):
            cs_ = slice(cb * 512, (cb + 1) * 512)
            A.op(dve, "scalar_tensor_tensor", R=[f"ps{cb}", "st5", "G_tm"], W=[("x1_t", cb)], out=x1_t[:, cs_], in0=ps[cb][:, 0:512], scalar=st[:, 8:9],
                 in1=G_tm[:, cs_], op0=ALU.mult, op1=ALU.mult)
            A.op(pool, "tensor_tensor", R=[("x1_t", cb), "x_t"], W=[("x1_t", cb)], out=x1_t[:, cs_], in0=x1_t[:, cs_], in1=x_t[:, cs_], op=ALU.add)
        X1 = [("x1_t", cb) for cb in range(4)]
        A.dma(sp, B.y_main[cols, :], x1_t[:], R=X1, W=[], stream="x1o")
        A.op(act, "activation", R=X1, W=["xs_t", "st5"], out=junk[:], in_=x1_t[:], func=AF.Square, accum_out=st[:, 9:10])
        A.op(act, "activation", R=["st5"], W=["st5"], out=st[:, 10:11], in_=st[:, 9:10], func=AF.Sqrt, scale=1.0 / D, bias=B.eps_t[:, 0:1])
        A.op(dve, "reciprocal", R=["st5"], W=["st5"], out=st[:, 11:12], in_=st[:, 10:11])
        A.op(dve, "tensor_scalar", R=X1 + ["st5"], W=["xs_t"], out=xs_t[:], in0=x1_t[:], scalar1=st[:, 11:12], scalar2=None, op0=ALU.mult)
        for g in range(4):
            pb = 4 + g
            A.mm([f"ps{pb}"], ["xs_t"], [TR(ps[pb][:, j * 128:(j + 1) * 128], xs_t[:, (g * 4 + j) * 128:(g * 4 + j + 1) * 128], B.ident[:]) for j in range(4)])
            if t == 8:
                A.op(dve, "tensor_tensor", R=[f"ps{pb}"], W=["tS5"], out=tS[:].rearrange("p a (b t) -> p a b t", t=TS),
                     in0=ps[pb][:].rearrange("p (a b t) -> p a b t", a=4, t=TS),
                     in1=B.A2[:, g * 4:(g + 1) * 4, 1:17].unsqueeze(3).broadcast_to([128, 4, NB, TS]), op=ALU.mult)
                A.op(dve, "tensor_tensor", R=["tS5"], W=[("h2T", t, g)], out=h2T[:, g * 4:(g + 1) * 4, cols].rearrange("p a (b t) -> p a b t", t=TS),
                     in0=tS[:].rearrange("p a (b t) -> p a b t", t=TS),
                     in1=B.modT[:, 48 + g * 4:48 + (g + 1) * 4, 1:17].unsqueeze(3).broadcast_to([128, 4, NB, TS]), op=ALU.add)
            else:
                for j in range(4):
                    dc = g * 4 + j
                    if True:
                        A.op(dve, "tensor_scalar", R=[f"ps{pb}"], W=[("h2T", t, g, j)], out=h2T[:, dc, cols], in0=ps[pb][:, j * 128:(j + 1) * 128],
                             scalar1=B.A2[:, dc, 0:1], scalar2=B.modT[:, 48 + dc, 0:1], op0=ALU.mult, op1=ALU.add)
                    else:
                        A.op(act, "activation", R=[f"ps{pb}"], W=[("h2T", t, g, j)], out=h2T[:, dc, cols], in_=ps[pb][:, j * 128:(j + 1) * 128],
                             func=AF.Identity, scale=B.A2[:, dc, 0:1], bias=B.modT[:, 48 + dc, 0:1])
    if B.debug and B.stop_after == 5:
        B.dbg("h2T", h2T[:], BF16)


def phase6(B):
    nc, K, A = B.nc, B.K, B.A
    pe, act, dve, pool, sp = K.pe, K.act, K.dve, K.pool, K.sp
    ps = K.ps
    h2T = B.view(OFF_H2, [128, 16, NT], BF16)
    uT = B.view(0, [128, 64, 512], BF16)
    tmp = B.bump([(65536, OFF_H2), (OFF_H2 + 36864, ARENA_BYTES)])
    NWU, NWD = 3, 4
    wup = [tmp(f"wup{i}", [128, 16, 128], BF16) for i in range(NWU)]
    wd = [tmp(f"wd{i}", [128, 1024], BF16) for i in range(NWD)]
    GB = tmp("GB6", [17, D], F32)
    sel = tmp("sel6", [17, 256], F32)
    st = tmp("st6", [128, 16], F32)
    rl = [tmp(f"rl{i}", [128, 512], F32) for i in range(2)]
    junk = tmp("junk6", [128, D], BF16)
    m_tm = tmp("m_tm", [128, 4, D], F32)
    G_tm = tmp("G_tm6", [128, D], F32)
    x1_t = tmp("x1_t6", [128, D], F32)
    wuv = B.w_up.rearrange("(c p) n -> p c n", p=128)
    A.dma(sp, sel[:], B.sel_d, R=[], W=["sel"], stream="c0")
    build_GB(B, B.G2, GB, "g2")
    iu = 0
    idn = 0
    for bi, (c0, n) in enumerate(BLKS):
        tiles = n // 128
        cols = slice(c0, c0 + n)
        if bi == 0 or bi == 2:
            build_G_tm(B, B.G2, GB, G_tm, sel, 0 if bi == 0 else 128, "g2")
        for ffc in range(64):
            s = iu % NWU
            pb = iu % 4
            iu += 1
            for hh in range(2):
                A.dma(pool, wup[s][:, hh * 8:(hh + 1) * 8, :], wuv[:, hh * 8:(hh + 1) * 8, ffc * 128:(ffc + 1) * 128], R=[], W=[(f"wup{s}", hh)],
                      stream=f"wup{s}{hh}")
            A.mm([f"ps{pb}"], [(f"wup{s}", 0), (f"wup{s}", 1)], [MM(ps[pb][:, 0:n], wup[s][:, dc, :], h2T[:, dc, cols], dc == 0, dc == 15) for dc in range(16)])
            A.op(act, "activation", R=[f"ps{pb}"], W=[f"rl{pb % 2}"], out=rl[pb % 2][:, 0:n], in_=ps[pb][:, 0:n], func=AF.Relu)
            A.op(dve if ffc % 2 else pool, "tensor_tensor", R=[f"rl{pb % 2}"], W=[("uT", ffc)], out=uT[:, ffc, 0:n], in0=rl[pb % 2][:, 0:n],
                 in1=rl[pb % 2][:, 0:n], op=ALU.mult)
        for half in range(2):
            banks = [f"ps{ti * 2 + cbk}" for ti in range(tiles) for cbk in range(2)]
            for ffc in range(64):
                s = idn % NWD
                idn += 1
                A.dma(pool, wd[s][:], B.w_down[ffc * 128:(ffc + 1) * 128, half * 1024:(half + 1) * 1024], R=[], W=[f"wd{s}"], stream=f"wd{s}")
                A.mm(banks, [f"wd{s}", ("uT", ffc)], [MM(ps[ti * 2 + cbk][:, 0:512], uT[:, ffc, ti * 128:(ti + 1) * 128], wd[s][:, cbk * 512:(cbk + 1) * 512],
                                                        ffc == 0, ffc == 63) for ti in range(tiles) for cbk in range(2)])
            for ti in range(tiles):
                for cbk in range(2):
                    pb = ti * 2 + cbk
                    c_ = half * 1024 + cbk * 512
                    A.op(act if cbk else dve, "copy" if cbk else "tensor_copy", R=[f"ps{pb}"], W=[("m_tm", ti)], out=m_tm[:, ti, c_:c_ + 512], in_=ps[pb][:, 0:512])
        for ti in range(tiles):
            t = c0 // 128 + ti
            rows = slice(t * 128, (t + 1) * 128)
            A.dma(sp, x1_t[:], B.y_main[rows, :], R=[], W=["x1_t"], stream="x1i")
            A.op(dve, "memset", W=["st6"], ap=st[:], constant=0.0)
            A.op(act, "activation", R=[("m_tm", ti)], W=["junk6", "st6"], out=junk[:], in_=m_tm[:, ti, :], func=AF.Square, accum_out=st[:, 0:1])
            A.op(act, "activation", R=["st6"], W=["st6"], out=st[:, 1:2], in_=st[:, 0:1], func=AF.Sqrt, scale=1.0 / D, bias=B.eps_t[:, 0:1])
            A.op(dve, "reciprocal", R=["st6"], W=["st6"], out=st[:, 2:3], in_=st[:, 1:2])
            A.op(dve, "scalar_tensor_tensor", R=[("m_tm", ti), "st6", "G_tm"], W=[("m_tm", ti)], out=m_tm[:, ti, :], in0=m_tm[:, ti, :], scalar=st[:, 2:3],
                 in1=G_tm[:], op0=ALU.mult, op1=ALU.mult)
            A.op(pool, "tensor_tensor", R=[("m_tm", ti), "x1_t"], W=["x1_t"], out=x1_t[:], in0=m_tm[:, ti, :], in1=x1_t[:], op=ALU.add)
            A.dma(sp, B.y_main[rows, :], x1_t[:], R=["x1_t"], W=[], stream="yo")


def _host_consts(core):
    half = core % 2
    inv = (10000.0 ** (-np.arange(32, dtype=np.float32) / 32.0)).astype(np.float32)
    pos_main = np.concatenate([half * NTP + np.arange(NTP), PAST + (np.arange(NTS) % TS)]).astype(np.float32)
    pos_ctx = np.arange(NCX).astype(np.float32)
    pos = np.concatenate([pos_main, pos_ctx])
    ang = pos[:, None] * inv[None, :]
    cos = np.cos(ang).astype(np.float32)
    sin = np.sin(ang).astype(np.float32)
    cs = np.concatenate([cos, cos], axis=1)
    sn = np.concatenate([-sin, sin], axis=1)
    masks = np.zeros((128, 1024), np.float32)
    r = np.arange(128)
    masks[:, 0:128] = ((r[:, None] // 64 == r[None, :] // 64) & (r[:, None] <= r[None, :])).astype(np.float32)
    masks[:, 128:256] = ((r[:, None] // 8 == r[None, :] // 8) & (r[:, None] <= r[None, :])).astype(np.float32)
    masks[:, 256:384] = (r[:, None] <= r[None, :]).astype(np.float32)
    masks[:, 760:776] = (r[:, None] // 8 == np.arange(16)[None, :]).astype(np.float32)
    masks[:, 800] = r.astype(np.float32)
    sm = np.zeros((128, NB, NH, TS), np.float32)
    for s in range(128):
        sm[s, s // 8, :, (s % 8):] = 1.0
    sel = np.zeros((17, 256), np.float32)
    sel[0, 0:128] = 1.0
    for tok in range(128):
        sel[1 + tok // 8, 128 + tok] = 1.0
    flag = np.full((128, 1), float(half), np.float32)
    cbias = np.full((1, NCX), 0.0 if half == 1 else NEG, np.float32)
    return dict(rope_cs=cs, rope_sn=sn, ropeT_cs=np.ascontiguousarray(cs[:NT].T), ropeT_sn=np.ascontiguousarray(sn[:NT].T),
                masks=masks, smask=sm.reshape(128, NB * 64), sel=sel, ctx_flag=flag, ctx_bias=cbias,
                ident=np.eye(128, dtype=np.float32))


def make_in_maps(inputs, cores, cache_all, ptab_all):
    f = lambda a: np.ascontiguousarray(np.asarray(a, dtype=np.float32))
    x_prompt = f(inputs["x_prompt"]); x_sample = f(inputs["x_sample"])
    c_prompt = f(inputs["c_prompt"]); c_sample = f(inputs["c_sample"])
    vecs = np.concatenate([
        f(inputs["g_pre_mix"]).reshape(16, 128), f(inputs["g_post_mix"]).reshape(16, 128),
        f(inputs["g_pre_mlp"]).reshape(16, 128), f(inputs["g_post_mlp"]).reshape(16, 128),
        f(inputs["g_q_norm"]).reshape(4, 128), f(inputs["g_kv_norm"]).reshape(4, 128),
        f(inputs["lb_logits"]).reshape(16, 128), f(inputs["g_hgrn_norm"]).reshape(8, 128)], axis=0)
    shared = dict(
        w_ada=f(inputs["w_ada"])[0], b_ada=f(inputs["b_ada"]).reshape(1, 6 * D), vecs=vecs,
        g_q=f(inputs["g_q_norm"]).reshape(1, QR),
        g_kv=f(inputs["g_kv_norm"]).reshape(1, KVR), w_in=f(inputs["w_in"])[0], w_a_out=f(inputs["w_a_out"])[0],
        w_q_up=f(inputs["w_q_up"])[0], w_kv_up=f(inputs["w_kv_up"])[0], w_b_out=f(inputs["w_b_out"])[0],
        w_o=f(inputs["w_o"])[0], w_up=f(inputs["w_up"])[0], w_down=f(inputs["w_down"])[0],
        cache_all=cache_all)
    state = f(inputs["state_hgrn"])[0]
    maps = []
    for c in cores:
        b, half = c // 2, c % 2
        m = dict(shared)
        m["x_main"] = np.concatenate([x_prompt[b, half * NTP:(half + 1) * NTP], x_sample[c * NB:(c + 1) * NB].reshape(NTS, D)], axis=0)
        m["x_ctx"] = x_prompt[b, 0:NCX]
        m["c_all"] = np.concatenate([c_prompt[b:b + 1], c_sample[c * NB:(c + 1) * NB]], axis=0)
        m["state_in"] = state[c * NB:(c + 1) * NB]
        m["ptab"] = np.ascontiguousarray(ptab_all[c * NB:(c + 1) * NB]).astype(np.int32).reshape(1, NB * NPAGES)
        m.update(_host_consts(c))
        maps.append(m)
    return maps


def assemble(results, cores):
    y_prompt = np.zeros((4, 2048, D), np.float32); y_sample = np.zeros((128, TS, D), np.float32)
    ckv_p = np.zeros((1, 4, 2048, KVR), np.float32); kpe_p = np.zeros((1, 4, 2048, ROPE), np.float32)
    s_p = np.zeros((1, 4, NH, DK, DV), np.float32)
    ckv_s = np.zeros((1, 128, TS, KVR), np.float32); kpe_s = np.zeros((1, 128, TS, ROPE), np.float32)
    s_s = np.zeros((1, 128, NH, DK, DV), np.float32)
    for r, c in zip(results, cores):
        b, half = c // 2, c % 2
        sl = slice(half * NTP, (half + 1) * NTP)
        y_prompt[b, sl] = r["y_main"][:NTP]; y_sample[c * NB:(c + 1) * NB] = r["y_main"][NTP:].reshape(NB, TS, D)
        ckv_p[0, b, sl] = r["ckv_main"][:NTP]; ckv_s[0, c * NB:(c + 1) * NB] = r["ckv_main"][NTP:].reshape(NB, TS, KVR)
        kpe_p[0, b, sl] = r["kpe_main"][:NTP]; kpe_s[0, c * NB:(c + 1) * NB] = r["kpe_main"][NTP:].reshape(NB, TS, ROPE)
        if half == 1:
            s_p[0, b] = r["s_prompt"]
        s_s[0, c * NB:(c + 1) * NB] = r["s_sample"]
    return (y_prompt, y_sample, ckv_p, kpe_p, s_p, ckv_s, kpe_s, s_s)


def kernel(**inputs):
    cores = list(range(8))
    cache_ckv = np.ascontiguousarray(np.asarray(inputs["cache_ckv"], dtype=np.float32)).reshape(-1, KVR)
    cache_kpe = np.ascontiguousarray(np.asarray(inputs["cache_kpe"], dtype=np.float32)).reshape(-1, ROPE)
    ptab = np.asarray(inputs["page_table"]).astype(np.int32)
    cache_all = np.concatenate([cache_ckv, cache_kpe], axis=1)
    del cache_ckv, cache_kpe
    nc = build_nc(cache_all.shape[0])
    maps = make_in_maps(inputs, cores, cache_all, ptab)
    res = run_bass_kernel_spmd(nc, maps, core_ids=cores)
    return assemble(res.results, cores)
```

```python
import math
import numpy as np
import ml_dtypes
import concourse.bass as bass
import concourse.mybir as mybir
from concourse.bass_utils import run_bass_kernel_spmd

F32 = mybir.dt.float32
BF16 = mybir.dt.bfloat16
I32 = mybir.dt.int32
AF = mybir.ActivationFunctionType
ALU = mybir.AluOpType
AX = mybir.AxisListType

D = 2048
NH = 8
DK = 128
DV = 128
QR = 512
KVR = 512
NOPE = 128
ROPE = 64
VD = 128
DFF = 8192
EPS = 1e-6
PAST = 8192
PAGE = 128
NPAGES = 64
SCALE = (NOPE + ROPE) ** -0.5
NTP = 1024
NTS = 128
NT = NTP + NTS
NCX = 1024
NB = 16
TS = 8
NEG = -30000.0
NKEY = NCX + NTP + NTS

O_QA, O_FA, O_IA, O_GA, O_QD, O_KVD, O_KPE, O_GTA, O_GTB = 0, 1024, 2048, 3072, 4096, 4608, 5120, 5184, 7232
IN_COLS = 9280
BLKS = [(0, 512), (512, 512), (1024, 128)]
CBLKS = [(0, 512), (512, 512)]


def _flat(deps):
    out = []
    if deps is None:
        return out
    if isinstance(deps, tuple) and len(deps) == 3 and isinstance(deps[0], str):
        return [deps]
    for d in deps:
        out.extend(_flat(d))
    return out


class Eng:
    def __init__(self, nc, name, eng):
        self.nc, self.name, self.eng = nc, name, eng
        self.gen = 0
        self.sem = nc.alloc_semaphore("sem_" + name)
        self.cnt = 0
        self.seen = {}
        self.last = None

    def wait(self, deps):
        best = {}
        for (key, sem, val) in _flat(deps):
            if self.seen.get(key, 0) >= val:
                continue
            if key not in best or best[key][1] < val:
                best[key] = (sem, val)
        for key, (sem, val) in best.items():
            self.eng.wait_ge(sem, val)
            self.seen[key] = val

    def __call__(self, opname, deps=None, sig=True, **kw):
        self.wait(deps)
        inst = getattr(self.eng, opname)(**kw)
        if sig:
            if self.cnt >= 30000:
                self.gen += 1
                self.sem = self.nc.alloc_semaphore(f"sem_{self.name}_{self.gen}")
                self.cnt = 0
            self.cnt += 1
            inst.then_inc(self.sem, 1)
            self.last = (f"{self.name}.{self.gen}", self.sem, self.cnt)
            return self.last
        return None

    def tok(self):
        return self.last


class Ctx:
    def __init__(self, nc):
        self.nc = nc
        self.pe = Eng(nc, "pe", nc.tensor)
        self.act = Eng(nc, "act", nc.scalar)
        self.dve = Eng(nc, "dve", nc.vector)
        self.pool = Eng(nc, "pool", nc.gpsimd)
        self.sp = Eng(nc, "sp", nc.sync)
        self.engs = [self.pe, self.act, self.dve, self.pool, self.sp]
        self.streams = {}
        self.dma_toks = []
        self.ps = [nc.alloc_psum_tensor(f"psb{i}", [128, 512], F32) for i in range(8)]
        self.ps_free = [None] * 8
        self.ps_i = 0
        self._n = 0

    def name(self, s):
        self._n += 1
        return f"{s}_{self._n}"

    def dma(self, q, out, in_, stream, deps=None, indirect=None):
        if stream not in self.streams:
            self.streams[stream] = [self.nc.alloc_semaphore("dsem_" + stream), 0]
        st = self.streams[stream]
        q.wait(deps)
        if indirect is not None:
            inst = q.eng.indirect_dma_start(out=out, out_offset=None, in_=in_,
                                            in_offset=bass.IndirectOffsetOnAxis(ap=indirect, axis=0))
        else:
            inst = q.eng.dma_start(out=out, in_=in_)
        st[1] += 16
        inst.then_inc(st[0], 16)
        tok = ("d_" + stream, st[0], st[1])
        self.dma_toks.append(tok)
        return tok

    def psum(self):
        i = self.ps_i
        self.ps_i = (i + 1) % 8
        return i, self.ps[i], self.ps_free[i]

    def psum_release(self, i, tok):
        self.ps_free[i] = tok

    def barrier(self):
        toks = [e.tok() for e in self.engs if e.tok() is not None]
        latest = {}
        for t in self.dma_toks:
            latest[t[0]] = t
        toks += list(latest.values())
        self.dma_toks = list(latest.values())
        for e in self.engs:
            e.wait(toks)
        self.ps_free = [None] * 8


class Auto:
    def __init__(self, K):
        self.K = K
        self.w = {}
        self.r = {}

    def reset(self):
        self.w = {}
        self.r = {}

    def deps(self, R, W):
        d = []
        for k in R:
            if k in self.w:
                d.append(self.w[k])
        for k in W:
            if k in self.w:
                d.append(self.w[k])
            d.extend(self.r.get(k, {}).values())
        return d

    def note(self, tok, R, W):
        for k in R:
            self.r.setdefault(k, {})[tok[0]] = tok
        for k in W:
            self.w[k] = tok
            self.r[k] = {}

    def op(self, eng, opname, R=(), W=(), **kw):
        tok = eng(opname, deps=self.deps(R, W), **kw)
        self.note(tok, R, W)
        return tok

    def mm(self, W, R, mms):
        pe = self.K.pe
        pe.wait(self.deps(R, W))
        tok = None
        for i, (opname, kw) in enumerate(mms):
            tok = pe(opname, sig=(i == len(mms) - 1), **kw)
        self.note(tok, R, W)
        return tok

    def dma(self, q, out, in_, R, W, stream, indirect=None):
        tok = self.K.dma(q, out, in_, stream, deps=self.deps(R, W), indirect=indirect)
        self.note(tok, R, W)
        return tok

    def dma_group(self, q, items, R, W, stream):
        d = self.deps(R, W)
        tok = None
        for (out, in_) in items:
            tok = self.K.dma(q, out, in_, stream, deps=d)
            d = None
        self.note(tok, R, W)
        return tok


def sb(nc, name, shape, dt):
    return nc.alloc_sbuf_tensor(name, list(shape), dt)


class Bld:
    pass


def MM(out, lhsT, rhs, start, stop):
    return ("matmul", dict(out=out, lhsT=lhsT, rhs=rhs, start=start, stop=stop))


def TR(out, in_, identity):
    return ("transpose", dict(out=out, in_=in_, identity=identity))


ARENA_BYTES = 196608 - 4096
OFF_H = 0
OFF_C = 36864
OFF_LAT = 69632
OFF_OA = 118016
OFF_OB = 136448
OFF_T = 154880


def build_nc(n_pool_rows, stop_after=99, debug=False):
    nc = bass.Bass("TRN2", target_bir_lowering=False)
    K = Ctx(nc)
    A = Auto(K)

    def dbg(name, ap, dt=F32):
        if not debug:
            return
        K.barrier()
        d = nc.dram_tensor("dbg_" + name, list(ap.shape), dt, kind="ExternalOutput").ap()
        K.dma(K.sp, d, ap, "dbg")
        K.barrier()
    pe, act, dve, pool, sp = K.pe, K.act, K.dve, K.pool, K.sp
    B = Bld()
    B.nc, B.K, B.A = nc, K, A
    B.dbg = dbg
    B.debug = debug

    def din(name, shape, dt=F32):
        return nc.dram_tensor(name, list(shape), dt, kind="ExternalInput").ap()

    def dout(name, shape, dt=F32):
        return nc.dram_tensor(name, list(shape), dt, kind="ExternalOutput").ap()

    x_main = din("x_main", [NT, D])
    x_ctx = din("x_ctx", [NCX, D])
    c_all = din("c_all", [17, D])
    cache_all = din("cache_all", [n_pool_rows, KVR + ROPE])
    state_in = din("state_in", [NB, NH, DK, DV])
    ptab = din("ptab", [1, NB * NPAGES], I32)
    w_ada = din("w_ada", [D, 6 * D])
    b_ada = din("b_ada", [1, 6 * D])
    vecs = din("vecs", [96, 128])
    g_q = din("g_q", [1, QR])
    g_kv = din("g_kv", [1, KVR])
    w_in = din("w_in", [D, IN_COLS])
    w_a_out = din("w_a_out", [NH * DV, D])
    w_q_up = din("w_q_up", [QR, NH * (NOPE + ROPE)])
    w_kv_up = din("w_kv_up", [KVR, NH * (NOPE + VD)])
    w_b_out = din("w_b_out", [NH * VD, D])
    w_o = din("w_o", [D, D])
    w_up = din("w_up", [D, DFF])
    w_down = din("w_down", [DFF, D])
    ident_d = din("ident", [128, 128])
    rope_cs = din("rope_cs", [17 * 128, 64])
    rope_sn = din("rope_sn", [17 * 128, 64])
    ropeT_cs = din("ropeT_cs", [64, NT])
    ropeT_sn = din("ropeT_sn", [64, NT])
    ctx_bias = din("ctx_bias", [1, NCX])
    ctx_flag = din("ctx_flag", [128, 1])
    masks_d = din("masks", [128, 1024])
    smask_d = din("smask", [128, NB * 64])
    sel_d = din("sel", [17, 256])

    y_main = dout("y_main", [NT, D])
    ckv_main = dout("ckv_main", [NT, KVR])
    kpe_main = dout("kpe_main", [NT, ROPE])
    s_prompt = dout("s_prompt", [NH, DK, DV])
    s_sample = dout("s_sample", [NB, NH, DK, DV])

    ident = sb(nc, "ident_sb", [128, 128], F32)
    ident_bf = sb(nc, "ident_bf", [128, 128], BF16)
    vecT = sb(nc, "vecT", [128, 96], F32)
    modT = sb(nc, "modT", [128, 96, 17], F32)
    A1 = sb(nc, "A1", [128, 16, 17], F32)
    A2 = sb(nc, "A2", [128, 16, 17], F32)
    G1 = sb(nc, "G1", [128, 16, 17], F32)
    G2 = sb(nc, "G2", [128, 16, 17], F32)
    lbT = sb(nc, "lbT", [128, 8], F32)
    omlT = sb(nc, "omlT", [128, 8], F32)
    nomlT = sb(nc, "nomlT", [128, 8], F32)
    masks = sb(nc, "masks_sb", [128, 1024], F32)
    cflag = sb(nc, "cflag", [128, 1], F32)
    ones_bf = sb(nc, "ones_bf", [128, 128], BF16)
    ones_f = sb(nc, "ones_f", [128, 128], F32)
    eps_t = sb(nc, "eps_t", [128, 1], F32)
    arena = sb(nc, "arena", [128, ARENA_BYTES // 4], F32)
    ar_f = arena[:]
    ar_b = arena[:].bitcast(BF16)
    ar_i = arena[:].bitcast(I32)

    def view(off, shape, dt):
        n = int(np.prod(shape[1:]))
        assert off % 4 == 0
        if dt == BF16:
            assert off + 2 * n <= ARENA_BYTES, (off, shape)
            v = ar_b[:, off // 2: off // 2 + n]
        else:
            assert off + 4 * n <= ARENA_BYTES, (off, shape)
            v = (ar_f if dt == F32 else ar_i)[:, off // 4: off // 4 + n]
        if shape[0] != 128:
            v = v[0:shape[0]]
        if len(shape) == 3:
            v = v.rearrange("p (a b) -> p a b", a=shape[1])
        elif len(shape) == 4:
            v = v.rearrange("p (a b c) -> p a b c", a=shape[1], b=shape[2])
        elif len(shape) == 5:
            v = v.rearrange("p (a b c d) -> p a b c d", a=shape[1], b=shape[2], c=shape[3])
        return v

    def bump(regions):
        state = {"r": 0, "o": regions[0][0]}

        def tmp(name, shape, dt):
            n = int(np.prod(shape[1:])) * (2 if dt == BF16 else 4)
            n = (n + 63) // 64 * 64
            while True:
                s_, e_ = regions[state["r"]]
                if state["o"] + n <= e_:
                    off = state["o"]
                    state["o"] += n
                    return view(off, shape, dt)
                state["r"] += 1
                assert state["r"] < len(regions), ("arena region overflow", name)
                state["o"] = regions[state["r"]][0]
        return tmp

    REG_H, REG_C, REG_LAT = (OFF_H, OFF_C), (OFF_C, OFF_LAT), (OFF_LAT, OFF_OA)
    REG_BIG = (OFF_LAT, OFF_T)
    REG_L = (OFF_T, ARENA_BYTES)
    hT = view(OFF_H, [128, 16, NT], BF16)
    hcT = view(OFF_C, [128, 16, NCX], BF16)
    o = OFF_LAT
    ckvT = view(o, [128, 4, NKEY], BF16); o += 2 * 4 * NKEY
    kpeT = view(o, [128, NKEY], BF16); o += 2 * NKEY
    ckv_tm = view(o, [128, 17, KVR], BF16); o += 2 * 17 * KVR
    qnT = view(o, [128, 4, NT], BF16); o += 2 * 4 * NT
    assert o == OFF_OA, o
    oaT = view(OFF_OA, [128, 8, NT], BF16)
    obT = view(OFF_OB, [128, 8, NT], BF16)
    B.__dict__.update(locals())

    phases = [phase0, phase1, phase2, phase3, phase4, phase5, phase6]
    for i, ph in enumerate(phases):
        if stop_after >= i:
            K.barrier()
            A.reset()
            ph(B)
    K.barrier()
    return nc


def evac_copy(K, which, out, in_, deps):
    if which == 0:
        return K.dve("tensor_copy", deps=deps, out=out, in_=in_)
    return K.act("copy", deps=deps, out=out, in_=in_)


def phase0(B):
    nc, K = B.nc, B.K
    pe, act, dve, pool, sp = K.pe, K.act, K.dve, K.pool, K.sp
    if True:
        tb = B.bump([B.REG_BIG]); tc = B.bump([B.REG_C]); th = B.bump([B.REG_H]); tl = B.bump([B.REG_L])
        modB = tb("modB", [17, 6 * D], F32)
        c_sb = tb("c_sb", [17, D], F32)
        c_si = tb("c_si", [17, D], F32)
        wa = [tc(f"wa{i}", [128, 16, 512], BF16) for i in range(2)]
        bada_bf = th("bada_bf", [1, 6 * D], BF16)
        scT = tl("scT", [128, 16, 17], BF16)
        vec_sb = tl("vec_sb", [96, 128], F32)
        l8 = tl("l8", [128, 8], F32)

        t_c = [K.dma(sp, B.ident[:], B.ident_d, "c0")]
        t_c.append(K.dma(sp, c_sb[:], B.c_all, "c0"))
        t_c.append(K.dma(sp, vec_sb[:], B.vecs, "c0"))
        t_c.append(K.dma(sp, B.masks[:], B.masks_d, "c0"))
        t_c.append(K.dma(sp, B.cflag[:], B.ctx_flag, "c0"))
        t_c = t_c[-1]
        t_b = K.dma(pool, bada_bf[:], B.b_ada, "c1")
        t_o1 = dve("memset", ap=B.ones_bf[:], constant=1.0)
        t_o2 = dve("memset", ap=B.ones_f[:], constant=1.0)
        dve("memset", ap=B.eps_t[:], constant=EPS)

        t_si = act("activation", deps=[t_c], out=c_si[:], in_=c_sb[:], func=AF.Silu)
        i, ps, fr = K.psum()
        for dc in range(16):
            t_tr = pe("transpose", deps=[t_si, t_c, fr], sig=(dc == 15), out=ps[:, dc * 17:(dc + 1) * 17],
                      in_=c_si[0:17, dc * 128:(dc + 1) * 128], identity=B.ident[0:17, 0:17])
        t_scT = dve("tensor_copy", deps=[t_tr], out=scT[:], in_=ps[:, 0:272].rearrange("p (a b) -> p a b", b=17))
        K.psum_release(i, t_scT)

        i, ps, fr = K.psum()
        t_tr = pe("transpose", deps=[t_c, fr], out=ps[:, 0:96], in_=vec_sb[0:96, :], identity=B.ident[0:96, 0:96])
        t_vec = dve("tensor_copy", deps=[t_tr], out=B.vecT[:], in_=ps[:, 0:96])
        K.psum_release(i, t_vec)

        wv = B.w_ada.rearrange("(c p) n -> p c n", p=128)
        mm_tok = [None, None]
        ev_toks = []
        for cb in range(24):
            buf = wa[cb % 2]
            t_w = K.dma(pool, buf[:], wv[:, :, cb * 512:(cb + 1) * 512], f"wa{cb % 2}", deps=[mm_tok[cb % 2]])
            i, ps, fr = K.psum()
            for dc in range(16):
                pe("matmul", deps=[t_w, fr, t_scT], sig=False, out=ps[0:17, :], lhsT=scT[:, dc, :], rhs=buf[:, dc, :],
                   start=(dc == 0), stop=False)
            t_mm = pe("matmul", deps=[t_b, t_o1], out=ps[0:17, :], lhsT=B.ones_bf[0:1, 0:17],
                      rhs=bada_bf[0:1, cb * 512:(cb + 1) * 512], start=False, stop=True)
            mm_tok[cb % 2] = t_mm
            t_ev = evac_copy(K, cb % 2, modB[:, cb * 512:(cb + 1) * 512], ps[0:17, :], [t_mm])
            K.psum_release(i, t_ev)
            ev_toks.append(t_ev)
        t_mt = []
        for g in range(4):
            i, ps, fr = K.psum()
            for k in range(24):
                j = g * 24 + k
                t_tr = pe("transpose", deps=[ev_toks, fr], sig=(k == 23), out=ps[:, k * 17:(k + 1) * 17],
                          in_=modB[0:17, j * 128:(j + 1) * 128], identity=B.ident[0:17, 0:17])
            t_e = dve("tensor_copy", deps=[t_tr], out=B.modT[:, g * 24:(g + 1) * 24, :],
                      in_=ps[:, 0:408].rearrange("p (a b) -> p a b", b=17))
            K.psum_release(i, t_e)
            t_mt.append(t_e)

        def bc(lo):
            return B.vecT[:, lo:lo + 16].unsqueeze(2).broadcast_to([128, 16, 17])
        dve("scalar_tensor_tensor", deps=[t_mt, t_vec], out=B.A1[:], in0=B.modT[:, 16:32, :], scalar=1.0, in1=bc(0),
            op0=ALU.add, op1=ALU.mult)
        dve("scalar_tensor_tensor", out=B.A2[:], in0=B.modT[:, 64:80, :], scalar=1.0, in1=bc(32),
            op0=ALU.add, op1=ALU.mult)
        dve("tensor_tensor", out=B.G1[:], in0=B.modT[:, 32:48, :], in1=bc(16), op=ALU.mult)
        dve("tensor_tensor", out=B.G2[:], in0=B.modT[:, 80:96, :], in1=bc(48), op=ALU.mult)
        t_l = dve("tensor_tensor", out=l8[:], in0=B.vecT[:, 72:80], in1=B.vecT[:, 80:88], op=ALU.subtract)
        t_lb = act("activation", deps=[t_l], out=B.lbT[:], in_=l8[:], func=AF.Sigmoid)
        dve("tensor_scalar", deps=[t_lb], out=B.omlT[:], in0=B.lbT[:], scalar1=-1.0, scalar2=1.0, op0=ALU.mult, op1=ALU.add)
        dve("tensor_scalar", deps=[t_lb], out=B.nomlT[:], in0=B.lbT[:], scalar1=1.0, scalar2=-1.0, op0=ALU.mult, op1=ALU.add)
        dve("tensor_copy", deps=[t_c], out=B.ident_bf[:], in_=B.ident[:])
        K.barrier()
        B.dbg("modT", B.modT[:])
        B.dbg("A1", B.A1[:])
        B.dbg("vecT", B.vecT[:])
        B.dbg("lbT", B.lbT[:])


def expand_mod(B, dst, src, deps=None):
    return B.K.dve("tensor_copy", deps=deps, out=dst[:].rearrange("p a (b t) -> p a b t", t=TS),
                   in_=src[:, :, 1:17].unsqueeze(3).broadcast_to([128, 16, NB, TS]))


def phase1(B):
    nc, K = B.nc, B.K
    pe, act, dve, pool, sp = K.pe, K.act, K.dve, K.pool, K.sp
    if True:
        tmp = B.bump([B.REG_BIG, B.REG_L])
        xt = [tmp(f"xt{i}", [128, D], F32) for i in range(2)]
        xs = [tmp(f"xs{i}", [128, D], F32) for i in range(2)]
        junk = tmp("junk", [128, D], BF16)
        ss = tmp("ss", [128, 17], F32)
        rt = tmp("rt", [128, 17], F32)
        rstd = tmp("rstd", [128, 17], F32)
        A1s = tmp("A1s", [128, 16, 128], F32)
        B1s = tmp("B1s", [128, 16, 128], F32)
        tmpS = [tmp(f"tmpS{i}", [128, 4, 128], F32) for i in range(2)]
        t_A1s = expand_mod(B, A1s, B.A1)
        t_B1s = expand_mod(B, B1s, B.modT[:, 0:16, :])
        xt_free = [None, None]
        xs_free = [None, None]
        tiles = [("m", t) for t in range(9)] + [("c", u) for u in range(8)]
        for k, (kind, t) in enumerate(tiles):
            p = k % 2
            src = B.x_main[t * 128:(t + 1) * 128, :] if kind == "m" else B.x_ctx[t * 128:(t + 1) * 128, :]
            t_x = K.dma(sp, xt[p][:], src, f"x{p}", deps=[xt_free[p]])
            CUT = 9
            if CUT == 0:
                xt_free[p] = t_x
                continue
            t_sq = act("activation", deps=[t_x], out=junk[:], in_=xt[p][:], func=AF.Square, accum_out=ss[:, k:k + 1])
            t_sr = act("activation", deps=[t_sq], out=rt[:, k:k + 1], in_=ss[:, k:k + 1], func=AF.Sqrt, scale=1.0 / D, bias=B.eps_t[:, 0:1])
            t_rs = dve("reciprocal", deps=[t_sr], out=rstd[:, k:k + 1], in_=rt[:, k:k + 1])
            t_xs = dve("tensor_scalar", deps=[t_rs, t_x, xs_free[p]], out=xs[p][:], in0=xt[p][:], scalar1=rstd[:, k:k + 1],
                       scalar2=None, op0=ALU.mult)
            xt_free[p] = [t_xs, t_sq]
            if CUT == 1:
                xs_free[p] = t_xs
                continue
            last_tr = None
            for g in range(4):
                i, ps, fr = K.psum()
                for j in range(4):
                    dc = g * 4 + j
                    t_tr = pe("transpose", deps=[t_xs, fr], sig=(j == 3), out=ps[:, j * 128:(j + 1) * 128],
                              in_=xs[p][:, dc * 128:(dc + 1) * 128], identity=B.ident[:])
                last_tr = t_tr
                evs = []
                if CUT == 2:
                    K.psum_release(i, t_tr)
                    continue
                if kind == "m" and t == 8:
                    ts_ = tmpS[g % 2]
                    t_1 = dve("tensor_tensor", deps=[t_tr, t_A1s], out=ts_[:], in0=ps[:].rearrange("p (a b) -> p a b", b=128),
                              in1=A1s[:, g * 4:(g + 1) * 4, :], op=ALU.mult)
                    t_2 = dve("tensor_tensor", deps=[t_1, t_B1s], out=B.hT[:, g * 4:(g + 1) * 4, t * 128:(t + 1) * 128], in0=ts_[:],
                              in1=B1s[:, g * 4:(g + 1) * 4, :], op=ALU.add)
                    evs = [t_1, t_2]
                else:
                    for j in range(4):
                        dc = g * 4 + j
                        dst = (B.hT if kind == "m" else B.hcT)[:, dc, t * 128:(t + 1) * 128]
                        if True:
                            evs.append(dve("tensor_scalar", deps=[t_tr], out=dst, in0=ps[:, j * 128:(j + 1) * 128],
                                           scalar1=B.A1[:, dc, 0:1], scalar2=B.modT[:, dc, 0:1], op0=ALU.mult, op1=ALU.add))
                        else:
                            evs.append(act("activation", deps=[t_tr], out=dst, in_=ps[:, j * 128:(j + 1) * 128], func=AF.Identity,
                                           scale=B.A1[:, dc, 0:1], bias=B.modT[:, dc, 0:1]))
                K.psum_release(i, evs)
            xs_free[p] = last_tr
        K.barrier()
        B.dbg("hT", B.hT[:], BF16)
        B.dbg("hcT", B.hcT[:], BF16)


def rms_rstd_a(B, ps_in, pskey, junk, st, c, n):
    A, K = B.A, B.K
    A.op(K.act, "activation", R=[pskey], W=["junk2", ("st", c)], out=junk, in_=ps_in, func=AF.Square,
         accum_out=st[:, c:c + 1])
    A.op(K.act, "activation", R=[("st", c)], W=[("st", c + 1)], out=st[:, c + 1:c + 2], in_=st[:, c:c + 1],
         func=AF.Sqrt, scale=1.0 / n, bias=B.eps_t[:, 0:1])
    A.op(K.dve, "reciprocal", R=[("st", c + 1)], W=[("st", c + 2)], out=st[:, c + 2:c + 3], in_=st[:, c + 1:c + 2])
    return st[:, c + 2:c + 3], ("st", c + 2)


def phase2(B):
    nc, K, A = B.nc, B.K, B.A
    pe, act, dve, pool, sp = K.pe, K.act, K.dve, K.pool, K.sp
    ps = K.ps
    tmp = B.bump([(OFF_OA, ARENA_BYTES)])
    wm = tmp("wm", [128, 16, 1088], BF16)
    gq_b = tmp("gq_b", [128, QR], F32)
    gkv_b = tmp("gkv_b", [128, KVR], F32)
    cs_t = tmp("cs_t", [128, 17, 64], F32)
    sn_t = tmp("sn_t", [128, 17, 64], F32)
    junk = tmp("junk2", [128, 512], BF16)
    st = tmp("st2", [128, 6 * 17], F32)
    ckv32 = [tmp(f"ckv32_{i}", [128, KVR], F32) for i in range(2)]
    qn32 = [tmp(f"qn32_{i}", [128, QR], F32) for i in range(2)]
    r1 = tmp("r1", [128, 64], F32)
    r2 = tmp("r2", [128, 64], F32)
    kpe32 = [tmp(f"kpe32_{i}", [128, 64], F32) for i in range(2)]
    wv = B.w_in.rearrange("(c p) n -> p c n", p=128)
    A.dma_group(sp, [(gq_b[:], B.g_q.partition_broadcast(128)), (gkv_b[:], B.g_kv.partition_broadcast(128)),
                     (cs_t[:], B.rope_cs.rearrange("(t p) r -> p t r", p=128)),
                     (sn_t[:], B.rope_sn.rearrange("(t p) r -> p t r", p=128))],
                R=[], W=["gq_b", "gkv_b", "cs_t", "sn_t"], stream="c0")
    for j, (c0, n) in enumerate([(0, 512), (512, 512), (1024, 64)]):
        for hh in range(2):
            A.dma(pool, wm[:, hh * 8:(hh + 1) * 8, c0:c0 + n], wv[:, hh * 8:(hh + 1) * 8, O_QD + c0:O_QD + c0 + n],
                  R=[], W=[("wm", j, hh)], stream=f"wm{j}{hh}")
    WM = [("wm", j, hh) for j in range(3) for hh in range(2)]
    A.dma(pool, B.kpeT[64:65, 0:NCX], B.ctx_bias, R=[], W=["kpeT_b"], stream="wmb")
    A.op(dve, "memset", W=["kpeT_b2"], ap=B.kpeT[64:65, NCX:], constant=0.0)
    A.op(dve, "memset", W=[("st", i) for i in range(6 * 17)], ap=st[:], constant=0.0)
    tiles = [("m", t) for t in range(9)] + [("c", u) for u in range(8)]
    for k, (kind, t) in enumerate(tiles):
        p = k % 2
        hsrc = B.hT if kind == "m" else B.hcT
        kcol = (NCX + t * 128) if kind == "m" else t * 128
        kidx = kcol // 128
        ti = t if kind == "m" else 9 + t
        cols = slice(t * 128, (t + 1) * 128)
        A.mm([f"ps{p}"], WM, [MM(ps[p][:, 0:512], hsrc[:, dc, cols], wm[:, dc, 512:1024], dc == 0, dc == 15) for dc in range(16)])
        A.mm(["ps4"], WM, [MM(ps[4][:, 0:64], hsrc[:, dc, cols], wm[:, dc, 1024:1088], dc == 0, dc == 15) for dc in range(16)])
        if kind == "m":
            A.mm([f"ps{2 + p}"], WM, [MM(ps[2 + p][:, 0:512], hsrc[:, dc, cols], wm[:, dc, 0:512], dc == 0, dc == 15) for dc in range(16)])
        rs, rk = rms_rstd_a(B, ps[p][:, 0:512], f"ps{p}", junk[:], st, 3 * k, KVR)
        A.op(dve, "scalar_tensor_tensor", R=[f"ps{p}", rk, "gkv_b"], W=[f"ckv32_{p}"], out=ckv32[p][:], in0=ps[p][:, 0:512],
             scalar=rs, in1=gkv_b[:], op0=ALU.mult, op1=ALU.mult)
        if kind == "m":
            A.dma(sp, B.ckv_main[cols, :], ckv32[p][:], R=[f"ckv32_{p}"], W=[], stream=f"ock{p}")
        A.op(pool, "tensor_copy", R=[f"ckv32_{p}"], W=[("ckv_tm", kidx)], out=B.ckv_tm[:, kidx, :], in_=ckv32[p][:])
        A.mm(["ps5"], [f"ckv32_{p}"], [TR(ps[5][:, cc * 128:(cc + 1) * 128], ckv32[p][:, cc * 128:(cc + 1) * 128], B.ident[:])
                                      for cc in range(4)])
        A.op(act, "copy", R=["ps5"], W=[("ckvT", kidx)], out=B.ckvT[:, :, kcol:kcol + 128],
             in_=ps[5][:].rearrange("p (a b) -> p a b", b=128))
        A.op(dve, "tensor_tensor", R=["ps4", "cs_t"], W=["r1"], out=r1[:], in0=ps[4][:, 0:64], in1=cs_t[:, ti, :], op=ALU.mult)
        A.op(dve, "tensor_tensor", R=["ps4", "sn_t"], W=["r2a"], out=r2[:, 0:32], in0=ps[4][:, 32:64], in1=sn_t[:, ti, 0:32], op=ALU.mult)
        A.op(dve, "tensor_tensor", R=["ps4", "sn_t"], W=["r2b"], out=r2[:, 32:64], in0=ps[4][:, 0:32], in1=sn_t[:, ti, 32:64], op=ALU.mult)
        A.op(dve, "tensor_tensor", R=["r1", "r2a", "r2b"], W=[f"kpe32_{p}"], out=kpe32[p][:], in0=r1[:], in1=r2[:], op=ALU.add)
        if kind == "m":
            A.dma(sp, B.kpe_main[cols, :], kpe32[p][:], R=[f"kpe32_{p}"], W=[], stream=f"okp{p}")
        A.mm(["ps7"], [f"kpe32_{p}"], [TR(ps[7][0:64, 0:128], kpe32[p][:, 0:64], B.ident[:])])
        A.op(act, "copy", R=["ps7"], W=[("kpeT", kidx)], out=B.kpeT[0:64, kcol:kcol + 128], in_=ps[7][0:64, 0:128])
        if kind == "m":
            rs, rk = rms_rstd_a(B, ps[2 + p][:, 0:512], f"ps{2 + p}", junk[:], st, 51 + 3 * k, QR)
            A.op(dve, "scalar_tensor_tensor", R=[f"ps{2 + p}", rk, "gq_b"], W=[f"qn32_{p}"], out=qn32[p][:], in0=ps[2 + p][:, 0:512],
                 scalar=rs, in1=gq_b[:], op0=ALU.mult, op1=ALU.mult)
            A.mm(["ps6"], [f"qn32_{p}"], [TR(ps[6][:, cc * 128:(cc + 1) * 128], qn32[p][:, cc * 128:(cc + 1) * 128], B.ident[:])
                                         for cc in range(4)])
            A.op(act, "copy", R=["ps6"], W=[("qnT", t)], out=B.qnT[:, :, cols], in_=ps[6][:].rearrange("p (a b) -> p a b", b=128))
    if B.debug and B.stop_after == 2:
        B.dbg("ckvT", B.ckvT[:], BF16)
        B.dbg("kpeT", B.kpeT[:], BF16)
        B.dbg("qnT", B.qnT[:], BF16)


def phase3(B):
    nc, K, A = B.nc, B.K, B.A
    pe, act, dve, pool, sp = K.pe, K.act, K.dve, K.pool, K.sp
    ps = K.ps
    tmp = B.bump([(OFF_OB, ARENA_BYTES), (OFF_C, OFF_LAT)])
    Sctx = tmp("Sctx", [128, 8, 128], F32)
    wf2 = [tmp(f"wf{i}", [128, 16, 128], BF16) for i in range(2)]
    wi2 = [tmp(f"wi{i}", [128, 16, 128], BF16) for i in range(2)]
    sig = tmp("sig", [128, 512], F32)
    logf = tmp("logf", [128, 512], F32)
    bT = tmp("bT", [128, 512], F32)
    kT = tmp("kT", [128, 512], F32)
    e1 = tmp("e1", [128, 512], F32)
    ke32 = tmp("ke32", [128, 512], F32)
    kdT = tmp("kdT", [128, 512], F32)
    kd_tm = tmp("kd_tm", [128, 4, 128], BF16)
    v_tm = tmp("v_tm", [128, 4, 128], BF16)
    ebl = tmp("ebl", [128, 16], F32)
    S32 = tmp("S32", [128, 128], F32)
    S_bf = tmp("S_bf", [128, 128], BF16)
    cm64 = tmp("cm64", [128, 512], F32)
    cm8 = tmp("cm8", [128, 128], F32)
    wq2 = [tmp(f"wq{i}", [128, 16, 128], BF16) for i in range(2)]
    wg2 = [tmp(f"wg{i}", [128, 16, 128], BF16) for i in range(2)]
    qT = tmp("qT", [128, 512], F32)
    qeT = tmp("qeT", [128, 512], BF16)
    keT = tmp("keT", [128, 512], BF16)
    AT = [tmp(f"AT{i}", [128, 128], BF16) for i in range(2)]
    oT = tmp("oT", [128, 512], F32)
    sq = tmp("sq", [128, 512], BF16)
    rstd = tmp("rstd", [128, 512], F32)
    sg = tmp("sg", [128, 512], F32)
    t1 = tmp("t1", [128, 512], F32)
    Sin32 = tmp("Sin32", [128, 16, 128], F32)
    Sin_bf = tmp("Sin_bf", [128, 16, 128], BF16)
    Sout32 = [tmp(f"Sout32_{i}", [128, 128], F32) for i in range(2)]
    kdm = [tmp(f"kdm{i}", [128, 128], BF16) for i in range(2)]
    wv = B.w_in.rearrange("(c p) n -> p c n", p=128)
    M64 = B.masks[:, 0:128]
    M8 = B.masks[:, 128:256]
    IND = B.masks[:, 760:776]
    ghT = B.vecT[:, 88:96]

    A.op(dve, "memset", W=["cm64"], ap=cm64[:], constant=1.0)
    A.op(dve, "memset", W=["cm64"], ap=cm64[:].rearrange("p (c t) -> p c t", t=64)[:, :, 0:1], constant=0.0)
    A.op(dve, "memset", W=["cm8"], ap=cm8[:], constant=1.0)
    A.op(dve, "memset", W=["cm8"], ap=cm8[:].rearrange("p (c t) -> p c t", t=8)[:, :, 0:1], constant=0.0)

    def ldw(dst, key, col0):
        for hh in range(2):
            A.dma(pool, dst[:, hh * 8:(hh + 1) * 8, :], wv[:, hh * 8:(hh + 1) * 8, col0:col0 + 128], R=[], W=[(key, hh)],
                  stream=f"{key}{hh}")
        return [(key, 0), (key, 1)]

    def proj_T(bank, w, wk, hsrc, c0, n):
        A.mm([f"ps{bank}"], wk, [MM(ps[bank][:, 0:n], w[:, dc, :], hsrc[:, dc, c0:c0 + n], dc == 0, dc == 15) for dc in range(16)])

    for is_ctx in (True, False):
        hsrc = B.hcT if is_ctx else B.hT
        blks = CBLKS if is_ctx else BLKS
        def ldhead(hh_):
            s_ = hh_ % 2
            ks = [ldw(wf2[s_], f"wf{s_}", O_FA + hh_ * 128), ldw(wi2[s_], f"wi{s_}", O_IA + hh_ * 128)]
            if not is_ctx:
                ks += [ldw(wq2[s_], f"wq{s_}", O_QA + hh_ * 128), ldw(wg2[s_], f"wg{s_}", O_GA + hh_ * 128)]
            return ks
        nxt = ldhead(0)
        for h in range(NH):
            cur = nxt
            if h + 1 < NH:
                nxt = ldhead(h + 1)
            wf, wi = wf2[h % 2], wi2[h % 2]
            wfk, wik = cur[0], cur[1]
            if not is_ctx:
                wq, wg = wq2[h % 2], wg2[h % 2]
                wqk, wgk = cur[2], cur[3]
            if is_ctx:
                A.op(dve, "memset", W=["S32"], ap=S32[:], constant=0.0)
            else:
                A.op(dve, "tensor_copy", R=[("Sctx", h)], W=["S32"], out=S32[:], in_=Sctx[:, h, :])
                A.op(act, "copy", R=["S32"], W=["S_bf"], out=S_bf[:], in_=S32[:])
            for (c0, n) in blks:
                sample = (c0 == NTP) and not is_ctx
                tiles = n // 128
                csz = 8 if sample else 64
                nch = n // csz
                cm = cm8 if sample else cm64
                proj_T(0, wf, wfk, hsrc, c0, n)
                A.op(act, "activation", R=["ps0"], W=["sig"], out=sig[:, 0:n], in_=ps[0][:, 0:n], func=AF.Sigmoid)
                A.op(dve, "tensor_scalar", R=["sig"], W=["kT"], out=kT[:, 0:n], in0=sig[:, 0:n], scalar1=B.omlT[:, h:h + 1],
                     scalar2=B.lbT[:, h:h + 1], op0=ALU.mult, op1=ALU.add)
                A.op(act, "activation", R=["kT"], W=["logf"], out=logf[:, 0:n], in_=kT[:, 0:n], func=AF.Ln)
                A.op(dve, "tensor_scalar", R=["sig", "logf"], W=["kT"], out=kT[:, 0:n], in0=sig[:, 0:n], scalar1=B.nomlT[:, h:h + 1],
                     scalar2=B.omlT[:, h:h + 1], op0=ALU.mult, op1=ALU.add)
                A.op(dve, "tensor_tensor_scan", R=["logf", "cm64", "cm8"], W=["bT"], out=bT[:, 0:n], data0=cm[:, 0:n],
                     data1=logf[:, 0:n], initial=0.0, op0=ALU.mult, op1=ALU.add)
                A.op(act, "activation", R=["bT"], W=["ebl"], out=ebl[:, 0:nch],
                     in_=bT[:, 0:n].rearrange("p (c t) -> p c t", t=csz)[:, :, csz - 1], func=AF.Exp)
                A.op(act, "activation", R=["bT"], W=["e1"], out=e1[:, 0:n], in_=bT[:, 0:n], func=AF.Exp, scale=-1.0)
                A.op(dve, "tensor_tensor", R=["kT", "e1"], W=["ke32"], out=ke32[:, 0:n], in0=kT[:, 0:n], in1=e1[:, 0:n], op=ALU.mult)
                A.op(dve, "tensor_tensor", R=["ke32", "ebl"], W=["kdT"], out=kdT[:, 0:n].rearrange("p (c t) -> p c t", t=csz),
                     in0=ke32[:, 0:n].rearrange("p (c t) -> p c t", t=csz),
                     in1=ebl[:, 0:nch].unsqueeze(2).broadcast_to([128, nch, csz]), op=ALU.mult)
                if not is_ctx:
                    A.op(pool, "tensor_copy", R=["ke32"], W=["keT"], out=keT[:, 0:n], in_=ke32[:, 0:n])
                    proj_T(1, wq, wqk, hsrc, c0, n)
                    A.op(act, "activation", R=["ps1"], W=["qT"], out=qT[:, 0:n], in_=ps[1][:, 0:n], func=AF.Silu)
                    A.op(act, "activation", R=["bT"], W=["e1"], out=e1[:, 0:n], in_=bT[:, 0:n], func=AF.Exp)
                    A.op(dve, "tensor_tensor", R=["qT", "e1"], W=["qeT"], out=qeT[:, 0:n], in0=qT[:, 0:n], in1=e1[:, 0:n], op=ALU.mult)
                    proj_T(2, wg, wgk, hsrc, c0, n)
                    A.op(act, "activation", R=["ps2"], W=["sg"], out=sg[:, 0:n], in_=ps[2][:, 0:n], func=AF.Sigmoid)
                for t in range(tiles):
                    A.mm(["ps3"], wik, [MM(ps[3][:, t * 128:(t + 1) * 128], hsrc[:, dc, c0 + t * 128:c0 + (t + 1) * 128], wi[:, dc, :],
                                           dc == 0, dc == 15) for dc in range(16)])
                A.op(act, "copy", R=["ps3"], W=["v_tm"], out=v_tm[:, 0:tiles, :], in_=ps[3][:, 0:n].rearrange("p (a b) -> p a b", b=128))
                A.mm(["ps4"], ["kdT"], [TR(ps[4][:, t * 128:(t + 1) * 128], kdT[:, t * 128:(t + 1) * 128], B.ident[:]) for t in range(tiles)])
                A.op(dve, "tensor_copy", R=["ps4"], W=["kd_tm"], out=kd_tm[:, 0:tiles, :], in_=ps[4][:, 0:n].rearrange("p (a b) -> p a b", b=128))
                if not sample:
                    for t in range(tiles):
                        tc_ = slice(t * 128, (t + 1) * 128)
                        if not is_ctx:
                            A.mm(["ps5"], ["keT", "qeT"], [MM(ps[5][:, 0:128], keT[:, tc_], qeT[:, tc_], True, True)])
                            A.op(dve, "tensor_tensor", R=["ps5"], W=[f"AT{t % 2}"], out=AT[t % 2][:], in0=ps[5][:, 0:128], in1=M64, op=ALU.mult)
                        for j in range(2):
                            ch = t * 2 + j
                            cc_ = slice(t * 128 + j * 64, t * 128 + (j + 1) * 64)
                            if not is_ctx:
                                A.mm(["ps6"], ["S_bf", "qeT"], [MM(ps[6][:, cc_], S_bf[:], qeT[:, cc_], j == 0, False)])
                            A.mm(["ps7"], ["kd_tm", "v_tm"], [MM(ps[7][:, 0:128], kd_tm[j * 64:(j + 1) * 64, t, :], v_tm[j * 64:(j + 1) * 64, t, :],
                                                                True, True)])
                            A.op(dve, "scalar_tensor_tensor", R=["ps7", "ebl", "S32"], W=["S32"], out=S32[:], in0=S32[:], scalar=ebl[:, ch:ch + 1],
                                 in1=ps[7][:, 0:128], op0=ALU.mult, op1=ALU.add)
                            if not is_ctx:
                                A.op(act, "copy", R=["S32"], W=["S_bf"], out=S_bf[:], in_=S32[:])
                        if not is_ctx:
                            A.mm(["ps6"], ["v_tm", f"AT{t % 2}"], [MM(ps[6][:, tc_], v_tm[:, t, :], AT[t % 2][:], False, True)])
                    if not is_ctx and c0 + n == NTP:
                        A.dma(sp, B.s_prompt[h], S32[:], R=["S32"], W=[], stream="osp")
                else:
                    A.dma(sp, Sin32[:], B.state_in[:, h].rearrange("b k v -> k b v"), R=[], W=["Sin32"], stream="sin")
                    A.op(pool, "tensor_copy", R=["Sin32"], W=["Sin_bf"], out=Sin_bf[:], in_=Sin32[:])
                    A.mm(["ps5"], ["keT", "qeT"], [MM(ps[5][:, 0:128], keT[:, 0:128], qeT[:, 0:128], True, True)])
                    A.op(dve, "tensor_tensor", R=["ps5"], W=["AT0"], out=AT[0][:], in0=ps[5][:, 0:128], in1=M8, op=ALU.mult)
                    A.mm(["ps6"], ["Sin_bf", "qeT"], [MM(ps[6][:, b * 8:(b + 1) * 8], Sin_bf[:, b, :], qeT[:, b * 8:(b + 1) * 8], b == 0, False)
                                                      for b in range(NB)])
                    A.mm(["ps6"], ["v_tm", "AT0"], [MM(ps[6][:, 0:128], v_tm[:, 0, :], AT[0][:], False, True)])
                    for b in range(NB):
                        pb = 7 if b % 2 == 0 else 4
                        A.op(dve, "tensor_scalar", R=["kd_tm"], W=[f"kdm{b % 2}"], out=kdm[b % 2][:], in0=kd_tm[:, 0, :], scalar1=IND[:, b:b + 1],
                             scalar2=None, op0=ALU.mult)
                        A.mm([f"ps{pb}"], [f"kdm{b % 2}", "v_tm"], [MM(ps[pb][:, 0:128], kdm[b % 2][:], v_tm[:, 0, :], True, True)])
                        A.op(dve, "scalar_tensor_tensor", R=[f"ps{pb}", "ebl", "Sin32"], W=[f"Sout32_{b % 2}"], out=Sout32[b % 2][:], in0=Sin32[:, b, :],
                             scalar=ebl[:, b:b + 1], in1=ps[pb][:, 0:128], op0=ALU.mult, op1=ALU.add)
                        A.dma(sp, B.s_sample[b, h], Sout32[b % 2][:], R=[f"Sout32_{b % 2}"], W=[], stream=f"oss{b % 2}")
                if not is_ctx:
                    A.op(act, "copy", R=["ps6"], W=["oT"], out=oT[:, 0:n], in_=ps[6][:, 0:n])
                    A.op(pool, "tensor_tensor", R=["oT"], W=["sq"], out=sq[:, 0:n], in0=oT[:, 0:n], in1=oT[:, 0:n], op=ALU.mult)
                    A.mm(["ps0"], ["sq"], [MM(ps[0][:, 0:n], B.ones_bf[:], sq[:, 0:n], True, True)])
                    A.op(act, "activation", R=["ps0"], W=["rstd"], out=rstd[:, 0:n], in_=ps[0][:, 0:n], func=AF.Sqrt, scale=1.0 / DV,
                         bias=B.eps_t[:, 0:1])
                    A.op(dve, "reciprocal", R=["rstd"], W=["rstd"], out=rstd[:, 0:n], in_=rstd[:, 0:n])
                    A.op(dve, "scalar_tensor_tensor", R=["oT", "rstd"], W=["t1"], out=t1[:, 0:n], in0=oT[:, 0:n], scalar=ghT[:, h:h + 1],
                         in1=rstd[:, 0:n], op0=ALU.mult, op1=ALU.mult)
                    A.op(dve, "tensor_tensor", R=["t1", "sg"], W=[("oaT", h, c0)], out=B.oaT[:, h, c0:c0 + n], in0=t1[:, 0:n], in1=sg[:, 0:n],
                         op=ALU.mult)
            if is_ctx:
                A.op(dve, "tensor_scalar", R=["S32"], W=[("Sctx", h)], out=Sctx[:, h, :], in0=S32[:], scalar1=B.cflag[:, 0:1], scalar2=None,
                     op0=ALU.mult)
    if B.debug and B.stop_after == 3:
        B.dbg("oaT", B.oaT[:], BF16)


def phase4(B):
    nc, K, A = B.nc, B.K, B.A
    pe, act, dve, pool, sp = K.pe, K.act, K.dve, K.pool, K.sp
    ps = K.ps
    psb = [p[:].bitcast(BF16) for p in ps]
    regions = [(OFF_C, OFF_LAT), (OFF_T, ARENA_BYTES)]
    tmp = B.bump(regions)
    wkv = tmp("wkv", [128, 4, 2048], BF16)
    qsT = tmp("qsT", [128, 5, NB, 64], BF16)
    OFF_4B = OFF_C + 2 * 4 * 2048 + 2 * 5 * NB * 64
    r1 = tmp("r1q", [64, 512], F32)
    r2 = tmp("r2q", [64, 512], F32)
    wqh = tmp("wqh", [128, 4, 192], BF16)
    tmp = B.bump([(OFF_T, ARENA_BYTES)])
    w_ukT = tmp("w_ukT", [128, 8, 512], BF16)
    q_nopeT = tmp("q_nopeT", [128, NT], BF16)
    q_peT = tmp("q_peT", [128, NT], BF16)
    csT = tmp("csT", [64, NT], F32)
    snT = tmp("snT", [64, NT], F32)
    q_latT = tmp("q_latT", [128, 4, 512], BF16)
    pT = [tmp(f"pT{i}", [128, 512], BF16) for i in range(2)]
    olat_sb = tmp("olat_sb", [128, 4, 512], BF16)
    rden = tmp("rden", [128, 512], F32)
    MC = B.masks[:, 256:384]
    LAT = ["lat"]

    wkv_v = B.w_kv_up.rearrange("(c p) n -> p c n", p=128)
    for cc in range(4):
        for q4 in range(4):
            A.dma(pool, wkv[:, cc, q4 * 512:(q4 + 1) * 512], wkv_v[:, cc, q4 * 512:(q4 + 1) * 512], R=[], W=[("wkv", cc, q4)], stream=f"wkv{q4}")
    WKV = [("wkv", cc, q4) for cc in range(4) for q4 in range(4)]
    A.dma_group(sp, [(csT[:], B.ropeT_cs), (snT[:], B.ropeT_sn)], R=[], W=["csT", "snT"], stream="c0")
    A.op(dve, "memset", W=["q_peT1"], ap=q_peT[64:65, :], constant=1.0)
    for h in range(NH):
        A.mm(["ps7"], WKV, [TR(psb[7][:, cc * 128:(cc + 1) * 128], wkv[:, cc, h * 256:h * 256 + 128], B.ident_bf[:]) for cc in range(4)])
        A.op(act if h % 2 else dve, "copy" if h % 2 else "tensor_copy", R=["ps7"], W=[("w_ukT", h)], out=w_ukT[:, h, :], in_=psb[7][:, 0:512])
    wq_v = B.w_q_up.rearrange("(c p) n -> p c n", p=128)
    for h in range(NH):
        A.dma(pool, wqh[:], wq_v[:, :, h * 192:(h + 1) * 192], R=[], W=["wqh"], stream="wqh")
        for bi, (c0, n) in enumerate(BLKS):
            cols = slice(c0, c0 + n)
            A.mm(["ps7"], ["wqh"], [MM(ps[7][:, 0:n], wqh[:, cc, 0:128], B.qnT[:, cc, cols], cc == 0, cc == 3) for cc in range(4)])
            A.op(act, "copy", R=["ps7"], W=[("q_nopeT", bi)], out=q_nopeT[:, cols], in_=ps[7][:, 0:n])
            A.mm(["ps5"], ["wqh"], [MM(ps[5][0:64, 0:n], wqh[:, cc, 128:192], B.qnT[:, cc, cols], cc == 0, cc == 3) for cc in range(4)])
            A.mm(["ps6"], ["wqh"], [MM(ps[6][0:32, 0:n], wqh[:, cc, 160:192], B.qnT[:, cc, cols], cc == 0, cc == 3) for cc in range(4)]
                 + [MM(ps[6][32:64, 0:n], wqh[:, cc, 128:160], B.qnT[:, cc, cols], cc == 0, cc == 3) for cc in range(4)])
            A.op(dve, "tensor_tensor", R=["ps5", "csT"], W=["r1q"], out=r1[:, 0:n], in0=ps[5][0:64, 0:n], in1=csT[:, cols], op=ALU.mult)
            A.op(dve, "tensor_tensor", R=["ps6", "snT"], W=["r2q"], out=r2[:, 0:n], in0=ps[6][0:64, 0:n], in1=snT[:, cols], op=ALU.mult)
            A.op(dve, "tensor_tensor", R=["r1q", "r2q"], W=[("q_peT", bi)], out=q_peT[0:64, cols], in0=r1[:, 0:n], in1=r2[:, 0:n], op=ALU.add)
        for cc in range(4):
            A.mm(["ps7"], [("w_ukT", h), ("q_nopeT", 2)], [MM(ps[7][:, 0:128], w_ukT[:, h, cc * 128:(cc + 1) * 128], q_nopeT[:, NTP:NT], True, True)])
            A.op(act if cc % 2 else dve, "copy" if cc % 2 else "tensor_copy", R=["ps7"], W=[("qsT", h)], out=qsT[:, cc, :, h * 8:(h + 1) * 8],
                 in_=ps[7][:, 0:128].rearrange("p (b t) -> p b t", t=TS))
        A.op(dve, "tensor_copy", R=[("q_peT", 2)], W=[("qsT", h)], out=qsT[0:64, 4, :, h * 8:(h + 1) * 8],
             in_=q_peT[0:64, NTP:NT].rearrange("p (b t) -> p b t", t=TS))
        for qb in range(2):
            c0 = qb * 512
            for cc in range(4):
                A.mm(["ps7"], [("w_ukT", h), ("q_nopeT", qb)], [MM(ps[7][:, 0:512], w_ukT[:, h, cc * 128:(cc + 1) * 128], q_nopeT[:, c0:c0 + 512], True, True)])
                A.op(act if cc % 2 else dve, "copy" if cc % 2 else "tensor_copy", R=["ps7"], W=["q_latT"], out=q_latT[:, cc, :], in_=ps[7][:, 0:512])
            keys = [(u, 0, False) for u in range(8)]
            for j in range(4 * qb + 4):
                keys.append((8 + j, 128 * max(0, j - 4 * qb), j >= 4 * qb))
            def st_score(i):
                kidx, cs, diag = keys[i]
                sbk = 5 + i % 2
                kc = slice(kidx * 128, (kidx + 1) * 128)
                A.mm([f"ps{sbk}"], LAT + ["q_latT", ("q_peT", qb), "q_peT1"],
                     [MM(ps[sbk][:, cs:512], B.ckvT[:, cc, kc], q_latT[:, cc, cs:512], cc == 0, False) for cc in range(4)]
                     + [MM(ps[sbk][:, cs:512], B.kpeT[0:65, kc], q_peT[0:65, c0 + cs:c0 + 512], False, True)])
                A.op(act, "activation", R=[f"ps{sbk}"], W=[f"pT{i % 2}"], out=pT[i % 2][:, cs:512], in_=ps[sbk][:, cs:512], func=AF.Exp, scale=SCALE)
                if diag:
                    A.op(pool, "tensor_tensor", R=[f"pT{i % 2}"], W=[f"pT{i % 2}"], out=pT[i % 2][:, cs:cs + 128], in0=pT[i % 2][:, cs:cs + 128],
                         in1=MC, op=ALU.mult)

            def st_pv(i):
                kidx, cs, diag = keys[i]
                last = i == len(keys) - 1
                for cc in range(4):
                    A.mm([f"ps{cc}"], LAT + [f"pT{i % 2}"], [MM(ps[cc][:, cs:512], B.ckv_tm[:, kidx, cc * 128:(cc + 1) * 128], pT[i % 2][:, cs:512], i == 0, last)])
                A.mm(["ps4"], [f"pT{i % 2}"], [MM(ps[4][:, cs:512], B.ones_bf[:], pT[i % 2][:, cs:512], i == 0, last)])

            for i in range(-1, len(keys)):
                if i + 1 < len(keys):
                    st_score(i + 1)
                if i >= 0:
                    st_pv(i)
            for cc in range(4):
                A.op(act if cc % 2 else dve, "copy" if cc % 2 else "tensor_copy", R=[f"ps{cc}"], W=["olat_sb"], out=olat_sb[:, cc, :], in_=ps[cc][:, 0:512])
            A.op(dve, "reciprocal", R=["ps4"], W=["rden"], out=rden[:], in_=ps[4][:, 0:512])
            A.mm(["ps7"], ["olat_sb"] + WKV, [MM(ps[7][:, 0:512], wkv[:, cc, h * 256 + 128:h * 256 + 256], olat_sb[:, cc, :], cc == 0, cc == 3) for cc in range(4)])
            A.op(dve, "tensor_tensor", R=["ps7", "rden"], W=[("obT", h, qb)], out=B.obT[:, h, c0:c0 + 512], in0=ps[7][:, 0:512], in1=rden[:], op=ALU.mult)

    K.barrier()
    A.reset()
    tmp = B.bump([(OFF_4B, OFF_LAT), (OFF_T, ARENA_BYTES)])
    idx_i = tmp("idx_i", [128, NB * NPAGES], I32)
    ptab_i = tmp("ptab_i", [128, NB * NPAGES], I32)
    ptab_f = tmp("ptab_f", [128, NB * NPAGES], F32)
    smask = tmp("smask", [128, NB * 64], F32)
    NSLOT = 4
    pg32 = [tmp(f"pg32_{i}", [128, 576], F32) for i in range(NSLOT)]
    pg_bf = [tmp(f"pg_bf{i}", [128, 576], BF16) for i in range(NSLOT)]
    KT_pg = [tmp(f"KT_pg{i}", [128, 640], BF16) for i in range(2)]
    pTs = [tmp(f"pTs{i}", [128, 64], BF16) for i in range(2)]
    olat_s = tmp("olat_s", [64, 512], BF16)
    olatT_s = tmp("olatT_s", [128, 256], BF16)
    rden_s = tmp("rden_s", [64, 1], F32)
    PIOTA = B.masks[:, 800:801]
    A.dma_group(sp, [(ptab_i[:], B.ptab.partition_broadcast(128)), (smask[:], B.smask_d)], R=[], W=["ptab_i", "smask"], stream="c0")
    A.op(dve, "tensor_copy", R=["ptab_i"], W=["ptab_f"], out=ptab_f[:], in_=ptab_i[:])
    A.op(dve, "tensor_scalar", R=["ptab_f"], W=["idx_i"], out=idx_i[:], in0=ptab_f[:], scalar1=128.0, scalar2=PIOTA, op0=ALU.mult, op1=ALU.add)
    pages = [(b, j) for b in range(NB) for j in range(NPAGES + 1)]
    NPG = len(pages)

    def stA(n):
        b, j = pages[n]
        if j == NPAGES:
            return
        sl = n % NSLOT
        col = b * NPAGES + j
        A.dma(pool, pg32[sl][:], B.cache_all, R=["idx_i"], W=[("pg32", sl)], stream=f"pgc{sl}", indirect=idx_i[:, col:col + 1])

    def stB(n):
        b, j = pages[n]
        if j == NPAGES:
            return
        sl = n % NSLOT
        A.op(dve, "tensor_copy", R=[("pg32", sl)], W=[("pg_bf", sl)], out=pg_bf[sl][:], in_=pg32[sl][:])

    def stC(n):
        b, j = pages[n]
        if j == NPAGES:
            return
        sl, s2 = n % NSLOT, n % 2
        tb = 2 + s2
        A.mm([f"ps{tb}"], [("pg_bf", sl)], [TR(psb[tb][:, i * 128:(i + 1) * 128], pg_bf[sl][:, i * 128:(i + 1) * 128], B.ident_bf[:]) for i in range(4)]
             + [TR(psb[tb][0:64, 512:640], pg_bf[sl][:, 512:576], B.ident_bf[:])])
        A.op(act, "copy", R=[f"ps{tb}"], W=[("KT", s2, 0)], out=KT_pg[s2][:, 0:512], in_=psb[tb][:, 0:512])
        A.op(act, "copy", R=[f"ps{tb}"], W=[("KT", s2, 1)], out=KT_pg[s2][0:64, 512:640], in_=psb[tb][0:64, 512:640])

    def stD(n):
        b, j = pages[n]
        s2 = n % 2
        qv = [qsT[:, i, b, :] for i in range(4)]
        qp = qsT[0:64, 4, b, :]
        if j < NPAGES:
            kts = [KT_pg[s2][:, i * 128:(i + 1) * 128] for i in range(4)]
            ktp = KT_pg[s2][0:64, 512:640]
            Rk = [("KT", s2, 0), ("KT", s2, 1)]
        else:
            kts = [B.ckvT[:, i, NCX + NTP:NKEY] for i in range(4)]
            ktp = B.kpeT[0:64, NCX + NTP:NKEY]
            Rk = []
        sbk = 4 + s2
        A.mm([f"ps{sbk}"], Rk, [MM(ps[sbk][:, 0:64], kts[i], qv[i], i == 0, False) for i in range(4)] + [MM(ps[sbk][:, 0:64], ktp, qp, False, True)])
        A.op(act, "activation", R=[f"ps{sbk}"], W=[f"pTs{s2}"], out=pTs[s2][:], in_=ps[sbk][:, 0:64], func=AF.Exp, scale=SCALE)
        if j == NPAGES:
            A.op(dve, "tensor_tensor", R=[f"pTs{s2}", "smask"], W=[f"pTs{s2}"], out=pTs[s2][:], in0=pTs[s2][:], in1=smask[:, b * 64:(b + 1) * 64], op=ALU.mult)

    def stE(n):
        b, j = pages[n]
        sl, s2 = n % NSLOT, n % 2
        if j < NPAGES:
            V = pg_bf[sl][:, 0:512]
            Rv = [("pg_bf", sl)]
        else:
            V = B.ckv_tm[:, 16, :]
            Rv = []
        A.mm(["ps0"], Rv + [f"pTs{s2}"], [MM(ps[0][0:64, 0:512], pTs[s2][:], V, j == 0, j == NPAGES)])
        A.mm(["ps1"], [f"pTs{s2}"], [MM(ps[1][0:64, 0:1], pTs[s2][:], B.ones_bf[:, 0:1], j == 0, j == NPAGES)])
        if j < NPAGES:
            return
        A.op(dve, "reciprocal", R=["ps1"], W=["rden_s"], out=rden_s[:], in_=ps[1][0:64, 0:1])
        A.op(dve, "tensor_scalar", R=["ps0", "rden_s"], W=["olat_s"], out=olat_s[:], in0=ps[0][0:64, 0:512], scalar1=rden_s[:, 0:1], scalar2=None, op0=ALU.mult)
        A.mm(["ps6"], ["olat_s"], [TR(psb[6][:, cc * 64:(cc + 1) * 64], olat_s[0:64, cc * 128:(cc + 1) * 128], B.ident_bf[0:64, 0:64]) for cc in range(4)])
        A.op(act, "copy", R=["ps6"], W=["olatT_s"], out=olatT_s[:], in_=psb[6][:, 0:256])
        A.mm(["ps7"], ["olatT_s"], [MM(ps[7][:, h * 8:(h + 1) * 8], wkv[:, cc, h * 256 + 128:h * 256 + 256], olatT_s[:, cc * 64 + h * 8:cc * 64 + (h + 1) * 8],
                                       (h == 0 and cc == 0), (h == NH - 1 and cc == 3)) for h in range(NH) for cc in range(4)])
        A.op(dve, "tensor_copy", R=["ps7"], W=[("obT_s", b)], out=B.obT[:, :, NTP + b * 8:NTP + (b + 1) * 8],
             in_=ps[7][:, 0:64].rearrange("p (h t) -> p h t", t=TS))

    for i in range(-6, NPG):
        for (st_, off) in ((stA, 6), (stB, 3), (stC, 2), (stD, 1), (stE, 0)):
            if 0 <= i + off < NPG:
                st_(i + off)
    if B.debug and B.stop_after == 4:
        B.dbg("obT", B.obT[:], BF16)


OFF_M = OFF_LAT
OFF_H2 = OFF_LAT + 36864


def build_G_tm(B, G_fm, GB, G_tm, sel, sel_lo, tag):
    K, A = B.K, B.A
    ps = K.ps
    for cb in range(4):
        A.mm([f"ps{cb}"], [tag + "GB", "sel"], [MM(ps[cb][:, 0:512], sel[0:17, sel_lo:sel_lo + 128], GB[0:17, cb * 512:(cb + 1) * 512], True, True)])
        A.op(K.act if cb % 2 else K.dve, "copy" if cb % 2 else "tensor_copy", R=[f"ps{cb}"], W=["G_tm"], out=G_tm[:, cb * 512:(cb + 1) * 512], in_=ps[cb][:, 0:512])


def build_GB(B, G_fm, GB, tag):
    K, A = B.K, B.A
    ps = K.ps
    for cb in range(4):
        A.mm([f"ps{cb}"], [], [TR(ps[cb][0:17, j * 128:(j + 1) * 128], G_fm[:, cb * 4 + j, :], B.ident[:]) for j in range(4)])
        A.op(K.act if cb % 2 else K.dve, "copy" if cb % 2 else "tensor_copy", R=[f"ps{cb}"], W=[tag + "GB"], out=GB[0:17, cb * 512:(cb + 1) * 512], in_=ps[cb][0:17, 0:512])


def phase5(B):
    nc, K, A = B.nc, B.K, B.A
    pe, act, dve, pool, sp = K.pe, K.act, K.dve, K.pool, K.sp
    ps = K.ps
    mT = B.view(OFF_M, [128, 16, NT], BF16)
    h2T = B.view(OFF_H2, [128, 16, NT], BF16)
    tw = B.bump([(OFF_C, OFF_LAT)])
    wa = [tw(f"wa{i}", [128, 8, 128], BF16) for i in range(2)]
    wb = [tw(f"wb{i}", [128, 8, 128], BF16) for i in range(2)]
    wga = [tw(f"wga{i}", [128, 16, 128], BF16) for i in range(2)]
    wgb = [tw(f"wgb{i}", [128, 16, 128], BF16) for i in range(2)]
    tt = B.bump([(OFF_T, ARENA_BYTES)])
    sga = [tt(f"sga{i}", [128, 512], F32) for i in range(2)]
    sgb = [tt(f"sgb{i}", [128, 512], F32) for i in range(2)]
    t1 = [tt(f"t1_{i}", [128, 512], F32) for i in range(2)]
    t2 = [tt(f"t2_{i}", [128, 512], F32) for i in range(2)]
    wv = B.w_in.rearrange("(c p) n -> p c n", p=128)
    wav = B.w_a_out.rearrange("(c p) n -> p c n", p=128)
    wbv = B.w_b_out.rearrange("(c p) n -> p c n", p=128)
    it = 0
    for fo in range(16):
        s = fo % 2
        fc = slice(fo * 128, (fo + 1) * 128)
        A.dma(pool, wa[s][:], wav[:, :, fc], R=[], W=[f"wa{s}"], stream=f"wa{s}")
        A.dma(pool, wb[s][:], wbv[:, :, fc], R=[], W=[f"wb{s}"], stream=f"wb{s}")
        for hh in range(2):
            A.dma(pool, wga[s][:, hh * 8:(hh + 1) * 8, :], wv[:, hh * 8:(hh + 1) * 8, O_GTA + fo * 128:O_GTA + (fo + 1) * 128], R=[], W=[(f"wga{s}", hh)],
                  stream=f"wga{s}{hh}")
            A.dma(pool, wgb[s][:, hh * 8:(hh + 1) * 8, :], wv[:, hh * 8:(hh + 1) * 8, O_GTB + fo * 128:O_GTB + (fo + 1) * 128], R=[], W=[(f"wgb{s}", hh)],
                  stream=f"wgb{s}{hh}")
        for (c0, n) in BLKS:
            p = it % 2
            it += 1
            b0 = 4 * p
            cols = slice(c0, c0 + n)
            A.mm([f"ps{b0}"], [f"wa{s}"], [MM(ps[b0][:, 0:n], wa[s][:, hc, :], B.oaT[:, hc, cols], hc == 0, hc == 7) for hc in range(8)])
            A.mm([f"ps{b0 + 1}"], [f"wb{s}"], [MM(ps[b0 + 1][:, 0:n], wb[s][:, hc, :], B.obT[:, hc, cols], hc == 0, hc == 7) for hc in range(8)])
            A.mm([f"ps{b0 + 2}"], [(f"wga{s}", 0), (f"wga{s}", 1)], [MM(ps[b0 + 2][:, 0:n], wga[s][:, dc, :], B.hT[:, dc, cols], dc == 0, dc == 15) for dc in range(16)])
            A.mm([f"ps{b0 + 3}"], [(f"wgb{s}", 0), (f"wgb{s}", 1)], [MM(ps[b0 + 3][:, 0:n], wgb[s][:, dc, :], B.hT[:, dc, cols], dc == 0, dc == 15) for dc in range(16)])
            A.op(act, "activation", R=[f"ps{b0 + 2}"], W=[f"sga{p}"], out=sga[p][:, 0:n], in_=ps[b0 + 2][:, 0:n], func=AF.Sigmoid)
            A.op(act, "activation", R=[f"ps{b0 + 3}"], W=[f"sgb{p}"], out=sgb[p][:, 0:n], in_=ps[b0 + 3][:, 0:n], func=AF.Sigmoid)
            A.op(dve, "tensor_tensor", R=[f"ps{b0}", f"sga{p}"], W=[f"t1_{p}"], out=t1[p][:, 0:n], in0=ps[b0][:, 0:n], in1=sga[p][:, 0:n], op=ALU.mult)
            A.op(dve, "tensor_tensor", R=[f"ps{b0 + 1}", f"sgb{p}"], W=[f"t2_{p}"], out=t2[p][:, 0:n], in0=ps[b0 + 1][:, 0:n], in1=sgb[p][:, 0:n], op=ALU.mult)
            A.op(pool, "tensor_tensor", R=[f"t1_{p}", f"t2_{p}"], W=[("mT", fo, c0)], out=mT[:, fo, cols], in0=t1[p][:, 0:n], in1=t2[p][:, 0:n], op=ALU.add)
    if B.debug and B.stop_after == 5:
        B.dbg("mT", mT[:], BF16)
    K.barrier()
    A.reset()
    wo = B.view(0, [128, 16, D], BF16)
    tmp = B.bump([(OFF_H2 + 36864, ARENA_BYTES)])
    G_tm = tmp("G_tm", [128, D], F32)
    x_t = tmp("x_t", [128, D], F32)
    x1_t = tmp("x1_t", [128, D], F32)
    xs_t = tmp("xs_t", [128, D], F32)
    GB = tmp("GB", [17, D], F32)
    sel = tmp("sel", [17, 256], F32)
    st = tmp("st5", [128, 16], F32)
    tS = tmp("tS5", [128, 4, 128], F32)
    junk = xs_t[:].bitcast(BF16)[:, 0:D]
    wov = B.w_o.rearrange("(c p) n -> p c n", p=128)
    for dc in range(16):
        for q4 in range(4):
            A.dma(pool, wo[:, dc, q4 * 512:(q4 + 1) * 512], wov[:, dc, q4 * 512:(q4 + 1) * 512], R=[], W=[("wo", dc, q4)], stream=f"wo{q4}")
    WO = [("wo", dc, q4) for dc in range(16) for q4 in range(4)]
    A.dma(sp, sel[:], B.sel_d, R=[], W=["sel"], stream="c0")
    build_GB(B, B.G1, GB, "g1")
    for t in range(9):
        cols = slice(t * 128, (t + 1) * 128)
        if t == 0 or t == 8:
            build_G_tm(B, B.G1, GB, G_tm, sel, 0 if t == 0 else 128, "g1")
        A.dma(sp, x_t[:], B.x_main[cols, :], R=[], W=["x_t"], stream="x_t")
        for cb in range(4):
            A.mm([f"ps{cb}"], WO, [MM(ps[cb][:, 0:512], mT[:, dc, cols], wo[:, dc, cb * 512:(cb + 1) * 512], dc == 0, dc == 15) for dc in range(16)])
        A.op(dve, "memset", W=["st5"], ap=st[:], constant=0.0)
        for cb in range(4):
            A.op(act, "activation", R=[f"ps{cb}"], W=["xs_t", "st5"], out=junk[:, 0:512], in_=ps[cb][:, 0:512], func=AF.Square, accum_out=st[:, cb:cb + 1])
        A.op(dve, "tensor_tensor", R=["st5"], W=["st5"], out=st[:, 4:5], in0=st[:, 0:1], in1=st[:, 1:2], op=ALU.add)
        A.op(dve, "tensor_tensor", R=["st5"], W=["st5"], out=st[:, 5:6], in0=st[:, 2:3], in1=st[:, 3:4], op=ALU.add)
        A.op(dve, "tensor_tensor", R=["st5"], W=["st5"], out=st[:, 6:7], in0=st[:, 4:5], in1=st[:, 5:6], op=ALU.add)
        A.op(act, "activation", R=["st5"], W=["st5"], out=st[:, 7:8], in_=st[:, 6:7], func=AF.Sqrt, scale=1.0 / D, bias=B.eps_t[:, 0:1])
        A.op(dve, "reciprocal", R=["st5"], W=["st5"], out=st[:, 8:9], in_=st[:, 7:8])
        for cb in range(4):
            cs_ = slice(cb * 512, (cb + 1) * 512)
            A.op(dve, "scalar_tensor_tensor", R=[f"ps{cb}", "st5", "G_tm"], W=[("x1_t", cb)], out=x1_t[:, cs_], in0=ps[cb][:, 0:512], scalar=st[:, 8:9],
                 in1=G_tm[:, cs_], op0=ALU.mult, op1=ALU.mult)
            A.op(pool, "tensor_tensor", R=[("x1_t", cb), "x_t"], W=[("x1_t", cb)], out=x1_t[:, cs_], in0=x1_t[:, cs_], in1=x_t[:, cs_], op=ALU.add)
        X1 = [("x1_t", cb) for cb in range(4)]
        A.dma(sp, B.y_main[cols, :], x1_t[:], R=X1, W=[], stream="x1o")
        A.op(act, "activation", R=X1, W=["xs_t", "st5"], out=junk[:], in_=x1_t[:], func=AF.Square, accum_out=st[:, 9:10])
        A.op(act, "activation", R=["st5"], W=["st5"], out=st[:, 10:11], in_=st[:, 9:10], func=AF.Sqrt, scale=1.0 / D, bias=B.eps_t[:, 0:1])
        A.op(dve, "reciprocal", R=["st5"], W=["st5"], out=st[:, 11:12], in_=st[:, 10:11])
        A.op(dve, "tensor_scalar", R=X1 + ["st5"], W=["xs_t"], out=xs_t[:], in0=x1_t[:], scalar1=st[:, 11:12], scalar2=None, op0=ALU.mult)
        for g in range(4):
            pb = 4 + g
            A.mm([f"ps{pb}"], ["xs_t"], [TR(ps[pb][:, j * 128:(j + 1) * 128], xs_t[:, (g * 4 + j) * 128:(g * 4 + j + 1) * 128], B.ident[:]) for j in range(4)])
            if t == 8:
                A.op(dve, "tensor_tensor", R=[f"ps{pb}"], W=["tS5"], out=tS[:].rearrange("p a (b t) -> p a b t", t=TS),
                     in0=ps[pb][:].rearrange("p (a b t) -> p a b t", a=4, t=TS),
                     in1=B.A2[:, g * 4:(g + 1) * 4, 1:17].unsqueeze(3).broadcast_to([128, 4, NB, TS]), op=ALU.mult)
                A.op(dve, "tensor_tensor", R=["tS5"], W=[("h2T", t, g)], out=h2T[:, g * 4:(g + 1) * 4, cols].rearrange("p a (b t) -> p a b t", t=TS),
                     in0=tS[:].rearrange("p a (b t) -> p a b t", t=TS),
                     in1=B.modT[:, 48 + g * 4:48 + (g + 1) * 4, 1:17].unsqueeze(3).broadcast_to([128, 4, NB, TS]), op=ALU.add)
            else:
                for j in range(4):
                    dc = g * 4 + j
                    if True:
                        A.op(dve, "tensor_scalar", R=[f"ps{pb}"], W=[("h2T", t, g, j)], out=h2T[:, dc, cols], in0=ps[pb][:, j * 128:(j + 1) * 128],
                             scalar1=B.A2[:, dc, 0:1], scalar2=B.modT[:, 48 + dc, 0:1], op0=ALU.mult, op1=ALU.add)
                    else:
                        A.op(act, "activation", R=[f"ps{pb}"], W=[("h2T", t, g, j)], out=h2T[:, dc, cols], in_=ps[pb][:, j * 128:(j + 1) * 128],
                             func=AF.Identity, scale=B.A2[:, dc, 0:1], bias=B.modT[:, 48 + dc, 0:1])
    if B.debug and B.stop_after == 5:
        B.dbg("h2T", h2T[:], BF16)


def phase6(B):
    nc, K, A = B.nc, B.K, B.A
    pe, act, dve, pool, sp = K.pe, K.act, K.dve, K.pool, K.sp
    ps = K.ps
    h2T = B.view(OFF_H2, [128, 16, NT], BF16)
    uT = B.view(0, [128, 64, 512], BF16)
    tmp = B.bump([(65536, OFF_H2), (OFF_H2 + 36864, ARENA_BYTES)])
    NWU, NWD = 3, 4
    wup = [tmp(f"wup{i}", [128, 16, 128], BF16) for i in range(NWU)]
    wd = [tmp(f"wd{i}", [128, 1024], BF16) for i in range(NWD)]
    GB = tmp("GB6", [17, D], F32)
    sel = tmp("sel6", [17, 256], F32)
    st = tmp("st6", [128, 16], F32)
    rl = [tmp(f"rl{i}", [128, 512], F32) for i in range(2)]
    rl3 = [tmp(f"rl3_{i}", [128, 128], F32) for i in range(2)]

    def uT3(ffc):
        return h2T[:, ffc // 4, (ffc % 4) * 128:(ffc % 4 + 1) * 128]
    junk = tmp("junk6", [128, D], BF16)
    m_tm = tmp("m_tm", [128, 4, D], F32)
    G_tm = tmp("G_tm6", [128, D], F32)
    x1_t = tmp("x1_t6", [128, D], F32)
    wuv = B.w_up.rearrange("(c p) n -> p c n", p=128)
    A.dma(sp, sel[:], B.sel_d, R=[], W=["sel"], stream="c0")
    build_GB(B, B.G2, GB, "g2")
    iu = 0
    idn = 0
    for bi, (c0, n) in enumerate(BLKS):
        tiles = n // 128
        cols = slice(c0, c0 + n)
        if bi == 0 or bi == 2:
            build_G_tm(B, B.G2, GB, G_tm, sel, 0 if bi == 0 else 128, "g2")
        for ffc in range(64 if bi < 2 else 0):
            s = iu % NWU
            pb = iu % 4
            iu += 1
            for hh in range(2):
                A.dma(pool, wup[s][:, hh * 8:(hh + 1) * 8, :], wuv[:, hh * 8:(hh + 1) * 8, ffc * 128:(ffc + 1) * 128], R=[], W=[(f"wup{s}", hh)],
                      stream=f"wup{s}{hh}")
            if bi == 0:
                A.mm([f"ps{pb}"], [(f"wup{s}", 0), (f"wup{s}", 1)], [MM(ps[pb][:, 0:n], wup[s][:, dc, :], h2T[:, dc, cols], dc == 0, dc == 15) for dc in range(16)])
            else:
                mms = []
                for dc in range(16):
                    mms.append(MM(ps[pb][:, 0:n], wup[s][:, dc, :], h2T[:, dc, cols], dc == 0, dc == 15))
                    mms.append(MM(ps[pb + 4][:, 0:NTS], wup[s][:, dc, :], h2T[:, dc, NTP:NTP + NTS], dc == 0, dc == 15))
                A.mm([f"ps{pb}", f"ps{pb + 4}"], [(f"wup{s}", 0), (f"wup{s}", 1)], mms)
            A.op(act, "activation", R=[f"ps{pb}"], W=[f"rl{pb % 2}"], out=rl[pb % 2][:, 0:n], in_=ps[pb][:, 0:n], func=AF.Relu)
            A.op(dve if ffc % 2 else pool, "tensor_tensor", R=[f"rl{pb % 2}"], W=[("uT", ffc)], out=uT[:, ffc, 0:n], in0=rl[pb % 2][:, 0:n],
                 in1=rl[pb % 2][:, 0:n], op=ALU.mult)
            if bi == 1:
                A.op(act, "activation", R=[f"ps{pb + 4}"], W=[f"rl3_{pb % 2}"], out=rl3[pb % 2][:], in_=ps[pb + 4][:, 0:NTS], func=AF.Relu)
                A.op(pool if ffc % 2 else dve, "tensor_tensor", R=[f"rl3_{pb % 2}"], W=[("uT3", ffc)], out=uT3(ffc), in0=rl3[pb % 2][:],
                     in1=rl3[pb % 2][:], op=ALU.mult)
        for half in range(2):
            banks = [f"ps{ti * 2 + cbk}" for ti in range(tiles) for cbk in range(2)]
            for ffc in range(64):
                s = idn % NWD
                idn += 1
                A.dma(pool, wd[s][:], B.w_down[ffc * 128:(ffc + 1) * 128, half * 1024:(half + 1) * 1024], R=[], W=[f"wd{s}"], stream=f"wd{s}")
                if bi < 2:
                    A.mm(banks, [f"wd{s}", ("uT", ffc)], [MM(ps[ti * 2 + cbk][:, 0:512], uT[:, ffc, ti * 128:(ti + 1) * 128], wd[s][:, cbk * 512:(cbk + 1) * 512],
                                                            ffc == 0, ffc == 63) for ti in range(tiles) for cbk in range(2)])
                else:
                    A.mm(banks, [f"wd{s}", ("uT3", ffc)], [MM(ps[cbk][:, 0:512], uT3(ffc), wd[s][:, cbk * 512:(cbk + 1) * 512],
                                                             ffc == 0, ffc == 63) for cbk in range(2)])
            for ti in range(tiles):
                for cbk in range(2):
                    pb = ti * 2 + cbk
                    c_ = half * 1024 + cbk * 512
                    A.op(act if cbk else dve, "copy" if cbk else "tensor_copy", R=[f"ps{pb}"], W=[("m_tm", ti)], out=m_tm[:, ti, c_:c_ + 512], in_=ps[pb][:, 0:512])
        for ti in range(tiles):
            t = c0 // 128 + ti
            rows = slice(t * 128, (t + 1) * 128)
            A.dma(sp, x1_t[:], B.y_main[rows, :], R=[], W=["x1_t"], stream="x1i")
            A.op(dve, "memset", W=["st6"], ap=st[:], constant=0.0)
            A.op(act, "activation", R=[("m_tm", ti)], W=["junk6", "st6"], out=junk[:], in_=m_tm[:, ti, :], func=AF.Square, accum_out=st[:, 0:1])
            A.op(act, "activation", R=["st6"], W=["st6"], out=st[:, 1:2], in_=st[:, 0:1], func=AF.Sqrt, scale=1.0 / D, bias=B.eps_t[:, 0:1])
            A.op(dve, "reciprocal", R=["st6"], W=["st6"], out=st[:, 2:3], in_=st[:, 1:2])
            A.op(dve, "scalar_tensor_tensor", R=[("m_tm", ti), "st6", "G_tm"], W=[("m_tm", ti)], out=m_tm[:, ti, :], in0=m_tm[:, ti, :], scalar=st[:, 2:3],
                 in1=G_tm[:], op0=ALU.mult, op1=ALU.mult)
            A.op(pool, "tensor_tensor", R=[("m_tm", ti), "x1_t"], W=["x1_t"], out=x1_t[:], in0=m_tm[:, ti, :], in1=x1_t[:], op=ALU.add)
            A.dma(sp, B.y_main[rows, :], x1_t[:], R=["x1_t"], W=[], stream="yo")


def _host_consts(core):
    half = core % 2
    inv = (10000.0 ** (-np.arange(32, dtype=np.float32) / 32.0)).astype(np.float32)
    pos_main = np.concatenate([half * NTP + np.arange(NTP), PAST + (np.arange(NTS) % TS)]).astype(np.float32)
    pos_ctx = np.arange(NCX).astype(np.float32)
    pos = np.concatenate([pos_main, pos_ctx])
    ang = pos[:, None] * inv[None, :]
    cos = np.cos(ang).astype(np.float32)
    sin = np.sin(ang).astype(np.float32)
    cs = np.concatenate([cos, cos], axis=1)
    sn = np.concatenate([-sin, sin], axis=1)
    masks = np.zeros((128, 1024), np.float32)
    r = np.arange(128)
    masks[:, 0:128] = ((r[:, None] // 64 == r[None, :] // 64) & (r[:, None] <= r[None, :])).astype(np.float32)
    masks[:, 128:256] = ((r[:, None] // 8 == r[None, :] // 8) & (r[:, None] <= r[None, :])).astype(np.float32)
    masks[:, 256:384] = (r[:, None] <= r[None, :]).astype(np.float32)
    masks[:, 760:776] = (r[:, None] // 8 == np.arange(16)[None, :]).astype(np.float32)
    masks[:, 800] = r.astype(np.float32)
    sm = np.zeros((128, NB, NH, TS), np.float32)
    for s in range(128):
        sm[s, s // 8, :, (s % 8):] = 1.0
    sel = np.zeros((17, 256), np.float32)
    sel[0, 0:128] = 1.0
    for tok in range(128):
        sel[1 + tok // 8, 128 + tok] = 1.0
    flag = np.full((128, 1), float(half), np.float32)
    cbias = np.full((1, NCX), 0.0 if half == 1 else NEG, np.float32)
    return dict(rope_cs=cs, rope_sn=sn, ropeT_cs=np.ascontiguousarray(cs[:NT].T), ropeT_sn=np.ascontiguousarray(sn[:NT].T),
                masks=masks, smask=sm.reshape(128, NB * 64), sel=sel, ctx_flag=flag, ctx_bias=cbias,
                ident=np.eye(128, dtype=np.float32))


def make_in_maps(inputs, cores, cache_all, ptab_all):
    f = lambda a: np.ascontiguousarray(np.asarray(a, dtype=np.float32))
    x_prompt = f(inputs["x_prompt"]); x_sample = f(inputs["x_sample"])
    c_prompt = f(inputs["c_prompt"]); c_sample = f(inputs["c_sample"])
    vecs = np.concatenate([
        f(inputs["g_pre_mix"]).reshape(16, 128), f(inputs["g_post_mix"]).reshape(16, 128),
        f(inputs["g_pre_mlp"]).reshape(16, 128), f(inputs["g_post_mlp"]).reshape(16, 128),
        f(inputs["g_q_norm"]).reshape(4, 128), f(inputs["g_kv_norm"]).reshape(4, 128),
        f(inputs["lb_logits"]).reshape(16, 128), f(inputs["g_hgrn_norm"]).reshape(8, 128)], axis=0)
    shared = dict(
        w_ada=f(inputs["w_ada"])[0], b_ada=f(inputs["b_ada"]).reshape(1, 6 * D), vecs=vecs,
        g_q=f(inputs["g_q_norm"]).reshape(1, QR),
        g_kv=f(inputs["g_kv_norm"]).reshape(1, KVR), w_in=f(inputs["w_in"])[0], w_a_out=f(inputs["w_a_out"])[0],
        w_q_up=f(inputs["w_q_up"])[0], w_kv_up=f(inputs["w_kv_up"])[0], w_b_out=f(inputs["w_b_out"])[0],
        w_o=f(inputs["w_o"])[0], w_up=f(inputs["w_up"])[0], w_down=f(inputs["w_down"])[0],
        cache_all=cache_all)
    state = f(inputs["state_hgrn"])[0]
    maps = []
    for c in cores:
        b, half = c // 2, c % 2
        m = dict(shared)
        m["x_main"] = np.concatenate([x_prompt[b, half * NTP:(half + 1) * NTP], x_sample[c * NB:(c + 1) * NB].reshape(NTS, D)], axis=0)
        m["x_ctx"] = x_prompt[b, 0:NCX]
        m["c_all"] = np.concatenate([c_prompt[b:b + 1], c_sample[c * NB:(c + 1) * NB]], axis=0)
        m["state_in"] = state[c * NB:(c + 1) * NB]
        m["ptab"] = np.ascontiguousarray(ptab_all[c * NB:(c + 1) * NB]).astype(np.int32).reshape(1, NB * NPAGES)
        m.update(_host_consts(c))
        maps.append(m)
    return maps


def assemble(results, cores):
    y_prompt = np.zeros((4, 2048, D), np.float32); y_sample = np.zeros((128, TS, D), np.float32)
    ckv_p = np.zeros((1, 4, 2048, KVR), np.float32); kpe_p = np.zeros((1, 4, 2048, ROPE), np.float32)
    s_p = np.zeros((1, 4, NH, DK, DV), np.float32)
    ckv_s = np.zeros((1, 128, TS, KVR), np.float32); kpe_s = np.zeros((1, 128, TS, ROPE), np.float32)
    s_s = np.zeros((1, 128, NH, DK, DV), np.float32)
    for r, c in zip(results, cores):
        b, half = c // 2, c % 2
        sl = slice(half * NTP, (half + 1) * NTP)
        y_prompt[b, sl] = r["y_main"][:NTP]; y_sample[c * NB:(c + 1) * NB] = r["y_main"][NTP:].reshape(NB, TS, D)
        ckv_p[0, b, sl] = r["ckv_main"][:NTP]; ckv_s[0, c * NB:(c + 1) * NB] = r["ckv_main"][NTP:].reshape(NB, TS, KVR)
        kpe_p[0, b, sl] = r["kpe_main"][:NTP]; kpe_s[0, c * NB:(c + 1) * NB] = r["kpe_main"][NTP:].reshape(NB, TS, ROPE)
        if half == 1:
            s_p[0, b] = r["s_prompt"]
        s_s[0, c * NB:(c + 1) * NB] = r["s_sample"]
    return (y_prompt, y_sample, ckv_p, kpe_p, s_p, ckv_s, kpe_s, s_s)


def kernel(**inputs):
    cores = list(range(8))
    cache_ckv = np.ascontiguousarray(np.asarray(inputs["cache_ckv"], dtype=np.float32)).reshape(-1, KVR)
    cache_kpe = np.ascontiguousarray(np.asarray(inputs["cache_kpe"], dtype=np.float32)).reshape(-1, ROPE)
    ptab = np.asarray(inputs["page_table"]).astype(np.int32)
    cache_all = np.concatenate([cache_ckv, cache_kpe], axis=1)
    del cache_ckv, cache_kpe
    nc = build_nc(cache_all.shape[0])
    maps = make_in_maps(inputs, cores, cache_all, ptab)
    res = run_bass_kernel_spmd(nc, maps, core_ids=cores)
    return assemble(res.results, cores)
```

```python
import math
import numpy as np
import ml_dtypes
import concourse.bass as bass
import concourse.mybir as mybir
from concourse.bass_utils import run_bass_kernel_spmd

F32 = mybir.dt.float32
BF16 = mybir.dt.bfloat16
I32 = mybir.dt.int32
AF = mybir.ActivationFunctionType
ALU = mybir.AluOpType
AX = mybir.AxisListType

D = 2048
NH = 8
DK = 128
DV = 128
QR = 512
KVR = 512
NOPE = 128
ROPE = 64
VD = 128
DFF = 8192
EPS = 1e-6
PAST = 8192
PAGE = 128
NPAGES = 64
SCALE = (NOPE + ROPE) ** -0.5
NTP = 1024
NTS = 128
NT = NTP + NTS
NCX = 1024
NB = 16
TS = 8
NEG = -30000.0
NKEY = NCX + NTP + NTS

O_QA, O_FA, O_IA, O_GA, O_QD, O_KVD, O_KPE, O_GTA, O_GTB = 0, 1024, 2048, 3072, 4096, 4608, 5120, 5184, 7232
IN_COLS = 9280
BLKS = [(0, 512), (512, 512), (1024, 128)]
CBLKS = [(0, 512), (512, 512)]


def _flat(deps):
    out = []
    if deps is None:
        return out
    if isinstance(deps, tuple) and len(deps) == 3 and isinstance(deps[0], str):
        return [deps]
    for d in deps:
        out.extend(_flat(d))
    return out


class Eng:
    def __init__(self, nc, name, eng):
        self.nc, self.name, self.eng = nc, name, eng
        self.gen = 0
        self.sem = nc.alloc_semaphore("sem_" + name)
        self.cnt = 0
        self.seen = {}
        self.last = None

    def wait(self, deps):
        best = {}
        for (key, sem, val) in _flat(deps):
            if self.seen.get(key, 0) >= val:
                continue
            if key not in best or best[key][1] < val:
                best[key] = (sem, val)
        for key, (sem, val) in best.items():
            self.eng.wait_ge(sem, val)
            self.seen[key] = val

    def __call__(self, opname, deps=None, sig=True, **kw):
        self.wait(deps)
        inst = getattr(self.eng, opname)(**kw)
        if sig:
            if self.cnt >= 30000:
                self.gen += 1
                self.sem = self.nc.alloc_semaphore(f"sem_{self.name}_{self.gen}")
                self.cnt = 0
            self.cnt += 1
            inst.then_inc(self.sem, 1)
            self.last = (f"{self.name}.{self.gen}", self.sem, self.cnt)
            return self.last
        return None

    def tok(self):
        return self.last


class Ctx:
    def __init__(self, nc):
        self.nc = nc
        self.pe = Eng(nc, "pe", nc.tensor)
        self.act = Eng(nc, "act", nc.scalar)
        self.dve = Eng(nc, "dve", nc.vector)
        self.pool = Eng(nc, "pool", nc.gpsimd)
        self.sp = Eng(nc, "sp", nc.sync)
        self.engs = [self.pe, self.act, self.dve, self.pool, self.sp]
        self.streams = {}
        self.dma_toks = []
        self.ps = [nc.alloc_psum_tensor(f"psb{i}", [128, 512], F32) for i in range(8)]
        self.ps_free = [None] * 8
        self.ps_i = 0
        self._n = 0

    def name(self, s):
        self._n += 1
        return f"{s}_{self._n}"

    def dma(self, q, out, in_, stream, deps=None, indirect=None):
        if stream not in self.streams:
            self.streams[stream] = [self.nc.alloc_semaphore("dsem_" + stream), 0]
        st = self.streams[stream]
        q.wait(deps)
        if indirect is not None:
            inst = q.eng.indirect_dma_start(out=out, out_offset=None, in_=in_,
                                            in_offset=bass.IndirectOffsetOnAxis(ap=indirect, axis=0))
        else:
            inst = q.eng.dma_start(out=out, in_=in_)
        st[1] += 16
        inst.then_inc(st[0], 16)
        tok = ("d_" + stream, st[0], st[1])
        self.dma_toks.append(tok)
        return tok

    def psum(self):
        i = self.ps_i
        self.ps_i = (i + 1) % 8
        return i, self.ps[i], self.ps_free[i]

    def psum_release(self, i, tok):
        self.ps_free[i] = tok

    def barrier(self):
        toks = [e.tok() for e in self.engs if e.tok() is not None]
        latest = {}
        for t in self.dma_toks:
            latest[t[0]] = t
        toks += list(latest.values())
        self.dma_toks = list(latest.values())
        for e in self.engs:
            e.wait(toks)
        self.ps_free = [None] * 8


class Auto:
    def __init__(self, K):
        self.K = K
        self.w = {}
        self.r = {}

    def reset(self):
        self.w = {}
        self.r = {}

    def deps(self, R, W):
        d = []
        for k in R:
            if k in self.w:
                d.append(self.w[k])
        for k in W:
            if k in self.w:
                d.append(self.w[k])
            d.extend(self.r.get(k, {}).values())
        return d

    def note(self, tok, R, W):
        for k in R:
            self.r.setdefault(k, {})[tok[0]] = tok
        for k in W:
            self.w[k] = tok
            self.r[k] = {}

    def op(self, eng, opname, R=(), W=(), **kw):
        tok = eng(opname, deps=self.deps(R, W), **kw)
        self.note(tok, R, W)
        return tok

    def mm(self, W, R, mms):
        pe = self.K.pe
        pe.wait(self.deps(R, W))
        tok = None
        for i, (opname, kw) in enumerate(mms):
            tok = pe(opname, sig=(i == len(mms) - 1), **kw)
        self.note(tok, R, W)
        return tok

    def dma(self, q, out, in_, R, W, stream, indirect=None):
        tok = self.K.dma(q, out, in_, stream, deps=self.deps(R, W), indirect=indirect)
        self.note(tok, R, W)
        return tok

    def dma_group(self, q, items, R, W, stream):
        d = self.deps(R, W)
        tok = None
        for (out, in_) in items:
            tok = self.K.dma(q, out, in_, stream, deps=d)
            d = None
        self.note(tok, R, W)
        return tok


def sb(nc, name, shape, dt):
    return nc.alloc_sbuf_tensor(name, list(shape), dt)


class Bld:
    pass


def MM(out, lhsT, rhs, start, stop):
    return ("matmul", dict(out=out, lhsT=lhsT, rhs=rhs, start=start, stop=stop))


def TR(out, in_, identity):
    return ("transpose", dict(out=out, in_=in_, identity=identity))


ARENA_BYTES = 196608 - 4096
OFF_H = 0
OFF_C = 36864
OFF_LAT = 69632
OFF_OA = 118016
OFF_OB = 136448
OFF_T = 154880


def build_nc(n_pool_rows, stop_after=99, debug=False):
    nc = bass.Bass("TRN2", target_bir_lowering=False)
    K = Ctx(nc)
    A = Auto(K)

    def dbg(name, ap, dt=F32):
        if not debug:
            return
        K.barrier()
        d = nc.dram_tensor("dbg_" + name, list(ap.shape), dt, kind="ExternalOutput").ap()
        K.dma(K.sp, d, ap, "dbg")
        K.barrier()
    pe, act, dve, pool, sp = K.pe, K.act, K.dve, K.pool, K.sp
    B = Bld()
    B.nc, B.K, B.A = nc, K, A
    B.dbg = dbg
    B.debug = debug

    def din(name, shape, dt=F32):
        return nc.dram_tensor(name, list(shape), dt, kind="ExternalInput").ap()

    def dout(name, shape, dt=F32):
        return nc.dram_tensor(name, list(shape), dt, kind="ExternalOutput").ap()

    x_main = din("x_main", [NT, D])
    x_ctx = din("x_ctx", [NCX, D])
    c_all = din("c_all", [17, D])
    cache_all = din("cache_all", [n_pool_rows, KVR + ROPE])
    state_in = din("state_in", [NB, NH, DK, DV])
    ptab = din("ptab", [1, NB * NPAGES], I32)
    w_ada = din("w_ada", [D, 6 * D])
    b_ada = din("b_ada", [1, 6 * D])
    vecs = din("vecs", [96, 128])
    g_q = din("g_q", [1, QR])
    g_kv = din("g_kv", [1, KVR])
    w_in = din("w_in", [D, IN_COLS])
    w_a_out = din("w_a_out", [NH * DV, D])
    w_q_up = din("w_q_up", [QR, NH * (NOPE + ROPE)])
    w_kv_up = din("w_kv_up", [KVR, NH * (NOPE + VD)])
    w_b_out = din("w_b_out", [NH * VD, D])
    w_o = din("w_o", [D, D])
    w_up = din("w_up", [D, DFF])
    w_down = din("w_down", [DFF, D])
    ident_d = din("ident", [128, 128])
    rope_cs = din("rope_cs", [17 * 128, 64])
    rope_sn = din("rope_sn", [17 * 128, 64])
    ropeT_cs = din("ropeT_cs", [64, NT])
    ropeT_sn = din("ropeT_sn", [64, NT])
    ctx_bias = din("ctx_bias", [1, NCX])
    ctx_flag = din("ctx_flag", [128, 1])
    masks_d = din("masks", [128, 1024])
    smask_d = din("smask", [128, NB * 64])
    sel_d = din("sel", [17, 256])

    y_main = dout("y_main", [NT, D])
    ckv_main = dout("ckv_main", [NT, KVR])
    kpe_main = dout("kpe_main", [NT, ROPE])
    s_prompt = dout("s_prompt", [NH, DK, DV])
    s_sample = dout("s_sample", [NB, NH, DK, DV])

    ident = sb(nc, "ident_sb", [128, 128], F32)
    ident_bf = sb(nc, "ident_bf", [128, 128], BF16)
    vecT = sb(nc, "vecT", [128, 96], F32)
    modT = sb(nc, "modT", [128, 96, 17], F32)
    A1 = sb(nc, "A1", [128, 16, 17], F32)
    A2 = sb(nc, "A2", [128, 16, 17], F32)
    G1 = sb(nc, "G1", [128, 16, 17], F32)
    G2 = sb(nc, "G2", [128, 16, 17], F32)
    lbT = sb(nc, "lbT", [128, 8], F32)
    omlT = sb(nc, "omlT", [128, 8], F32)
    nomlT = sb(nc, "nomlT", [128, 8], F32)
    masks = sb(nc, "masks_sb", [128, 1024], F32)
    cflag = sb(nc, "cflag", [128, 1], F32)
    ones_bf = sb(nc, "ones_bf", [128, 128], BF16)
    ones_f = sb(nc, "ones_f", [128, 128], F32)
    eps_t = sb(nc, "eps_t", [128, 1], F32)
    arena = sb(nc, "arena", [128, ARENA_BYTES // 4], F32)
    ar_f = arena[:]
    ar_b = arena[:].bitcast(BF16)
    ar_i = arena[:].bitcast(I32)

    def view(off, shape, dt):
        n = int(np.prod(shape[1:]))
        assert off % 4 == 0
        if dt == BF16:
            assert off + 2 * n <= ARENA_BYTES, (off, shape)
            v = ar_b[:, off // 2: off // 2 + n]
        else:
            assert off + 4 * n <= ARENA_BYTES, (off, shape)
            v = (ar_f if dt == F32 else ar_i)[:, off // 4: off // 4 + n]
        if shape[0] != 128:
            v = v[0:shape[0]]
        if len(shape) == 3:
            v = v.rearrange("p (a b) -> p a b", a=shape[1])
        elif len(shape) == 4:
            v = v.rearrange("p (a b c) -> p a b c", a=shape[1], b=shape[2])
        elif len(shape) == 5:
            v = v.rearrange("p (a b c d) -> p a b c d", a=shape[1], b=shape[2], c=shape[3])
        return v

    def bump(regions):
        state = {"r": 0, "o": regions[0][0]}

        def tmp(name, shape, dt):
            n = int(np.prod(shape[1:])) * (2 if dt == BF16 else 4)
            n = (n + 63) // 64 * 64
            while True:
                s_, e_ = regions[state["r"]]
                if state["o"] + n <= e_:
                    off = state["o"]
                    state["o"] += n
                    return view(off, shape, dt)
                state["r"] += 1
                assert state["r"] < len(regions), ("arena region overflow", name)
                state["o"] = regions[state["r"]][0]
        return tmp

    REG_H, REG_C, REG_LAT = (OFF_H, OFF_C), (OFF_C, OFF_LAT), (OFF_LAT, OFF_OA)
    REG_BIG = (OFF_LAT, OFF_T)
    REG_L = (OFF_T, ARENA_BYTES)
    hT = view(OFF_H, [128, 16, NT], BF16)
    hcT = view(OFF_C, [128, 16, NCX], BF16)
    o = OFF_LAT
    ckvT = view(o, [128, 4, NKEY], BF16); o += 2 * 4 * NKEY
    kpeT = view(o, [128, NKEY], BF16); o += 2 * NKEY
    ckv_tm = view(o, [128, 17, KVR], BF16); o += 2 * 17 * KVR
    qnT = view(o, [128, 4, NT], BF16); o += 2 * 4 * NT
    assert o == OFF_OA, o
    oaT = view(OFF_OA, [128, 8, NT], BF16)
    obT = view(OFF_OB, [128, 8, NT], BF16)
    B.__dict__.update(locals())

    phases = [phase0, phase1, phase2, phase3, phase4, phase5, phase6]
    for i, ph in enumerate(phases):
        if stop_after >= i:
            K.barrier()
            A.reset()
            ph(B)
    K.barrier()
    return nc


def evac_copy(K, which, out, in_, deps):
    if which == 0:
        return K.dve("tensor_copy", deps=deps, out=out, in_=in_)
    return K.act("copy", deps=deps, out=out, in_=in_)


def phase0(B):
    nc, K = B.nc, B.K
    pe, act, dve, pool, sp = K.pe, K.act, K.dve, K.pool, K.sp
    if True:
        tb = B.bump([B.REG_BIG]); tc = B.bump([B.REG_C]); th = B.bump([B.REG_H]); tl = B.bump([B.REG_L])
        modB = tb("modB", [17, 6 * D], F32)
        c_sb = tb("c_sb", [17, D], F32)
        c_si = tb("c_si", [17, D], F32)
        wa = [tc(f"wa{i}", [128, 16, 512], BF16) for i in range(2)]
        bada_bf = th("bada_bf", [1, 6 * D], BF16)
        scT = tl("scT", [128, 16, 17], BF16)
        vec_sb = tl("vec_sb", [96, 128], F32)
        l8 = tl("l8", [128, 8], F32)

        t_c = [K.dma(sp, B.ident[:], B.ident_d, "c0")]
        t_c.append(K.dma(sp, c_sb[:], B.c_all, "c0"))
        t_c.append(K.dma(sp, vec_sb[:], B.vecs, "c0"))
        t_c.append(K.dma(sp, B.masks[:], B.masks_d, "c0"))
        t_c.append(K.dma(sp, B.cflag[:], B.ctx_flag, "c0"))
        t_c = t_c[-1]
        t_b = K.dma(pool, bada_bf[:], B.b_ada, "c1")
        t_o1 = dve("memset", ap=B.ones_bf[:], constant=1.0)
        t_o2 = dve("memset", ap=B.ones_f[:], constant=1.0)
        dve("memset", ap=B.eps_t[:], constant=EPS)

        t_si = act("activation", deps=[t_c], out=c_si[:], in_=c_sb[:], func=AF.Silu)
        i, ps, fr = K.psum()
        for dc in range(16):
            t_tr = pe("transpose", deps=[t_si, t_c, fr], sig=(dc == 15), out=ps[:, dc * 17:(dc + 1) * 17],
                      in_=c_si[0:17, dc * 128:(dc + 1) * 128], identity=B.ident[0:17, 0:17])
        t_scT = dve("tensor_copy", deps=[t_tr], out=scT[:], in_=ps[:, 0:272].rearrange("p (a b) -> p a b", b=17))
        K.psum_release(i, t_scT)

        i, ps, fr = K.psum()
        t_tr = pe("transpose", deps=[t_c, fr], out=ps[:, 0:96], in_=vec_sb[0:96, :], identity=B.ident[0:96, 0:96])
        t_vec = dve("tensor_copy", deps=[t_tr], out=B.vecT[:], in_=ps[:, 0:96])
        K.psum_release(i, t_vec)

        wv = B.w_ada.rearrange("(c p) n -> p c n", p=128)
        mm_tok = [None, None]
        ev_toks = []
        for cb in range(24):
            buf = wa[cb % 2]
            t_w = K.dma(pool, buf[:], wv[:, :, cb * 512:(cb + 1) * 512], f"wa{cb % 2}", deps=[mm_tok[cb % 2]])
            i, ps, fr = K.psum()
            for dc in range(16):
                pe("matmul", deps=[t_w, fr, t_scT], sig=False, out=ps[0:17, :], lhsT=scT[:, dc, :], rhs=buf[:, dc, :],
                   start=(dc == 0), stop=False)
            t_mm = pe("matmul", deps=[t_b, t_o1], out=ps[0:17, :], lhsT=B.ones_bf[0:1, 0:17],
                      rhs=bada_bf[0:1, cb * 512:(cb + 1) * 512], start=False, stop=True)
            mm_tok[cb % 2] = t_mm
            t_ev = evac_copy(K, cb % 2, modB[:, cb * 512:(cb + 1) * 512], ps[0:17, :], [t_mm])
            K.psum_release(i, t_ev)
            ev_toks.append(t_ev)
        t_mt = []
        for g in range(4):
            i, ps, fr = K.psum()
            for k in range(24):
                j = g * 24 + k
                t_tr = pe("transpose", deps=[ev_toks, fr], sig=(k == 23), out=ps[:, k * 17:(k + 1) * 17],
                          in_=modB[0:17, j * 128:(j + 1) * 128], identity=B.ident[0:17, 0:17])
            t_e = dve("tensor_copy", deps=[t_tr], out=B.modT[:, g * 24:(g + 1) * 24, :],
                      in_=ps[:, 0:408].rearrange("p (a b) -> p a b", b=17))
            K.psum_release(i, t_e)
            t_mt.append(t_e)

        def bc(lo):
            return B.vecT[:, lo:lo + 16].unsqueeze(2).broadcast_to([128, 16, 17])
        dve("scalar_tensor_tensor", deps=[t_mt, t_vec], out=B.A1[:], in0=B.modT[:, 16:32, :], scalar=1.0, in1=bc(0),
            op0=ALU.add, op1=ALU.mult)
        dve("scalar_tensor_tensor", out=B.A2[:], in0=B.modT[:, 64:80, :], scalar=1.0, in1=bc(32),
            op0=ALU.add, op1=ALU.mult)
        dve("tensor_tensor", out=B.G1[:], in0=B.modT[:, 32:48, :], in1=bc(16), op=ALU.mult)
        dve("tensor_tensor", out=B.G2[:], in0=B.modT[:, 80:96, :], in1=bc(48), op=ALU.mult)
        t_l = dve("tensor_tensor", out=l8[:], in0=B.vecT[:, 72:80], in1=B.vecT[:, 80:88], op=ALU.subtract)
        t_lb = act("activation", deps=[t_l], out=B.lbT[:], in_=l8[:], func=AF.Sigmoid)
        dve("tensor_scalar", deps=[t_lb], out=B.omlT[:], in0=B.lbT[:], scalar1=-1.0, scalar2=1.0, op0=ALU.mult, op1=ALU.add)
        dve("tensor_scalar", deps=[t_lb], out=B.nomlT[:], in0=B.lbT[:], scalar1=1.0, scalar2=-1.0, op0=ALU.mult, op1=ALU.add)
        dve("tensor_copy", deps=[t_c], out=B.ident_bf[:], in_=B.ident[:])
        K.barrier()
        B.dbg("modT", B.modT[:])
        B.dbg("A1", B.A1[:])
        B.dbg("vecT", B.vecT[:])
        B.dbg("lbT", B.lbT[:])


def expand_mod(B, dst, src, deps=None):
    return B.K.dve("tensor_copy", deps=deps, out=dst[:].rearrange("p a (b t) -> p a b t", t=TS),
                   in_=src[:, :, 1:17].unsqueeze(3).broadcast_to([128, 16, NB, TS]))


def phase1(B):
    nc, K = B.nc, B.K
    pe, act, dve, pool, sp = K.pe, K.act, K.dve, K.pool, K.sp
    if True:
        tmp = B.bump([B.REG_BIG, B.REG_L])
        xt = [tmp(f"xt{i}", [128, D], F32) for i in range(2)]
        xs = [tmp(f"xs{i}", [128, D], F32) for i in range(2)]
        junk = tmp("junk", [128, D], BF16)
        ss = tmp("ss", [128, 17], F32)
        rt = tmp("rt", [128, 17], F32)
        rstd = tmp("rstd", [128, 17], F32)
        A1s = tmp("A1s", [128, 16, 128], F32)
        B1s = tmp("B1s", [128, 16, 128], F32)
        tmpS = [tmp(f"tmpS{i}", [128, 4, 128], F32) for i in range(2)]
        t_A1s = expand_mod(B, A1s, B.A1)
        t_B1s = expand_mod(B, B1s, B.modT[:, 0:16, :])
        xt_free = [None, None]
        xs_free = [None, None]
        tiles = [("m", t) for t in range(9)] + [("c", u) for u in range(8)]
        for k, (kind, t) in enumerate(tiles):
            p = k % 2
            src = B.x_main[t * 128:(t + 1) * 128, :] if kind == "m" else B.x_ctx[t * 128:(t + 1) * 128, :]
            t_x = K.dma(sp, xt[p][:], src, f"x{p}", deps=[xt_free[p]])
            CUT = 9
            if CUT == 0:
                xt_free[p] = t_x
                continue
            t_sq = act("activation", deps=[t_x], out=junk[:], in_=xt[p][:], func=AF.Square, accum_out=ss[:, k:k + 1])
            t_sr = act("activation", deps=[t_sq], out=rt[:, k:k + 1], in_=ss[:, k:k + 1], func=AF.Sqrt, scale=1.0 / D, bias=B.eps_t[:, 0:1])
            t_rs = dve("reciprocal", deps=[t_sr], out=rstd[:, k:k + 1], in_=rt[:, k:k + 1])
            t_xs = dve("tensor_scalar", deps=[t_rs, t_x, xs_free[p]], out=xs[p][:], in0=xt[p][:], scalar1=rstd[:, k:k + 1],
                       scalar2=None, op0=ALU.mult)
            xt_free[p] = [t_xs, t_sq]
            if CUT == 1:
                xs_free[p] = t_xs
                continue
            last_tr = None
            for g in range(4):
                i, ps, fr = K.psum()
                for j in range(4):
                    dc = g * 4 + j
                    t_tr = pe("transpose", deps=[t_xs, fr], sig=(j == 3), out=ps[:, j * 128:(j + 1) * 128],
                              in_=xs[p][:, dc * 128:(dc + 1) * 128], identity=B.ident[:])
                last_tr = t_tr
                evs = []
                if CUT == 2:
                    K.psum_release(i, t_tr)
                    continue
                if kind == "m" and t == 8:
                    ts_ = tmpS[g % 2]
                    t_1 = dve("tensor_tensor", deps=[t_tr, t_A1s], out=ts_[:], in0=ps[:].rearrange("p (a b) -> p a b", b=128),
                              in1=A1s[:, g * 4:(g + 1) * 4, :], op=ALU.mult)
                    t_2 = dve("tensor_tensor", deps=[t_1, t_B1s], out=B.hT[:, g * 4:(g + 1) * 4, t * 128:(t + 1) * 128], in0=ts_[:],
                              in1=B1s[:, g * 4:(g + 1) * 4, :], op=ALU.add)
                    evs = [t_1, t_2]
                else:
                    for j in range(4):
                        dc = g * 4 + j
                        dst = (B.hT if kind == "m" else B.hcT)[:, dc, t * 128:(t + 1) * 128]
                        if True:
                            evs.append(dve("tensor_scalar", deps=[t_tr], out=dst, in0=ps[:, j * 128:(j + 1) * 128],
                                           scalar1=B.A1[:, dc, 0:1], scalar2=B.modT[:, dc, 0:1], op0=ALU.mult, op1=ALU.add))
                        else:
                            evs.append(act("activation", deps=[t_tr], out=dst, in_=ps[:, j * 128:(j + 1) * 128], func=AF.Identity,
                                           scale=B.A1[:, dc, 0:1], bias=B.modT[:, dc, 0:1]))
                K.psum_release(i, evs)
            xs_free[p] = last_tr
        K.barrier()
        B.dbg("hT", B.hT[:], BF16)
        B.dbg("hcT", B.hcT[:], BF16)


def rms_rstd_a(B, ps_in, pskey, junk, st, c, n):
    A, K = B.A, B.K
    A.op(K.act, "activation", R=[pskey], W=["junk2", ("st", c)], out=junk, in_=ps_in, func=AF.Square,
         accum_out=st[:, c:c + 1])
    A.op(K.act, "activation", R=[("st", c)], W=[("st", c + 1)], out=st[:, c + 1:c + 2], in_=st[:, c:c + 1],
         func=AF.Sqrt, scale=1.0 / n, bias=B.eps_t[:, 0:1])
    A.op(K.dve, "reciprocal", R=[("st", c + 1)], W=[("st", c + 2)], out=st[:, c + 2:c + 3], in_=st[:, c + 1:c + 2])
    return st[:, c + 2:c + 3], ("st", c + 2)


def phase2(B):
    nc, K, A = B.nc, B.K, B.A
    pe, act, dve, pool, sp = K.pe, K.act, K.dve, K.pool, K.sp
    ps = K.ps
    tmp = B.bump([(OFF_OA, ARENA_BYTES)])
    wm = tmp("wm", [128, 16, 1088], BF16)
    gq_b = tmp("gq_b", [128, QR], F32)
    gkv_b = tmp("gkv_b", [128, KVR], F32)
    cs_t = tmp("cs_t", [128, 17, 64], F32)
    sn_t = tmp("sn_t", [128, 17, 64], F32)
    junk = tmp("junk2", [128, 512], BF16)
    st = tmp("st2", [128, 6 * 17], F32)
    ckv32 = [tmp(f"ckv32_{i}", [128, KVR], F32) for i in range(2)]
    qn32 = [tmp(f"qn32_{i}", [128, QR], F32) for i in range(2)]
    r1 = tmp("r1", [128, 64], F32)
    r2 = tmp("r2", [128, 64], F32)
    kpe32 = [tmp(f"kpe32_{i}", [128, 64], F32) for i in range(2)]
    wv = B.w_in.rearrange("(c p) n -> p c n", p=128)
    A.dma_group(sp, [(gq_b[:], B.g_q.partition_broadcast(128)), (gkv_b[:], B.g_kv.partition_broadcast(128)),
                     (cs_t[:], B.rope_cs.rearrange("(t p) r -> p t r", p=128)),
                     (sn_t[:], B.rope_sn.rearrange("(t p) r -> p t r", p=128))],
                R=[], W=["gq_b", "gkv_b", "cs_t", "sn_t"], stream="c0")
    for j, (c0, n) in enumerate([(0, 512), (512, 512), (1024, 64)]):
        for hh in range(2):
            A.dma(pool, wm[:, hh * 8:(hh + 1) * 8, c0:c0 + n], wv[:, hh * 8:(hh + 1) * 8, O_QD + c0:O_QD + c0 + n],
                  R=[], W=[("wm", j, hh)], stream=f"wm{j}{hh}")
    WM = [("wm", j, hh) for j in range(3) for hh in range(2)]
    A.dma(pool, B.kpeT[64:65, 0:NCX], B.ctx_bias, R=[], W=["kpeT_b"], stream="wmb")
    A.op(dve, "memset", W=["kpeT_b2"], ap=B.kpeT[64:65, NCX:], constant=0.0)
    A.op(dve, "memset", W=[("st", i) for i in range(6 * 17)], ap=st[:], constant=0.0)
    tiles = [("m", t) for t in range(9)] + [("c", u) for u in range(8)]
    for k, (kind, t) in enumerate(tiles):
        p = k % 2
        hsrc = B.hT if kind == "m" else B.hcT
        kcol = (NCX + t * 128) if kind == "m" else t * 128
        kidx = kcol // 128
        ti = t if kind == "m" else 9 + t
        cols = slice(t * 128, (t + 1) * 128)
        A.mm([f"ps{p}"], WM, [MM(ps[p][:, 0:512], hsrc[:, dc, cols], wm[:, dc, 512:1024], dc == 0, dc == 15) for dc in range(16)])
        A.mm(["ps4"], WM, [MM(ps[4][:, 0:64], hsrc[:, dc, cols], wm[:, dc, 1024:1088], dc == 0, dc == 15) for dc in range(16)])
        if kind == "m":
            A.mm([f"ps{2 + p}"], WM, [MM(ps[2 + p][:, 0:512], hsrc[:, dc, cols], wm[:, dc, 0:512], dc == 0, dc == 15) for dc in range(16)])
        rs, rk = rms_rstd_a(B, ps[p][:, 0:512], f"ps{p}", junk[:], st, 3 * k, KVR)
        A.op(dve, "scalar_tensor_tensor", R=[f"ps{p}", rk, "gkv_b"], W=[f"ckv32_{p}"], out=ckv32[p][:], in0=ps[p][:, 0:512],
             scalar=rs, in1=gkv_b[:], op0=ALU.mult, op1=ALU.mult)
        if kind == "m":
            A.dma(sp, B.ckv_main[cols, :], ckv32[p][:], R=[f"ckv32_{p}"], W=[], stream=f"ock{p}")
        A.op(pool, "tensor_copy", R=[f"ckv32_{p}"], W=[("ckv_tm", kidx)], out=B.ckv_tm[:, kidx, :], in_=ckv32[p][:])
        A.mm(["ps5"], [f"ckv32_{p}"], [TR(ps[5][:, cc * 128:(cc + 1) * 128], ckv32[p][:, cc * 128:(cc + 1) * 128], B.ident[:])
                                      for cc in range(4)])
        A.op(act, "copy", R=["ps5"], W=[("ckvT", kidx)], out=B.ckvT[:, :, kcol:kcol + 128],
             in_=ps[5][:].rearrange("p (a b) -> p a b", b=128))
        A.op(dve, "tensor_tensor", R=["ps4", "cs_t"], W=["r1"], out=r1[:], in0=ps[4][:, 0:64], in1=cs_t[:, ti, :], op=ALU.mult)
        A.op(dve, "tensor_tensor", R=["ps4", "sn_t"], W=["r2a"], out=r2[:, 0:32], in0=ps[4][:, 32:64], in1=sn_t[:, ti, 0:32], op=ALU.mult)
        A.op(dve, "tensor_tensor", R=["ps4", "sn_t"], W=["r2b"], out=r2[:, 32:64], in0=ps[4][:, 0:32], in1=sn_t[:, ti, 32:64], op=ALU.mult)
        A.op(dve, "tensor_tensor", R=["r1", "r2a", "r2b"], W=[f"kpe32_{p}"], out=kpe32[p][:], in0=r1[:], in1=r2[:], op=ALU.add)
        if kind == "m":
            A.dma(sp, B.kpe_main[cols, :], kpe32[p][:], R=[f"kpe32_{p}"], W=[], stream=f"okp{p}")
        A.mm(["ps7"], [f"kpe32_{p}"], [TR(ps[7][0:64, 0:128], kpe32[p][:, 0:64], B.ident[:])])
        A.op(act, "copy", R=["ps7"], W=[("kpeT", kidx)], out=B.kpeT[0:64, kcol:kcol + 128], in_=ps[7][0:64, 0:128])
        if kind == "m":
            rs, rk = rms_rstd_a(B, ps[2 + p][:, 0:512], f"ps{2 + p}", junk[:], st, 51 + 3 * k, QR)
            A.op(dve, "scalar_tensor_tensor", R=[f"ps{2 + p}", rk, "gq_b"], W=[f"qn32_{p}"], out=qn32[p][:], in0=ps[2 + p][:, 0:512],
                 scalar=rs, in1=gq_b[:], op0=ALU.mult, op1=ALU.mult)
            A.mm(["ps6"], [f"qn32_{p}"], [TR(ps[6][:, cc * 128:(cc + 1) * 128], qn32[p][:, cc * 128:(cc + 1) * 128], B.ident[:])
                                         for cc in range(4)])
            A.op(act, "copy", R=["ps6"], W=[("qnT", t)], out=B.qnT[:, :, cols], in_=ps[6][:].rearrange("p (a b) -> p a b", b=128))
    if B.debug and B.stop_after == 2:
        B.dbg("ckvT", B.ckvT[:], BF16)
        B.dbg("kpeT", B.kpeT[:], BF16)
        B.dbg("qnT", B.qnT[:], BF16)


def phase3(B):
    nc, K, A = B.nc, B.K, B.A
    pe, act, dve, pool, sp = K.pe, K.act, K.dve, K.pool, K.sp
    ps = K.ps
    tmp = B.bump([(OFF_OB, ARENA_BYTES), (OFF_C, OFF_LAT)])
    Sctx = tmp("Sctx", [128, 8, 128], F32)
    wf2 = [tmp(f"wf{i}", [128, 16, 128], BF16) for i in range(2)]
    wi2 = [tmp(f"wi{i}", [128, 16, 128], BF16) for i in range(2)]
    sig = tmp("sig", [128, 512], F32)
    logf = tmp("logf", [128, 512], F32)
    bT = tmp("bT", [128, 512], F32)
    kT = tmp("kT", [128, 512], F32)
    e1 = tmp("e1", [128, 512], F32)
    ke32 = tmp("ke32", [128, 512], F32)
    kdT = tmp("kdT", [128, 512], F32)
    kd_tm = tmp("kd_tm", [128, 4, 128], BF16)
    v_tm = tmp("v_tm", [128, 4, 128], BF16)
    ebl = tmp("ebl", [128, 16], F32)
    S32 = tmp("S32", [128, 128], F32)
    S_bf = tmp("S_bf", [128, 128], BF16)
    cm64 = tmp("cm64", [128, 512], F32)
    cm8 = tmp("cm8", [128, 128], F32)
    wq2 = [tmp(f"wq{i}", [128, 16, 128], BF16) for i in range(2)]
    wg2 = [tmp(f"wg{i}", [128, 16, 128], BF16) for i in range(2)]
    qT = tmp("qT", [128, 512], F32)
    qeT = tmp("qeT", [128, 512], BF16)
    keT = tmp("keT", [128, 512], BF16)
    AT = [tmp(f"AT{i}", [128, 128], BF16) for i in range(2)]
    oT = tmp("oT", [128, 512], F32)
    sq = tmp("sq", [128, 512], BF16)
    rstd = tmp("rstd", [128, 512], F32)
    sg = tmp("sg", [128, 512], F32)
    t1 = tmp("t1", [128, 512], F32)
    Sin32 = tmp("Sin32", [128, 16, 128], F32)
    Sin_bf = tmp("Sin_bf", [128, 16, 128], BF16)
    Sout32 = [tmp(f"Sout32_{i}", [128, 128], F32) for i in range(2)]
    kdm = [tmp(f"kdm{i}", [128, 128], BF16) for i in range(2)]
    wv = B.w_in.rearrange("(c p) n -> p c n", p=128)
    M64 = B.masks[:, 0:128]
    M8 = B.masks[:, 128:256]
    IND = B.masks[:, 760:776]
    ghT = B.vecT[:, 88:96]

    A.op(dve, "memset", W=["cm64"], ap=cm64[:], constant=1.0)
    A.op(dve, "memset", W=["cm64"], ap=cm64[:].rearrange("p (c t) -> p c t", t=64)[:, :, 0:1], constant=0.0)
    A.op(dve, "memset", W=["cm8"], ap=cm8[:], constant=1.0)
    A.op(dve, "memset", W=["cm8"], ap=cm8[:].rearrange("p (c t) -> p c t", t=8)[:, :, 0:1], constant=0.0)

    def ldw(dst, key, col0):
        for hh in range(2):
            A.dma(pool, dst[:, hh * 8:(hh + 1) * 8, :], wv[:, hh * 8:(hh + 1) * 8, col0:col0 + 128], R=[], W=[(key, hh)],
                  stream=f"{key}{hh}")
        return [(key, 0), (key, 1)]

    def proj_T(bank, w, wk, hsrc, c0, n):
        A.mm([f"ps{bank}"], wk, [MM(ps[bank][:, 0:n], w[:, dc, :], hsrc[:, dc, c0:c0 + n], dc == 0, dc == 15) for dc in range(16)])

    for is_ctx in (True, False):
        hsrc = B.hcT if is_ctx else B.hT
        blks = CBLKS if is_ctx else BLKS
        def ldhead(hh_):
            s_ = hh_ % 2
            ks = [ldw(wf2[s_], f"wf{s_}", O_FA + hh_ * 128), ldw(wi2[s_], f"wi{s_}", O_IA + hh_ * 128)]
            if not is_ctx:
                ks += [ldw(wq2[s_], f"wq{s_}", O_QA + hh_ * 128), ldw(wg2[s_], f"wg{s_}", O_GA + hh_ * 128)]
            return ks
        nxt = ldhead(0)
        for h in range(NH):
            cur = nxt
            if h + 1 < NH:
                nxt = ldhead(h + 1)
            wf, wi = wf2[h % 2], wi2[h % 2]
            wfk, wik = cur[0], cur[1]
            if not is_ctx:
                wq, wg = wq2[h % 2], wg2[h % 2]
                wqk, wgk = cur[2], cur[3]
            if is_ctx:
                A.op(dve, "memset", W=["S32"], ap=S32[:], constant=0.0)
            else:
                A.op(dve, "tensor_copy", R=[("Sctx", h)], W=["S32"], out=S32[:], in_=Sctx[:, h, :])
                A.op(act, "copy", R=["S32"], W=["S_bf"], out=S_bf[:], in_=S32[:])
            for (c0, n) in blks:
                sample = (c0 == NTP) and not is_ctx
                tiles = n // 128
                csz = 8 if sample else 64
                nch = n // csz
                cm = cm8 if sample else cm64
                proj_T(0, wf, wfk, hsrc, c0, n)
                A.op(act, "activation", R=["ps0"], W=["sig"], out=sig[:, 0:n], in_=ps[0][:, 0:n], func=AF.Sigmoid)
                A.op(dve, "tensor_scalar", R=["sig"], W=["kT"], out=kT[:, 0:n], in0=sig[:, 0:n], scalar1=B.omlT[:, h:h + 1],
                     scalar2=B.lbT[:, h:h + 1], op0=ALU.mult, op1=ALU.add)
                A.op(act, "activation", R=["kT"], W=["logf"], out=logf[:, 0:n], in_=kT[:, 0:n], func=AF.Ln)
                A.op(dve, "tensor_scalar", R=["sig", "logf"], W=["kT"], out=kT[:, 0:n], in0=sig[:, 0:n], scalar1=B.nomlT[:, h:h + 1],
                     scalar2=B.omlT[:, h:h + 1], op0=ALU.mult, op1=ALU.add)
                A.op(dve, "tensor_tensor_scan", R=["logf", "cm64", "cm8"], W=["bT"], out=bT[:, 0:n], data0=cm[:, 0:n],
                     data1=logf[:, 0:n], initial=0.0, op0=ALU.mult, op1=ALU.add)
                A.op(act, "activation", R=["bT"], W=["ebl"], out=ebl[:, 0:nch],
                     in_=bT[:, 0:n].rearrange("p (c t) -> p c t", t=csz)[:, :, csz - 1], func=AF.Exp)
                A.op(act, "activation", R=["bT"], W=["e1"], out=e1[:, 0:n], in_=bT[:, 0:n], func=AF.Exp, scale=-1.0)
                A.op(dve, "tensor_tensor", R=["kT", "e1"], W=["ke32"], out=ke32[:, 0:n], in0=kT[:, 0:n], in1=e1[:, 0:n], op=ALU.mult)
                A.op(dve, "tensor_tensor", R=["ke32", "ebl"], W=["kdT"], out=kdT[:, 0:n].rearrange("p (c t) -> p c t", t=csz),
                     in0=ke32[:, 0:n].rearrange("p (c t) -> p c t", t=csz),
                     in1=ebl[:, 0:nch].unsqueeze(2).broadcast_to([128, nch, csz]), op=ALU.mult)
                if not is_ctx:
                    A.op(pool, "tensor_copy", R=["ke32"], W=["keT"], out=keT[:, 0:n], in_=ke32[:, 0:n])
                    proj_T(1, wq, wqk, hsrc, c0, n)
                    A.op(act, "activation", R=["ps1"], W=["qT"], out=qT[:, 0:n], in_=ps[1][:, 0:n], func=AF.Silu)
                    A.op(act, "activation", R=["bT"], W=["e1"], out=e1[:, 0:n], in_=bT[:, 0:n], func=AF.Exp)
                    A.op(dve, "tensor_tensor", R=["qT", "e1"], W=["qeT"], out=qeT[:, 0:n], in0=qT[:, 0:n], in1=e1[:, 0:n], op=ALU.mult)
                    proj_T(2, wg, wgk, hsrc, c0, n)
                    A.op(act, "activation", R=["ps2"], W=["sg"], out=sg[:, 0:n], in_=ps[2][:, 0:n], func=AF.Sigmoid)
                for t in range(tiles):
                    A.mm(["ps3"], wik, [MM(ps[3][:, t * 128:(t + 1) * 128], hsrc[:, dc, c0 + t * 128:c0 + (t + 1) * 128], wi[:, dc, :],
                                           dc == 0, dc == 15) for dc in range(16)])
                A.op(act, "copy", R=["ps3"], W=["v_tm"], out=v_tm[:, 0:tiles, :], in_=ps[3][:, 0:n].rearrange("p (a b) -> p a b", b=128))
                A.mm(["ps4"], ["kdT"], [TR(ps[4][:, t * 128:(t + 1) * 128], kdT[:, t * 128:(t + 1) * 128], B.ident[:]) for t in range(tiles)])
                A.op(dve, "tensor_copy", R=["ps4"], W=["kd_tm"], out=kd_tm[:, 0:tiles, :], in_=ps[4][:, 0:n].rearrange("p (a b) -> p a b", b=128))
                if not sample:
                    for t in range(tiles):
                        tc_ = slice(t * 128, (t + 1) * 128)
                        if not is_ctx:
                            A.mm(["ps5"], ["keT", "qeT"], [MM(ps[5][:, 0:128], keT[:, tc_], qeT[:, tc_], True, True)])
                            A.op(dve, "tensor_tensor", R=["ps5"], W=[f"AT{t % 2}"], out=AT[t % 2][:], in0=ps[5][:, 0:128], in1=M64, op=ALU.mult)
                        for j in range(2):
                            ch = t * 2 + j
                            cc_ = slice(t * 128 + j * 64, t * 128 + (j + 1) * 64)
                            if not is_ctx:
                                A.mm(["ps6"], ["S_bf", "qeT"], [MM(ps[6][:, cc_], S_bf[:], qeT[:, cc_], j == 0, False)])
                            A.mm(["ps7"], ["kd_tm", "v_tm"], [MM(ps[7][:, 0:128], kd_tm[j * 64:(j + 1) * 64, t, :], v_tm[j * 64:(j + 1) * 64, t, :],
                                                                True, True)])
                            A.op(dve, "scalar_tensor_tensor", R=["ps7", "ebl", "S32"], W=["S32"], out=S32[:], in0=S32[:], scalar=ebl[:, ch:ch + 1],
                                 in1=ps[7][:, 0:128], op0=ALU.mult, op1=ALU.add)
                            if not is_ctx:
                                A.op(act, "copy", R=["S32"], W=["S_bf"], out=S_bf[:], in_=S32[:])
                        if not is_ctx:
                            A.mm(["ps6"], ["v_tm", f"AT{t % 2}"], [MM(ps[6][:, tc_], v_tm[:, t, :], AT[t % 2][:], False, True)])
                    if not is_ctx and c0 + n == NTP:
                        A.dma(sp, B.s_prompt[h], S32[:], R=["S32"], W=[], stream="osp")
                else:
                    A.dma(sp, Sin32[:], B.state_in[:, h].rearrange("b k v -> k b v"), R=[], W=["Sin32"], stream="sin")
                    A.op(pool, "tensor_copy", R=["Sin32"], W=["Sin_bf"], out=Sin_bf[:], in_=Sin32[:])
                    A.mm(["ps5"], ["keT", "qeT"], [MM(ps[5][:, 0:128], keT[:, 0:128], qeT[:, 0:128], True, True)])
                    A.op(dve, "tensor_tensor", R=["ps5"], W=["AT0"], out=AT[0][:], in0=ps[5][:, 0:128], in1=M8, op=ALU.mult)
                    A.mm(["ps6"], ["Sin_bf", "qeT"], [MM(ps[6][:, b * 8:(b + 1) * 8], Sin_bf[:, b, :], qeT[:, b * 8:(b + 1) * 8], b == 0, False)
                                                      for b in range(NB)])
                    A.mm(["ps6"], ["v_tm", "AT0"], [MM(ps[6][:, 0:128], v_tm[:, 0, :], AT[0][:], False, True)])
                    for b in range(NB):
                        pb = 7 if b % 2 == 0 else 4
                        A.op(dve, "tensor_scalar", R=["kd_tm"], W=[f"kdm{b % 2}"], out=kdm[b % 2][:], in0=kd_tm[:, 0, :], scalar1=IND[:, b:b + 1],
                             scalar2=None, op0=ALU.mult)
                        A.mm([f"ps{pb}"], [f"kdm{b % 2}", "v_tm"], [MM(ps[pb][:, 0:128], kdm[b % 2][:], v_tm[:, 0, :], True, True)])
                        A.op(dve, "scalar_tensor_tensor", R=[f"ps{pb}", "ebl", "Sin32"], W=[f"Sout32_{b % 2}"], out=Sout32[b % 2][:], in0=Sin32[:, b, :],
                             scalar=ebl[:, b:b + 1], in1=ps[pb][:, 0:128], op0=ALU.mult, op1=ALU.add)
                        A.dma(sp, B.s_sample[b, h], Sout32[b % 2][:], R=[f"Sout32_{b % 2}"], W=[], stream=f"oss{b % 2}")
                if not is_ctx:
                    A.op(act, "copy", R=["ps6"], W=["oT"], out=oT[:, 0:n], in_=ps[6][:, 0:n])
                    A.op(pool, "tensor_tensor", R=["oT"], W=["sq"], out=sq[:, 0:n], in0=oT[:, 0:n], in1=oT[:, 0:n], op=ALU.mult)
                    A.mm(["ps0"], ["sq"], [MM(ps[0][:, 0:n], B.ones_bf[:], sq[:, 0:n], True, True)])
                    A.op(act, "activation", R=["ps0"], W=["rstd"], out=rstd[:, 0:n], in_=ps[0][:, 0:n], func=AF.Sqrt, scale=1.0 / DV,
                         bias=B.eps_t[:, 0:1])
                    A.op(dve, "reciprocal", R=["rstd"], W=["rstd"], out=rstd[:, 0:n], in_=rstd[:, 0:n])
                    A.op(dve, "scalar_tensor_tensor", R=["oT", "rstd"], W=["t1"], out=t1[:, 0:n], in0=oT[:, 0:n], scalar=ghT[:, h:h + 1],
                         in1=rstd[:, 0:n], op0=ALU.mult, op1=ALU.mult)
                    A.op(dve, "tensor_tensor", R=["t1", "sg"], W=[("oaT", h, c0)], out=B.oaT[:, h, c0:c0 + n], in0=t1[:, 0:n], in1=sg[:, 0:n],
                         op=ALU.mult)
            if is_ctx:
                A.op(dve, "tensor_scalar", R=["S32"], W=[("Sctx", h)], out=Sctx[:, h, :], in0=S32[:], scalar1=B.cflag[:, 0:1], scalar2=None,
                     op0=ALU.mult)
    if B.debug and B.stop_after == 3:
        B.dbg("oaT", B.oaT[:], BF16)


def phase4(B):
    nc, K, A = B.nc, B.K, B.A
    pe, act, dve, pool, sp = K.pe, K.act, K.dve, K.pool, K.sp
    ps = K.ps
    psb = [p[:].bitcast(BF16) for p in ps]
    regions = [(OFF_C, OFF_LAT), (OFF_T, ARENA_BYTES)]
    tmp = B.bump(regions)
    wkv = tmp("wkv", [128, 4, 2048], BF16)
    qsT = tmp("qsT", [128, 5, NB, 64], BF16)
    OFF_4B = OFF_C + 2 * 4 * 2048 + 2 * 5 * NB * 64
    r1 = tmp("r1q", [64, 512], F32)
    r2 = tmp("r2q", [64, 512], F32)
    wqh = tmp("wqh", [128, 4, 192], BF16)
    tmp = B.bump([(OFF_T, ARENA_BYTES)])
    w_ukT = tmp("w_ukT", [128, 8, 512], BF16)
    q_nopeT = tmp("q_nopeT", [128, NT], BF16)
    q_peT = tmp("q_peT", [128, NT], BF16)
    csT = tmp("csT", [64, NT], F32)
    snT = tmp("snT", [64, NT], F32)
    q_latT = tmp("q_latT", [128, 4, 512], BF16)
    pT = [tmp(f"pT{i}", [128, 512], BF16) for i in range(2)]
    olat_sb = tmp("olat_sb", [128, 4, 512], BF16)
    rden = tmp("rden", [128, 512], F32)
    MC = B.masks[:, 256:384]
    LAT = ["lat"]

    wkv_v = B.w_kv_up.rearrange("(c p) n -> p c n", p=128)
    for cc in range(4):
        for q4 in range(4):
            A.dma(pool, wkv[:, cc, q4 * 512:(q4 + 1) * 512], wkv_v[:, cc, q4 * 512:(q4 + 1) * 512], R=[], W=[("wkv", cc, q4)], stream=f"wkv{q4}")
    WKV = [("wkv", cc, q4) for cc in range(4) for q4 in range(4)]
    A.dma_group(sp, [(csT[:], B.ropeT_cs), (snT[:], B.ropeT_sn)], R=[], W=["csT", "snT"], stream="c0")
    A.op(dve, "memset", W=["q_peT1"], ap=q_peT[64:65, :], constant=1.0)
    for h in range(NH):
        A.mm(["ps7"], WKV, [TR(psb[7][:, cc * 128:(cc + 1) * 128], wkv[:, cc, h * 256:h * 256 + 128], B.ident_bf[:]) for cc in range(4)])
        A.op(act if h % 2 else dve, "copy" if h % 2 else "tensor_copy", R=["ps7"], W=[("w_ukT", h)], out=w_ukT[:, h, :], in_=psb[7][:, 0:512])
    wq_v = B.w_q_up.rearrange("(c p) n -> p c n", p=128)
    for h in range(NH):
        A.dma(pool, wqh[:], wq_v[:, :, h * 192:(h + 1) * 192], R=[], W=["wqh"], stream="wqh")
        for bi, (c0, n) in enumerate(BLKS):
            cols = slice(c0, c0 + n)
            A.mm(["ps7"], ["wqh"], [MM(ps[7][:, 0:n], wqh[:, cc, 0:128], B.qnT[:, cc, cols], cc == 0, cc == 3) for cc in range(4)])
            A.op(act, "copy", R=["ps7"], W=[("q_nopeT", bi)], out=q_nopeT[:, cols], in_=ps[7][:, 0:n])
            A.mm(["ps5"], ["wqh"], [MM(ps[5][0:64, 0:n], wqh[:, cc, 128:192], B.qnT[:, cc, cols], cc == 0, cc == 3) for cc in range(4)])
            A.mm(["ps6"], ["wqh"], [MM(ps[6][0:32, 0:n], wqh[:, cc, 160:192], B.qnT[:, cc, cols], cc == 0, cc == 3) for cc in range(4)]
                 + [MM(ps[6][32:64, 0:n], wqh[:, cc, 128:160], B.qnT[:, cc, cols], cc == 0, cc == 3) for cc in range(4)])
            A.op(dve, "tensor_tensor", R=["ps5", "csT"], W=["r1q"], out=r1[:, 0:n], in0=ps[5][0:64, 0:n], in1=csT[:, cols], op=ALU.mult)
            A.op(dve, "tensor_tensor", R=["ps6", "snT"], W=["r2q"], out=r2[:, 0:n], in0=ps[6][0:64, 0:n], in1=snT[:, cols], op=ALU.mult)
            A.op(dve, "tensor_tensor", R=["r1q", "r2q"], W=[("q_peT", bi)], out=q_peT[0:64, cols], in0=r1[:, 0:n], in1=r2[:, 0:n], op=ALU.add)
        for cc in range(4):
            A.mm(["ps7"], [("w_ukT", h), ("q_nopeT", 2)], [MM(ps[7][:, 0:128], w_ukT[:, h, cc * 128:(cc + 1) * 128], q_nopeT[:, NTP:NT], True, True)])
            A.op(act if cc % 2 else dve, "copy" if cc % 2 else "tensor_copy", R=["ps7"], W=[("qsT", h)], out=qsT[:, cc, :, h * 8:(h + 1) * 8],
                 in_=ps[7][:, 0:128].rearrange("p (b t) -> p b t", t=TS))
        A.op(dve, "tensor_copy", R=[("q_peT", 2)], W=[("qsT", h)], out=qsT[0:64, 4, :, h * 8:(h + 1) * 8],
             in_=q_peT[0:64, NTP:NT].rearrange("p (b t) -> p b t", t=TS))
        for qb in range(2):
            c0 = qb * 512
            for cc in range(4):
                A.mm(["ps7"], [("w_ukT", h), ("q_nopeT", qb)], [MM(ps[7][:, 0:512], w_ukT[:, h, cc * 128:(cc + 1) * 128], q_nopeT[:, c0:c0 + 512], True, True)])
                A.op(act if cc % 2 else dve, "copy" if cc % 2 else "tensor_copy", R=["ps7"], W=["q_latT"], out=q_latT[:, cc, :], in_=ps[7][:, 0:512])
            keys = [(u, 0, False) for u in range(8)]
            for j in range(4 * qb + 4):
                keys.append((8 + j, 128 * max(0, j - 4 * qb), j >= 4 * qb))
            def st_score(i):
                kidx, cs, diag = keys[i]
                sbk = 5 + i % 2
                kc = slice(kidx * 128, (kidx + 1) * 128)
                A.mm([f"ps{sbk}"], LAT + ["q_latT", ("q_peT", qb), "q_peT1"],
                     [MM(ps[sbk][:, cs:512], B.ckvT[:, cc, kc], q_latT[:, cc, cs:512], cc == 0, False) for cc in range(4)]
                     + [MM(ps[sbk][:, cs:512], B.kpeT[0:65, kc], q_peT[0:65, c0 + cs:c0 + 512], False, True)])
                A.op(act, "activation", R=[f"ps{sbk}"], W=[f"pT{i % 2}"], out=pT[i % 2][:, cs:512], in_=ps[sbk][:, cs:512], func=AF.Exp, scale=SCALE)
                if diag:
                    A.op(pool, "tensor_tensor", R=[f"pT{i % 2}"], W=[f"pT{i % 2}"], out=pT[i % 2][:, cs:cs + 128], in0=pT[i % 2][:, cs:cs + 128],
                         in1=MC, op=ALU.mult)

            def st_pv(i):
                kidx, cs, diag = keys[i]
                last = i == len(keys) - 1
                for cc in range(4):
                    A.mm([f"ps{cc}"], LAT + [f"pT{i % 2}"], [MM(ps[cc][:, cs:512], B.ckv_tm[:, kidx, cc * 128:(cc + 1) * 128], pT[i % 2][:, cs:512], i == 0, last)])
                A.mm(["ps4"], [f"pT{i % 2}"], [MM(ps[4][:, cs:512], B.ones_bf[:], pT[i % 2][:, cs:512], i == 0, last)])

            for i in range(-1, len(keys)):
                if i + 1 < len(keys):
                    st_score(i + 1)
                if i >= 0:
                    st_pv(i)
            for cc in range(4):
                A.op(act if cc % 2 else dve, "copy" if cc % 2 else "tensor_copy", R=[f"ps{cc}"], W=["olat_sb"], out=olat_sb[:, cc, :], in_=ps[cc][:, 0:512])
            A.op(dve, "reciprocal", R=["ps4"], W=["rden"], out=rden[:], in_=ps[4][:, 0:512])
            A.mm(["ps7"], ["olat_sb"] + WKV, [MM(ps[7][:, 0:512], wkv[:, cc, h * 256 + 128:h * 256 + 256], olat_sb[:, cc, :], cc == 0, cc == 3) for cc in range(4)])
            A.op(dve, "tensor_tensor", R=["ps7", "rden"], W=[("obT", h, qb)], out=B.obT[:, h, c0:c0 + 512], in0=ps[7][:, 0:512], in1=rden[:], op=ALU.mult)

    K.barrier()
    A.reset()
    tmp = B.bump([(OFF_4B, OFF_LAT), (OFF_T, ARENA_BYTES)])
    idx_i = tmp("idx_i", [128, NB * NPAGES], I32)
    ptab_i = tmp("ptab_i", [128, NB * NPAGES], I32)
    ptab_f = tmp("ptab_f", [128, NB * NPAGES], F32)
    smask = tmp("smask", [128, NB * 64], F32)
    NSLOT = 4
    pg32 = [tmp(f"pg32_{i}", [128, 576], F32) for i in range(NSLOT)]
    pg_bf = [tmp(f"pg_bf{i}", [128, 576], BF16) for i in range(NSLOT)]
    KT_pg = [tmp(f"KT_pg{i}", [128, 640], BF16) for i in range(2)]
    pTs = [tmp(f"pTs{i}", [128, 64], BF16) for i in range(2)]
    olat_s = tmp("olat_s", [64, 512], BF16)
    olatT_s = tmp("olatT_s", [128, 256], BF16)
    rden_s = tmp("rden_s", [64, 1], F32)
    PIOTA = B.masks[:, 800:801]
    A.dma_group(sp, [(ptab_i[:], B.ptab.partition_broadcast(128)), (smask[:], B.smask_d)], R=[], W=["ptab_i", "smask"], stream="c0")
    A.op(dve, "tensor_copy", R=["ptab_i"], W=["ptab_f"], out=ptab_f[:], in_=ptab_i[:])
    A.op(dve, "tensor_scalar", R=["ptab_f"], W=["idx_i"], out=idx_i[:], in0=ptab_f[:], scalar1=128.0, scalar2=PIOTA, op0=ALU.mult, op1=ALU.add)
    pages = [(b, j) for b in range(NB) for j in range(NPAGES + 1)]
    NPG = len(pages)

    def stA(n):
        b, j = pages[n]
        if j == NPAGES:
            return
        sl = n % NSLOT
        col = b * NPAGES + j
        A.dma(pool, pg32[sl][:], B.cache_all, R=["idx_i"], W=[("pg32", sl)], stream=f"pgc{sl}", indirect=idx_i[:, col:col + 1])

    def stB(n):
        b, j = pages[n]
        if j == NPAGES:
            return
        sl = n % NSLOT
        A.op(dve, "tensor_copy", R=[("pg32", sl)], W=[("pg_bf", sl)], out=pg_bf[sl][:], in_=pg32[sl][:])

    def stC(n):
        b, j = pages[n]
        if j == NPAGES:
            return
        sl, s2 = n % NSLOT, n % 2
        tb = 2 + s2
        A.mm([f"ps{tb}"], [("pg_bf", sl)], [TR(psb[tb][:, i * 128:(i + 1) * 128], pg_bf[sl][:, i * 128:(i + 1) * 128], B.ident_bf[:]) for i in range(4)]
             + [TR(psb[tb][0:64, 512:640], pg_bf[sl][:, 512:576], B.ident_bf[:])])
        A.op(act, "copy", R=[f"ps{tb}"], W=[("KT", s2, 0)], out=KT_pg[s2][:, 0:512], in_=psb[tb][:, 0:512])
        A.op(act, "copy", R=[f"ps{tb}"], W=[("KT", s2, 1)], out=KT_pg[s2][0:64, 512:640], in_=psb[tb][0:64, 512:640])

    def stD(n):
        b, j = pages[n]
        s2 = n % 2
        qv = [qsT[:, i, b, :] for i in range(4)]
        qp = qsT[0:64, 4, b, :]
        if j < NPAGES:
            kts = [KT_pg[s2][:, i * 128:(i + 1) * 128] for i in range(4)]
            ktp = KT_pg[s2][0:64, 512:640]
            Rk = [("KT", s2, 0), ("KT", s2, 1)]
        else:
            kts = [B.ckvT[:, i, NCX + NTP:NKEY] for i in range(4)]
            ktp = B.kpeT[0:64, NCX + NTP:NKEY]
            Rk = []
        sbk = 4 + s2
        A.mm([f"ps{sbk}"], Rk, [MM(ps[sbk][:, 0:64], kts[i], qv[i], i == 0, False) for i in range(4)] + [MM(ps[sbk][:, 0:64], ktp, qp, False, True)])
        A.op(act, "activation", R=[f"ps{sbk}"], W=[f"pTs{s2}"], out=pTs[s2][:], in_=ps[sbk][:, 0:64], func=AF.Exp, scale=SCALE)
        if j == NPAGES:
            A.op(dve, "tensor_tensor", R=[f"pTs{s2}", "smask"], W=[f"pTs{s2}"], out=pTs[s2][:], in0=pTs[s2][:], in1=smask[:, b * 64:(b + 1) * 64], op=ALU.mult)

    def stE(n):
        b, j = pages[n]
        sl, s2 = n % NSLOT, n % 2
        if j < NPAGES:
            V = pg_bf[sl][:, 0:512]
            Rv = [("pg_bf", sl)]
        else:
            V = B.ckv_tm[:, 16, :]
            Rv = []
        A.mm(["ps0"], Rv + [f"pTs{s2}"], [MM(ps[0][0:64, 0:512], pTs[s2][:], V, j == 0, j == NPAGES)])
        A.mm(["ps1"], [f"pTs{s2}"], [MM(ps[1][0:64, 0:1], pTs[s2][:], B.ones_bf[:, 0:1], j == 0, j == NPAGES)])
        if j < NPAGES:
            return
        A.op(dve, "reciprocal", R=["ps1"], W=["rden_s"], out=rden_s[:], in_=ps[1][0:64, 0:1])
        A.op(dve, "tensor_scalar", R=["ps0", "rden_s"], W=["olat_s"], out=olat_s[:], in0=ps[0][0:64, 0:512], scalar1=rden_s[:, 0:1], scalar2=None, op0=ALU.mult)
        A.mm(["ps6"], ["olat_s"], [TR(psb[6][:, cc * 64:(cc + 1) * 64], olat_s[0:64, cc * 128:(cc + 1) * 128], B.ident_bf[0:64, 0:64]) for cc in range(4)])
        A.op(act, "copy", R=["ps6"], W=["olatT_s"], out=olatT_s[:], in_=psb[6][:, 0:256])
        A.mm(["ps7"], ["olatT_s"], [MM(ps[7][:, h * 8:(h + 1) * 8], wkv[:, cc, h * 256 + 128:h * 256 + 256], olatT_s[:, cc * 64 + h * 8:cc * 64 + (h + 1) * 8],
                                       (h == 0 and cc == 0), (h == NH - 1 and cc == 3)) for h in range(NH) for cc in range(4)])
        A.op(dve, "tensor_copy", R=["ps7"], W=[("obT_s", b)], out=B.obT[:, :, NTP + b * 8:NTP + (b + 1) * 8],
             in_=ps[7][:, 0:64].rearrange("p (h t) -> p h t", t=TS))

    for i in range(-6, NPG):
        for (st_, off) in ((stA, 6), (stB, 3), (stC, 2), (stD, 1), (stE, 0)):
            if 0 <= i + off < NPG:
                st_(i + off)
    if B.debug and B.stop_after == 4:
        B.dbg("obT", B.obT[:], BF16)


OFF_M = OFF_LAT
OFF_H2 = OFF_LAT + 36864


def build_G_tm(B, G_fm, GB, G_tm, sel, sel_lo, tag):
    K, A = B.K, B.A
    ps = K.ps
    for cb in range(4):
        A.mm([f"ps{cb}"], [tag + "GB", "sel"], [MM(ps[cb][:, 0:512], sel[0:17, sel_lo:sel_lo + 128], GB[0:17, cb * 512:(cb + 1) * 512], True, True)])
        A.op(K.act if cb % 2 else K.dve, "copy" if cb % 2 else "tensor_copy", R=[f"ps{cb}"], W=["G_tm"], out=G_tm[:, cb * 512:(cb + 1) * 512], in_=ps[cb][:, 0:512])


def build_GB(B, G_fm, GB, tag):
    K, A = B.K, B.A
    ps = K.ps
    for cb in range(4):
        A.mm([f"ps{cb}"], [], [TR(ps[cb][0:17, j * 128:(j + 1) * 128], G_fm[:, cb * 4 + j, :], B.ident[:]) for j in range(4)])
        A.op(K.act if cb % 2 else K.dve, "copy" if cb % 2 else "tensor_copy", R=[f"ps{cb}"], W=[tag + "GB"], out=GB[0:17, cb * 512:(cb + 1) * 512], in_=ps[cb][0:17, 0:512])


def phase5(B):
    nc, K, A = B.nc, B.K, B.A
    pe, act, dve, pool, sp = K.pe, K.act, K.dve, K.pool, K.sp
    ps = K.ps
    mT = B.view(OFF_M, [128, 16, NT], BF16)
    h2T = B.view(OFF_H2, [128, 16, NT], BF16)
    tw = B.bump([(OFF_C, OFF_LAT)])
    wa = [tw(f"wa{i}", [128, 8, 128], BF16) for i in range(2)]
    wb = [tw(f"wb{i}", [128, 8, 128], BF16) for i in range(2)]
    wga = [tw(f"wga{i}", [128, 16, 128], BF16) for i in range(2)]
    wgb = [tw(f"wgb{i}", [128, 16, 128], BF16) for i in range(2)]
    tt = B.bump([(OFF_T, ARENA_BYTES)])
    sga = [tt(f"sga{i}", [128, 512], F32) for i in range(2)]
    sgb = [tt(f"sgb{i}", [128, 512], F32) for i in range(2)]
    t1 = [tt(f"t1_{i}", [128, 512], F32) for i in range(2)]
    t2 = [tt(f"t2_{i}", [128, 512], F32) for i in range(2)]
    wv = B.w_in.rearrange("(c p) n -> p c n", p=128)
    wav = B.w_a_out.rearrange("(c p) n -> p c n", p=128)
    wbv = B.w_b_out.rearrange("(c p) n -> p c n", p=128)
    it = 0
    for fo in range(16):
        s = fo % 2
        fc = slice(fo * 128, (fo + 1) * 128)
        A.dma(pool, wa[s][:], wav[:, :, fc], R=[], W=[f"wa{s}"], stream=f"wa{s}")
        A.dma(pool, wb[s][:], wbv[:, :, fc], R=[], W=[f"wb{s}"], stream=f"wb{s}")
        for hh in range(2):
            A.dma(pool, wga[s][:, hh * 8:(hh + 1) * 8, :], wv[:, hh * 8:(hh + 1) * 8, O_GTA + fo * 128:O_GTA + (fo + 1) * 128], R=[], W=[(f"wga{s}", hh)],
                  stream=f"wga{s}{hh}")
            A.dma(pool, wgb[s][:, hh * 8:(hh + 1) * 8, :], wv[:, hh * 8:(hh + 1) * 8, O_GTB + fo * 128:O_GTB + (fo + 1) * 128], R=[], W=[(f"wgb{s}", hh)],
                  stream=f"wgb{s}{hh}")
        for (c0, n) in BLKS:
            p = it % 2
            it += 1
            b0 = 4 * p
            cols = slice(c0, c0 + n)
            A.mm([f"ps{b0}"], [f"wa{s}"], [MM(ps[b0][:, 0:n], wa[s][:, hc, :], B.oaT[:, hc, cols], hc == 0, hc == 7) for hc in range(8)])
            A.mm([f"ps{b0 + 1}"], [f"wb{s}"], [MM(ps[b0 + 1][:, 0:n], wb[s][:, hc, :], B.obT[:, hc, cols], hc == 0, hc == 7) for hc in range(8)])
            A.mm([f"ps{b0 + 2}"], [(f"wga{s}", 0), (f"wga{s}", 1)], [MM(ps[b0 + 2][:, 0:n], wga[s][:, dc, :], B.hT[:, dc, cols], dc == 0, dc == 15) for dc in range(16)])
            A.mm([f"ps{b0 + 3}"], [(f"wgb{s}", 0), (f"wgb{s}", 1)], [MM(ps[b0 + 3][:, 0:n], wgb[s][:, dc, :], B.hT[:, dc, cols], dc == 0, dc == 15) for dc in range(16)])
            A.op(act, "activation", R=[f"ps{b0 + 2}"], W=[f"sga{p}"], out=sga[p][:, 0:n], in_=ps[b0 + 2][:, 0:n], func=AF.Sigmoid)
            A.op(act, "activation", R=[f"ps{b0 + 3}"], W=[f"sgb{p}"], out=sgb[p][:, 0:n], in_=ps[b0 + 3][:, 0:n], func=AF.Sigmoid)
            A.op(dve, "tensor_tensor", R=[f"ps{b0}", f"sga{p}"], W=[f"t1_{p}"], out=t1[p][:, 0:n], in0=ps[b0][:, 0:n], in1=sga[p][:, 0:n], op=ALU.mult)
            A.op(dve, "tensor_tensor", R=[f"ps{b0 + 1}", f"sgb{p}"], W=[f"t2_{p}"], out=t2[p][:, 0:n], in0=ps[b0 + 1][:, 0:n], in1=sgb[p][:, 0:n], op=ALU.mult)
            A.op(pool, "tensor_tensor", R=[f"t1_{p}", f"t2_{p}"], W=[("mT", fo, c0)], out=mT[:, fo, cols], in0=t1[p][:, 0:n], in1=t2[p][:, 0:n], op=ALU.add)
    if B.debug and B.stop_after == 5:
        B.dbg("mT", mT[:], BF16)
    K.barrier()
    A.reset()
    wo = B.view(0, [128, 16, D], BF16)
    tmp = B.bump([(OFF_H2 + 36864, ARENA_BYTES)])
    G_tm = tmp("G_tm", [128, D], F32)
    x_t = tmp("x_t", [128, D], F32)
    x1_t = tmp("x1_t", [128, D], F32)
    xs_t = tmp("xs_t", [128, D], F32)
    GB = tmp("GB", [17, D], F32)
    sel = tmp("sel", [17, 256], F32)
    st = tmp("st5", [128, 16], F32)
    tS = tmp("tS5", [128, 4, 128], F32)
    junk = xs_t[:].bitcast(BF16)[:, 0:D]
    wov = B.w_o.rearrange("(c p) n -> p c n", p=128)
    for dc in range(16):
        for q4 in range(4):
            A.dma(pool, wo[:, dc, q4 * 512:(q4 + 1) * 512], wov[:, dc, q4 * 512:(q4 + 1) * 512], R=[], W=[("wo", dc, q4)], stream=f"wo{q4}")
    WO = [("wo", dc, q4) for dc in range(16) for q4 in range(4)]
    A.dma(sp, sel[:], B.sel_d, R=[], W=["sel"], stream="c0")
    build_GB(B, B.G1, GB, "g1")
    for t in range(9):
        cols = slice(t * 128, (t + 1) * 128)
        if t == 0 or t == 8:
            build_G_tm(B, B.G1, GB, G_tm, sel, 0 if t == 0 else 128, "g1")
        A.dma(sp, x_t[:], B.x_main[cols, :], R=[], W=["x_t"], stream="x_t")
        for cb in range(4):
            A.mm([f"ps{cb}"], WO, [MM(ps[cb][:, 0:512], mT[:, dc, cols], wo[:, dc, cb * 512:(cb + 1) * 512], dc == 0, dc == 15) for dc in range(16)])
        A.op(dve, "memset", W=["st5"], ap=st[:], constant=0.0)
        for cb in range(4):
            A.op(act, "activation", R=[f"ps{cb}"], W=["xs_t", "st5"], out=junk[:, 0:512], in_=ps[cb][:, 0:512], func=AF.Square, accum_out=st[:, cb:cb + 1])
        A.op(dve, "tensor_tensor", R=["st5"], W=["st5"], out=st[:, 4:5], in0=st[:, 0:1], in1=st[:, 1:2], op=ALU.add)
        A.op(dve, "tensor_tensor", R=["st5"], W=["st5"], out=st[:, 5:6], in0=st[:, 2:3], in1=st[:, 3:4], op=ALU.add)
        A.op(dve, "tensor_tensor", R=["st5"], W=["st5"], out=st[:, 6:7], in0=st[:, 4:5], in1=st[:, 5:6], op=ALU.add)
        A.op(act, "activation", R=["st5"], W=["st5"], out=st[:, 7:8], in_=st[:, 6:7], func=AF.Sqrt, scale=1.0 / D, bias=B.eps_t[:, 0:1])
        A.op(dve, "reciprocal", R=["st5"], W=["st5"], out=st[:, 8:9], in_=st[:, 7:8])
        for cb in range(4):
            cs_ = slice(cb * 512, (cb + 1) * 512)
            A.op(dve, "scalar_tensor_tensor", R=[f"ps{cb}", "st5", "G_tm"], W=[("x1_t", cb)], out=x1_t[:, cs_], in0=ps[cb][:, 0:512], scalar=st[:, 8:9],
                 in1=G_tm[:, cs_], op0=ALU.mult, op1=ALU.mult)
            A.op(pool, "tensor_tensor", R=[("x1_t", cb), "x_t"], W=[("x1_t", cb)], out=x1_t[:, cs_], in0=x1_t[:, cs_], in1=x_t[:, cs_], op=ALU.add)
        X1 = [("x1_t", cb) for cb in range(4)]
        A.dma(sp, B.y_main[cols, :], x1_t[:], R=X1, W=[], stream="x1o")
        A.op(act, "activation", R=X1, W=["xs_t", "st5"], out=junk[:], in_=x1_t[:], func=AF.Square, accum_out=st[:, 9:10])
        A.op(act, "activation", R=["st5"], W=["st5"], out=st[:, 10:11], in_=st[:, 9:10], func=AF.Sqrt, scale=1.0 / D, bias=B.eps_t[:, 0:1])
        A.op(dve, "reciprocal", R=["st5"], W=["st5"], out=st[:, 11:12], in_=st[:, 10:11])
        A.op(dve, "tensor_scalar", R=X1 + ["st5"], W=["xs_t"], out=xs_t[:], in0=x1_t[:], scalar1=st[:, 11:12], scalar2=None, op0=ALU.mult)
        for g in range(4):
            pb = 4 + g
            A.mm([f"ps{pb}"], ["xs_t"], [TR(ps[pb][:, j * 128:(j + 1) * 128], xs_t[:, (g * 4 + j) * 128:(g * 4 + j + 1) * 128], B.ident[:]) for j in range(4)])
            if t == 8:
                A.op(dve, "tensor_tensor", R=[f"ps{pb}"], W=["tS5"], out=tS[:].rearrange("p a (b t) -> p a b t", t=TS),
                     in0=ps[pb][:].rearrange("p (a b t) -> p a b t", a=4, t=TS),
                     in1=B.A2[:, g * 4:(g + 1) * 4, 1:17].unsqueeze(3).broadcast_to([128, 4, NB, TS]), op=ALU.mult)
                A.op(dve, "tensor_tensor", R=["tS5"], W=[("h2T", t, g)], out=h2T[:, g * 4:(g + 1) * 4, cols].rearrange("p a (b t) -> p a b t", t=TS),
                     in0=tS[:].rearrange("p a (b t) -> p a b t", t=TS),
                     in1=B.modT[:, 48 + g * 4:48 + (g + 1) * 4, 1:17].unsqueeze(3).broadcast_to([128, 4, NB, TS]), op=ALU.add)
            else:
                for j in range(4):
                    dc = g * 4 + j
                    if True:
                        A.op(dve, "tensor_scalar", R=[f"ps{pb}"], W=[("h2T", t, g, j)], out=h2T[:, dc, cols], in0=ps[pb][:, j * 128:(j + 1) * 128],
                             scalar1=B.A2[:, dc, 0:1], scalar2=B.modT[:, 48 + dc, 0:1], op0=ALU.mult, op1=ALU.add)
                    else:
                        A.op(act, "activation", R=[f"ps{pb}"], W=[("h2T", t, g, j)], out=h2T[:, dc, cols], in_=ps[pb][:, j * 128:(j + 1) * 128],
                             func=AF.Identity, scale=B.A2[:, dc, 0:1], bias=B.modT[:, 48 + dc, 0:1])
    if B.debug and B.stop_after == 5:
        B.dbg("h2T", h2T[:], BF16)


def phase6(B):
    nc, K, A = B.nc, B.K, B.A
    pe, act, dve, pool, sp = K.pe, K.act, K.dve, K.pool, K.sp
    ps = K.ps
    h2T = B.view(OFF_H2, [128, 16, NT], BF16)
    uT = B.view(0, [128, 64, 512], BF16)
    tmp = B.bump([(65536, OFF_H2), (OFF_H2 + 36864, ARENA_BYTES)])
    NWU, NWD = 3, 4
    wup = [tmp(f"wup{i}", [128, 16, 128], BF16) for i in range(NWU)]
    wd = [tmp(f"wd{i}", [128, 1024], BF16) for i in range(NWD)]
    GB = tmp("GB6", [17, D], F32)
    sel = tmp("sel6", [17, 256], F32)
    st = tmp("st6", [128, 16], F32)
    rl = [tmp(f"rl{i}", [128, 512], F32) for i in range(2)]
    rl3 = [tmp(f"rl3_{i}", [128, 128], F32) for i in range(2)]

    def uT3(ffc):
        return h2T[:, ffc // 4, (ffc % 4) * 128:(ffc % 4 + 1) * 128]
    junk = tmp("junk6", [128, D], BF16)
    m_tm = tmp("m_tm", [128, 4, D], F32)
    G_tm = tmp("G_tm6", [128, D], F32)
    x1_t = tmp("x1_t6", [128, D], F32)
    wuv = B.w_up.rearrange("(c p) n -> p c n", p=128)
    A.dma(sp, sel[:], B.sel_d, R=[], W=["sel"], stream="c0")
    build_GB(B, B.G2, GB, "g2")
    iu = 0
    idn = 0
    for bi, (c0, n) in enumerate(BLKS):
        tiles = n // 128
        cols = slice(c0, c0 + n)
        if bi == 0 or bi == 2:
            build_G_tm(B, B.G2, GB, G_tm, sel, 0 if bi == 0 else 128, "g2")
        for ffc in range(64 if bi < 2 else 0):
            s = iu % NWU
            pb = iu % 4
            iu += 1
            for hh in range(2):
                A.dma(pool, wup[s][:, hh * 8:(hh + 1) * 8, :], wuv[:, hh * 8:(hh + 1) * 8, ffc * 128:(ffc + 1) * 128], R=[], W=[(f"wup{s}", hh)],
                      stream=f"wup{s}{hh}")
            if bi == 0:
                A.mm([f"ps{pb}"], [(f"wup{s}", 0), (f"wup{s}", 1)], [MM(ps[pb][:, 0:n], wup[s][:, dc, :], h2T[:, dc, cols], dc == 0, dc == 15) for dc in range(16)])
            else:
                mms = []
                for dc in range(16):
                    mms.append(MM(ps[pb][:, 0:n], wup[s][:, dc, :], h2T[:, dc, cols], dc == 0, dc == 15))
                    mms.append(MM(ps[pb + 4][:, 0:NTS], wup[s][:, dc, :], h2T[:, dc, NTP:NTP + NTS], dc == 0, dc == 15))
                A.mm([f"ps{pb}", f"ps{pb + 4}"], [(f"wup{s}", 0), (f"wup{s}", 1)], mms)
            A.op(act, "activation", R=[f"ps{pb}"], W=[f"rl{pb % 2}"], out=rl[pb % 2][:, 0:n], in_=ps[pb][:, 0:n], func=AF.Relu)
            A.op(dve, "tensor_tensor", R=[f"rl{pb % 2}"], W=[("uT", ffc)], out=uT[:, ffc, 0:n], in0=rl[pb % 2][:, 0:n],
                 in1=rl[pb % 2][:, 0:n], op=ALU.mult)
            if bi == 1:
                A.op(act, "activation", R=[f"ps{pb + 4}"], W=[f"rl3_{pb % 2}"], out=rl3[pb % 2][:], in_=ps[pb + 4][:, 0:NTS], func=AF.Relu)
                A.op(dve, "tensor_tensor", R=[f"rl3_{pb % 2}"], W=[("uT3", ffc)], out=uT3(ffc), in0=rl3[pb % 2][:],
                     in1=rl3[pb % 2][:], op=ALU.mult)
        for half in range(2):
            banks = [f"ps{ti * 2 + cbk}" for ti in range(tiles) for cbk in range(2)]
            for ffc in range(64):
                s = idn % NWD
                idn += 1
                A.dma(pool, wd[s][:], B.w_down[ffc * 128:(ffc + 1) * 128, half * 1024:(half + 1) * 1024], R=[], W=[f"wd{s}"], stream=f"wd{s}")
                if bi < 2:
                    A.mm(banks, [f"wd{s}", ("uT", ffc)], [MM(ps[ti * 2 + cbk][:, 0:512], uT[:, ffc, ti * 128:(ti + 1) * 128], wd[s][:, cbk * 512:(cbk + 1) * 512],
                                                            ffc == 0, ffc == 63) for ti in range(tiles) for cbk in range(2)])
                else:
                    A.mm(banks, [f"wd{s}", ("uT3", ffc)], [MM(ps[cbk][:, 0:512], uT3(ffc), wd[s][:, cbk * 512:(cbk + 1) * 512],
                                                             ffc == 0, ffc == 63) for cbk in range(2)])
            for ti in range(tiles):
                for cbk in range(2):
                    pb = ti * 2 + cbk
                    c_ = half * 1024 + cbk * 512
                    A.op(act if cbk else dve, "copy" if cbk else "tensor_copy", R=[f"ps{pb}"], W=[("m_tm", ti)], out=m_tm[:, ti, c_:c_ + 512], in_=ps[pb][:, 0:512])
        for ti in range(tiles):
            t = c0 // 128 + ti
            rows = slice(t * 128, (t + 1) * 128)
            A.dma(sp, x1_t[:], B.y_main[rows, :], R=[], W=["x1_t"], stream="x1i")
            A.op(dve, "memset", W=["st6"], ap=st[:], constant=0.0)
            A.op(act, "activation", R=[("m_tm", ti)], W=["junk6", "st6"], out=junk[:], in_=m_tm[:, ti, :], func=AF.Square, accum_out=st[:, 0:1])
            A.op(act, "activation", R=["st6"], W=["st6"], out=st[:, 1:2], in_=st[:, 0:1], func=AF.Sqrt, scale=1.0 / D, bias=B.eps_t[:, 0:1])
            A.op(dve, "reciprocal", R=["st6"], W=["st6"], out=st[:, 2:3], in_=st[:, 1:2])
            A.op(dve, "scalar_tensor_tensor", R=[("m_tm", ti), "st6", "G_tm"], W=[("m_tm", ti)], out=m_tm[:, ti, :], in0=m_tm[:, ti, :], scalar=st[:, 2:3],
                 in1=G_tm[:], op0=ALU.mult, op1=ALU.mult)
            A.op(pool, "tensor_tensor", R=[("m_tm", ti), "x1_t"], W=["x1_t"], out=x1_t[:], in0=m_tm[:, ti, :], in1=x1_t[:], op=ALU.add)
            A.dma(sp, B.y_main[rows, :], x1_t[:], R=["x1_t"], W=[], stream="yo")


def _host_consts(core):
    half = core % 2
    inv = (10000.0 ** (-np.arange(32, dtype=np.float32) / 32.0)).astype(np.float32)
    pos_main = np.concatenate([half * NTP + np.arange(NTP), PAST + (np.arange(NTS) % TS)]).astype(np.float32)
    pos_ctx = np.arange(NCX).astype(np.float32)
    pos = np.concatenate([pos_main, pos_ctx])
    ang = pos[:, None] * inv[None, :]
    cos = np.cos(ang).astype(np.float32)
    sin = np.sin(ang).astype(np.float32)
    cs = np.concatenate([cos, cos], axis=1)
    sn = np.concatenate([-sin, sin], axis=1)
    masks = np.zeros((128, 1024), np.float32)
    r = np.arange(128)
    masks[:, 0:128] = ((r[:, None] // 64 == r[None, :] // 64) & (r[:, None] <= r[None, :])).astype(np.float32)
    masks[:, 128:256] = ((r[:, None] // 8 == r[None, :] // 8) & (r[:, None] <= r[None, :])).astype(np.float32)
    masks[:, 256:384] = (r[:, None] <= r[None, :]).astype(np.float32)
    masks[:, 760:776] = (r[:, None] // 8 == np.arange(16)[None, :]).astype(np.float32)
    masks[:, 800] = r.astype(np.float32)
    sm = np.zeros((128, NB, NH, TS), np.float32)
    for s in range(128):
        sm[s, s // 8, :, (s % 8):] = 1.0
    sel = np.zeros((17, 256), np.float32)
    sel[0, 0:128] = 1.0
    for tok in range(128):
        sel[1 + tok // 8, 128 + tok] = 1.0
    flag = np.full((128, 1), float(half), np.float32)
    cbias = np.full((1, NCX), 0.0 if half == 1 else NEG, np.float32)
    return dict(rope_cs=cs, rope_sn=sn, ropeT_cs=np.ascontiguousarray(cs[:NT].T), ropeT_sn=np.ascontiguousarray(sn[:NT].T),
                masks=masks, smask=sm.reshape(128, NB * 64), sel=sel, ctx_flag=flag, ctx_bias=cbias,
                ident=np.eye(128, dtype=np.float32))


def make_in_maps(inputs, cores, cache_all, ptab_all):
    f = lambda a: np.ascontiguousarray(np.asarray(a, dtype=np.float32))
    x_prompt = f(inputs["x_prompt"]); x_sample = f(inputs["x_sample"])
    c_prompt = f(inputs["c_prompt"]); c_sample = f(inputs["c_sample"])
    vecs = np.concatenate([
        f(inputs["g_pre_mix"]).reshape(16, 128), f(inputs["g_post_mix"]).reshape(16, 128),
        f(inputs["g_pre_mlp"]).reshape(16, 128), f(inputs["g_post_mlp"]).reshape(16, 128),
        f(inputs["g_q_norm"]).reshape(4, 128), f(inputs["g_kv_norm"]).reshape(4, 128),
        f(inputs["lb_logits"]).reshape(16, 128), f(inputs["g_hgrn_norm"]).reshape(8, 128)], axis=0)
    shared = dict(
        w_ada=f(inputs["w_ada"])[0], b_ada=f(inputs["b_ada"]).reshape(1, 6 * D), vecs=vecs,
        g_q=f(inputs["g_q_norm"]).reshape(1, QR),
        g_kv=f(inputs["g_kv_norm"]).reshape(1, KVR), w_in=f(inputs["w_in"])[0], w_a_out=f(inputs["w_a_out"])[0],
        w_q_up=f(inputs["w_q_up"])[0], w_kv_up=f(inputs["w_kv_up"])[0], w_b_out=f(inputs["w_b_out"])[0],
        w_o=f(inputs["w_o"])[0], w_up=f(inputs["w_up"])[0], w_down=f(inputs["w_down"])[0],
        cache_all=cache_all)
    state = f(inputs["state_hgrn"])[0]
    maps = []
    for c in cores:
        b, half = c // 2, c % 2
        m = dict(shared)
        m["x_main"] = np.concatenate([x_prompt[b, half * NTP:(half + 1) * NTP], x_sample[c * NB:(c + 1) * NB].reshape(NTS, D)], axis=0)
        m["x_ctx"] = x_prompt[b, 0:NCX]
        m["c_all"] = np.concatenate([c_prompt[b:b + 1], c_sample[c * NB:(c + 1) * NB]], axis=0)
        m["state_in"] = state[c * NB:(c + 1) * NB]
        m["ptab"] = np.ascontiguousarray(ptab_all[c * NB:(c + 1) * NB]).astype(np.int32).reshape(1, NB * NPAGES)
        m.update(_host_consts(c))
        maps.append(m)
    return maps


def assemble(results, cores):
    y_prompt = np.zeros((4, 2048, D), np.float32); y_sample = np.zeros((128, TS, D), np.float32)
    ckv_p = np.zeros((1, 4, 2048, KVR), np.float32); kpe_p = np.zeros((1, 4, 2048, ROPE), np.float32)
    s_p = np.zeros((1, 4, NH, DK, DV), np.float32)
    ckv_s = np.zeros((1, 128, TS, KVR), np.float32); kpe_s = np.zeros((1, 128, TS, ROPE), np.float32)
    s_s = np.zeros((1, 128, NH, DK, DV), np.float32)
    for r, c in zip(results, cores):
        b, half = c // 2, c % 2
        sl = slice(half * NTP, (half + 1) * NTP)
        y_prompt[b, sl] = r["y_main"][:NTP]; y_sample[c * NB:(c + 1) * NB] = r["y_main"][NTP:].reshape(NB, TS, D)
        ckv_p[0, b, sl] = r["ckv_main"][:NTP]; ckv_s[0, c * NB:(c + 1) * NB] = r["ckv_main"][NTP:].reshape(NB, TS, KVR)
        kpe_p[0, b, sl] = r["kpe_main"][:NTP]; kpe_s[0, c * NB:(c + 1) * NB] = r["kpe_main"][NTP:].reshape(NB, TS, ROPE)
        if half == 1:
            s_p[0, b] = r["s_prompt"]
        s_s[0, c * NB:(c + 1) * NB] = r["s_sample"]
    return (y_prompt, y_sample, ckv_p, kpe_p, s_p, ckv_s, kpe_s, s_s)


def kernel(**inputs):
    cores = list(range(8))
    cache_ckv = np.ascontiguousarray(np.asarray(inputs["cache_ckv"], dtype=np.float32)).reshape(-1, KVR)
    cache_kpe = np.ascontiguousarray(np.asarray(inputs["cache_kpe"], dtype=np.float32)).reshape(-1, ROPE)
    ptab = np.asarray(inputs["page_table"]).astype(np.int32)
    cache_all = np.concatenate([cache_ckv, cache_kpe], axis=1)
    del cache_ckv, cache_kpe
    nc = build_nc(cache_all.shape[0])
    maps = make_in_maps(inputs, cores, cache_all, ptab)
    res = run_bass_kernel_spmd(nc, maps, core_ids=cores)
    return assemble(res.results, cores)
```
